# Optimizing a Trainium2 kernel written in Bass

```python
import math
import jax, jax.numpy as jnp
from jax import lax
import numpy as np

D_MODEL = 1024
BATCH = 16
SEQ = 4096
DEPTH = 2

CHUNK = 64
Q_BLOCK = 128
POOL_WINDOWS = (2, 4, 8, 16)
POOL_GROUP = D_MODEL // 8
POOL_WIDTH = POOL_GROUP * len(POOL_WINDOWS)
MLSTM_HEADS = 4
MLSTM_HEAD_DIM = D_MODEL // 8
MLSTM_WIDTH = MLSTM_HEADS * MLSTM_HEAD_DIM
MLSTM_CONV = 4
EVEN_IN = POOL_WIDTH + 4 * MLSTM_WIDTH + 2 * MLSTM_HEADS
MIX_WIDTH = POOL_WIDTH + MLSTM_WIDTH
MLA_HEADS = 8
MLA_NOPE = 128
MLA_ROPE = 64
MLA_V = 128
MLA_Q_LORA = D_MODEL // 2
MLA_KV_LORA = D_MODEL // 4
ODD_IN = MLA_Q_LORA + MLA_KV_LORA + MLA_ROPE
ROPE_THETA = 10000.0
D_FF = 2816
FFN_CONV = 3
N_EVEN = (DEPTH + 1) // 2
N_ODD = DEPTH // 2
DEEPNORM_ALPHA = (2 * DEPTH) ** 0.25
DEEPNORM_BETA = (8 * DEPTH) ** -0.25
LN_EPS = 1e-5
RMS_EPS = 1e-6

kernel_name = 'hybrid_pool_mlstm_mla_convffn_trunk'


def layer_norm(x, g, b):
    xf = x.astype(jnp.float32)
    mu = jnp.mean(xf, -1, keepdims=True)
    var = jnp.mean(jnp.square(xf - mu), -1, keepdims=True)
    return ((xf - mu) * lax.rsqrt(var + LN_EPS) * g + b).astype(x.dtype)


def rms_norm(x, g):
    xf = x.astype(jnp.float32)
    return (xf * lax.rsqrt(jnp.mean(xf * xf, -1, keepdims=True) + RMS_EPS) * g).astype(x.dtype)


def causal_dwconv(x, w):
    k = w.shape[0]
    return lax.conv_general_dilated(
        x, w[:, None, :].astype(x.dtype), window_strides=(1,), padding=[(k - 1, 0)],
        dimension_numbers=('NWC', 'WIO', 'NWC'), feature_group_count=x.shape[-1])


def ada_modulation(c, w, b):
    mod = jnp.einsum('bd,de->be', jax.nn.silu(c), w) + b
    shift, scale, gate = jnp.split(mod, 3, axis=-1)
    return shift[:, None], scale[:, None], gate[:, None]


def post_norm_residual(x, y, gate, g, b):
    return layer_norm(DEEPNORM_ALPHA * x + gate * y, g, b)


def rope_tables(positions):
    half = MLA_ROPE // 2
    inv = ROPE_THETA ** (-jnp.arange(half, dtype=jnp.float32) / half)
    ang = positions.astype(jnp.float32)[..., None] * inv
    return jnp.cos(ang), jnp.sin(ang)


def apply_rope(x, cos, sin):
    x1, x2 = jnp.split(x.astype(jnp.float32), 2, axis=-1)
    return jnp.concatenate([x1 * cos - x2 * sin, x1 * sin + x2 * cos], -1).astype(x.dtype)


def multiscale_pool(u, w_group, scale):
    b, s, _ = u.shape
    uf = u.astype(jnp.float32).reshape(b, s, len(POOL_WINDOWS), POOL_GROUP)
    cs = jnp.cumsum(uf, axis=1)
    t = jnp.arange(1, s + 1, dtype=jnp.float32)
    outs = []
    for gi, win in enumerate(POOL_WINDOWS):
        csg = cs[:, :, gi]
        lag = jnp.pad(csg, ((0, 0), (win, 0), (0, 0)))[:, :s]
        mean = (csg - lag) / jnp.minimum(t, float(win))[None, :, None]
        outs.append(mean - uf[:, :, gi])
    pooled = jnp.stack(outs, axis=2).astype(u.dtype)
    mixed = jnp.einsum('bsgc,gcd->bsgd', pooled, w_group)
    return mixed.reshape(b, s, POOL_WIDTH) * scale


def mlstm_chunkwise(q, k, v, i_pre, f_pre):
    b, s, h, dh = q.shape
    nc = s // CHUNK

    def to_chunks(a):
        a = a.astype(jnp.float32).reshape((b, nc, CHUNK, h) + a.shape[3:])
        return jnp.moveaxis(jnp.moveaxis(a, 1, 0), 3, 2)

    log_f = jax.nn.log_sigmoid(f_pre.astype(jnp.float32))
    xs = (to_chunks(q), to_chunks(k), to_chunks(v), to_chunks(i_pre), to_chunks(log_f))
    causal = jnp.tril(jnp.ones((CHUNK, CHUNK), dtype=bool))

    def step(carry, inp):
        c_mat, n_vec, m = carry
        qc, kc, vc, ic, lfc = inp
        bcum = jnp.cumsum(lfc, axis=-1)
        dmat = jnp.where(causal, bcum[..., :, None] - bcum[..., None, :] + ic[..., None, :], -jnp.inf)
        m_inter = bcum + m[..., None]
        m_t = jnp.maximum(m_inter, jnp.max(dmat, -1))
        decay = jnp.exp(dmat - m_t[..., None])
        inter = jnp.exp(m_inter - m_t)
        scores = jnp.einsum('bhtd,bhsd->bhts', qc, kc) * decay
        num = (jnp.einsum('bhts,bhse->bhte', scores, vc)
               + inter[..., None] * jnp.einsum('bhtd,bhde->bhte', qc, c_mat))
        den = jnp.sum(scores, -1) + inter * jnp.einsum('bhtd,bhd->bht', qc, n_vec)
        hc = num / jnp.maximum(jnp.abs(den), jnp.exp(-m_t))[..., None]
        b_last = bcum[..., -1]
        g = b_last[..., None] - bcum + ic
        m_new = jnp.maximum(b_last + m, jnp.max(g, -1))
        wk = jnp.exp(g - m_new[..., None])
        carry_scale = jnp.exp(b_last + m - m_new)
        c_mat = carry_scale[..., None, None] * c_mat + jnp.einsum('bhs,bhsd,bhse->bhde', wk, kc, vc)
        n_vec = carry_scale[..., None] * n_vec + jnp.einsum('bhs,bhsd->bhd', wk, kc)
        return (c_mat, n_vec, m_new), hc

    init = (jnp.zeros((b, h, dh, dh), jnp.float32), jnp.zeros((b, h, dh), jnp.float32),
            jnp.zeros((b, h), jnp.float32))
    _, hs = lax.scan(step, init, xs)
    return jnp.transpose(hs, (1, 0, 3, 2, 4)).reshape(b, s, h, dh)


def pool_mlstm_mixer(h, w_in, pool_w, pool_scale, conv_qk, gate_b, head_norm, w_out):
    b, s, _ = h.shape
    u = jnp.einsum('bsd,de->bse', h, w_in)
    u_pool, u_qk, u_v, u_o, u_if = jnp.split(
        u, [POOL_WIDTH, POOL_WIDTH + 2 * MLSTM_WIDTH, POOL_WIDTH + 3 * MLSTM_WIDTH,
            POOL_WIDTH + 4 * MLSTM_WIDTH], axis=-1)
    y_pool = multiscale_pool(u_pool, pool_w, pool_scale)
    qk = jax.nn.silu(causal_dwconv(u_qk, conv_qk))
    q, k = jnp.split(qk, 2, axis=-1)
    shp = (b, s, MLSTM_HEADS, MLSTM_HEAD_DIM)
    i_pre, f_pre = jnp.split(u_if.astype(jnp.float32) + gate_b, 2, axis=-1)
    hm = mlstm_chunkwise(q.reshape(shp), k.reshape(shp) * (MLSTM_HEAD_DIM ** -0.5),
                         u_v.reshape(shp), i_pre, f_pre)
    mu = jnp.mean(hm, -1, keepdims=True)
    var = jnp.mean(jnp.square(hm - mu), -1, keepdims=True)
    hm = ((hm - mu) * lax.rsqrt(var + LN_EPS)).reshape(b, s, MLSTM_WIDTH) * head_norm
    y_mlstm = (jax.nn.sigmoid(u_o.astype(jnp.float32)) * hm).astype(h.dtype)
    y = jnp.concatenate([y_pool, y_mlstm], axis=-1)
    return jnp.einsum('bse,ed->bsd', y, w_out)


def mla_mixer(h, cos, sin, w_in, q_norm, kv_norm, w_uq, w_ukv, w_out):
    b, s, _ = h.shape
    u = jnp.einsum('bsd,de->bse', h, w_in)
    c_q, c_kv, k_r = jnp.split(u, [MLA_Q_LORA, MLA_Q_LORA + MLA_KV_LORA], axis=-1)
    q = jnp.einsum('bsr,re->bse', rms_norm(c_q, q_norm), w_uq).reshape(b, s, MLA_HEADS, MLA_NOPE + MLA_ROPE)
    q_nope = q[..., :MLA_NOPE]
    q_rope = apply_rope(q[..., MLA_NOPE:], cos[:, :, None], sin[:, :, None])
    kv = jnp.einsum('bsr,re->bse', rms_norm(c_kv, kv_norm), w_ukv).reshape(b, s, MLA_HEADS, MLA_NOPE + MLA_V)
    k_nope, v = kv[..., :MLA_NOPE], kv[..., MLA_NOPE:]
    k_rope = apply_rope(k_r, cos, sin)
    scale = (MLA_NOPE + MLA_ROPE) ** -0.5
    chunk_id = jnp.arange(s) // CHUNK
    outs = []
    for start in range(0, s, Q_BLOCK):
        end = start + Q_BLOCK
        sc = (jnp.einsum('bqhd,bkhd->bhqk', q_nope[:, start:end], k_nope[:, :end])
              + jnp.einsum('bqhd,bkd->bhqk', q_rope[:, start:end], k_rope[:, :end]))
        mask = chunk_id[start:end, None] >= chunk_id[None, :end]
        sc = jnp.where(mask, sc.astype(jnp.float32) * scale, -jnp.inf)
        p = jax.nn.softmax(sc, axis=-1).astype(v.dtype)
        outs.append(jnp.einsum('bhqk,bkhd->bqhd', p, v[:, :end]))
    o = jnp.concatenate(outs, axis=1).reshape(b, s, MLA_HEADS * MLA_V)
    return jnp.einsum('bse,ed->bsd', o, w_out)


def conv_ffn(h, w_up, conv_w, w_down):
    a, g = jnp.split(jnp.einsum('bsd,df->bsf', h, w_up), 2, axis=-1)
    g = jax.nn.gelu(causal_dwconv(g, conv_w), approximate=False)
    return jnp.einsum('bsf,fd->bsd', a * g, w_down)


def setup_inputs(seed: int = 0) -> dict:
    key = jax.random.key(seed)
    keys = iter(jax.random.split(key, 48))

    def nrm(shape, std):
        return jax.random.normal(next(keys), shape, jnp.float32) * std

    d, ne, no = D_MODEL, N_EVEN, N_ODD
    x = nrm((BATCH, SEQ, d), 1.0)
    c = nrm((BATCH, d), 1.0)
    positions = (jnp.arange(SEQ, dtype=jnp.int32)[None, :]
                 + jax.random.randint(next(keys), (BATCH, 1), 0, 1024, dtype=jnp.int32))
    out_std = DEEPNORM_BETA * MIX_WIDTH ** -0.5
    inp = {'x': x, 'c': c, 'positions': positions}
    inp['e_ada_w'] = nrm((ne, d, 3 * d), d ** -0.5)
    inp['e_ada_b'] = nrm((ne, 3 * d), 0.02)
    inp['e_w_in'] = nrm((ne, d, EVEN_IN), d ** -0.5)
    inp['e_pool_w'] = nrm((ne, len(POOL_WINDOWS), POOL_GROUP, POOL_GROUP), POOL_GROUP ** -0.5)
    inp['e_pool_scale'] = 1.0 + nrm((ne, POOL_WIDTH), 0.1)
    inp['e_conv_qk'] = nrm((ne, MLSTM_CONV, 2 * MLSTM_WIDTH), MLSTM_CONV ** -0.5)
    gate_i = nrm((ne, MLSTM_HEADS), 0.1)
    gate_f = jnp.linspace(3.0, 6.0, MLSTM_HEADS, dtype=jnp.float32)[None] + nrm((ne, MLSTM_HEADS), 0.1)
    inp['e_gate_b'] = jnp.concatenate([gate_i, gate_f], axis=-1)
    inp['e_head_norm'] = 1.0 + nrm((ne, MLSTM_WIDTH), 0.02)
    inp['e_w_out'] = nrm((ne, MIX_WIDTH, d), out_std)
    inp['e_ln_g'] = 1.0 + nrm((ne, d), 0.02)
    inp['e_ln_b'] = nrm((ne, d), 0.02)
    inp['o_ada_w'] = nrm((no, d, 3 * d), d ** -0.5)
    inp['o_ada_b'] = nrm((no, 3 * d), 0.02)
    inp['o_w_in'] = nrm((no, d, ODD_IN), d ** -0.5)
    inp['o_q_norm'] = 1.0 + nrm((no, MLA_Q_LORA), 0.02)
    inp['o_kv_norm'] = 1.0 + nrm((no, MLA_KV_LORA), 0.02)
    inp['o_w_uq'] = nrm((no, MLA_Q_LORA, MLA_HEADS * (MLA_NOPE + MLA_ROPE)), MLA_Q_LORA ** -0.5)
    inp['o_w_ukv'] = nrm((no, MLA_KV_LORA, MLA_HEADS * (MLA_NOPE + MLA_V)), MLA_KV_LORA ** -0.5)
    inp['o_w_out'] = nrm((no, MLA_HEADS * MLA_V, d), DEEPNORM_BETA * (MLA_HEADS * MLA_V) ** -0.5)
    inp['o_ln_g'] = 1.0 + nrm((no, d), 0.02)
    inp['o_ln_b'] = nrm((no, d), 0.02)
    inp['f_ada_w'] = nrm((DEPTH, d, 3 * d), d ** -0.5)
    inp['f_ada_b'] = nrm((DEPTH, 3 * d), 0.02)
    inp['f_w_up'] = nrm((DEPTH, d, 2 * D_FF), d ** -0.5)
    inp['f_conv'] = nrm((DEPTH, FFN_CONV, D_FF), FFN_CONV ** -0.5)
    inp['f_w_down'] = nrm((DEPTH, D_FF, d), DEEPNORM_BETA * D_FF ** -0.5)
    inp['f_ln_g'] = 1.0 + nrm((DEPTH, d), 0.02)
    inp['f_ln_b'] = nrm((DEPTH, d), 0.02)
    return inp


def reference(x, c, positions,
              e_ada_w, e_ada_b, e_w_in, e_pool_w, e_pool_scale, e_conv_qk, e_gate_b,
              e_head_norm, e_w_out, e_ln_g, e_ln_b,
              o_ada_w, o_ada_b, o_w_in, o_q_norm, o_kv_norm, o_w_uq, o_w_ukv, o_w_out,
              o_ln_g, o_ln_b,
              f_ada_w, f_ada_b, f_w_up, f_conv, f_w_down, f_ln_g, f_ln_b):
    cos, sin = rope_tables(positions)
    for layer in range(DEPTH):
        j = layer // 2
        if layer % 2 == 0:
            shift, scale, gate = ada_modulation(c, e_ada_w[j], e_ada_b[j])
            y = pool_mlstm_mixer(x * (1.0 + scale) + shift, e_w_in[j], e_pool_w[j], e_pool_scale[j],
                                 e_conv_qk[j], e_gate_b[j], e_head_norm[j], e_w_out[j])
            x = post_norm_residual(x, y, gate, e_ln_g[j], e_ln_b[j])
        else:
            shift, scale, gate = ada_modulation(c, o_ada_w[j], o_ada_b[j])
            y = mla_mixer(x * (1.0 + scale) + shift, cos, sin, o_w_in[j], o_q_norm[j], o_kv_norm[j],
                          o_w_uq[j], o_w_ukv[j], o_w_out[j])
            x = post_norm_residual(x, y, gate, o_ln_g[j], o_ln_b[j])
        shift, scale, gate = ada_modulation(c, f_ada_w[layer], f_ada_b[layer])
        y = conv_ffn(x * (1.0 + scale) + shift, f_w_up[layer], f_conv[layer], f_w_down[layer])
        x = post_norm_residual(x, y, gate, f_ln_g[layer], f_ln_b[layer])
    return x
```

```python
import math
DBG = 99
from contextlib import ExitStack
import numpy as np
import concourse.bass as bass
import concourse.mybir as mybir
from concourse.bass_utils import run_bass_kernel_spmd

F32 = mybir.dt.float32
BF16 = mybir.dt.bfloat16
I32 = mybir.dt.int32
ALU = mybir.AluOpType
AF = mybir.ActivationFunctionType

D = 1024
DFF = 2816
NFC = 22
EVEN_IN = 2568
ALPHA = 4.0 ** 0.25
LN_EPS = 1e-5
RMS_EPS = 1e-6
SEM_MAX = 3500
NDS = 40
TWO_PI = 2.0 * math.pi


ACT_SET = {AF.Exp: 'exp', AF.Ln: 'ln', AF.Sigmoid: 'sig', AF.Silu: 'silu', AF.Gelu: 'gelu', AF.Sin: 'sin'}


class Buf:
    __slots__ = ("w", "rd", "psum")

    def __init__(self, psum=False):
        self.w = None
        self.rd = []
        self.psum = psum


class Op:
    __slots__ = ("idx", "eng", "fn", "est", "deps", "ev", "inc", "is_dma", "nbytes", "waits", "ready", "fin", "fl", "extra", "aset", "delay")

    def __init__(self, eng, fn, est, is_dma=False, nbytes=0):
        self.eng = eng
        self.fn = fn
        self.est = est
        self.is_dma = is_dma
        self.nbytes = nbytes
        self.ev = None
        self.inc = 16 if is_dma else 1
        self.deps = ()
        self.waits = []
        self.ready = 0.0
        self.fin = 0.0
        self.fl = -1
        self.idx = -1
        self.extra = None
        self.aset = None
        self.delay = 0.0


class Tile:
    def __init__(self, P, ctx, name, shape, dt, nsub=1, psum=False):
        if psum:
            self.t = ctx.enter_context(P.nc.psum_tensor("ps_" + name, list(shape), dt))
        else:
            self.t = ctx.enter_context(P.nc.sbuf_tensor("sb_" + name, list(shape), dt))
        self.b = [Buf(psum) for _ in range(nsub)]


def _fsize(ap):
    n = 1
    for d in list(ap.shape)[1:]:
        n *= int(d)
    return n


class Prog:
    ENG = ("pe", "act", "dve", "pool", "sp")
    LAT = 220.0

    def __init__(self, nc):
        self.nc = nc
        self.top = ExitStack()
        self.sems = []
        self.pending = []
        self.semidx = {}
        self.cnt = {}
        for e in ("pe", "act", "dve", "pool"):
            self._newsem(e)
        self.known = {e: {} for e in self.ENG}
        self.dsem = []
        for i in range(NDS):
            self.sems.append(self.top.enter_context(nc.semaphore(f"dq{i}")))
            self.dsem.append(len(self.sems) - 1)
        self.dval = [0] * NDS
        self.dnext = 0
        self.nblk = 0
        self.rr = 0
        self.final_ops = []
        self.sched = True

    def _newsem(self, e):
        s = self.top.enter_context(self.nc.semaphore(f"s_{e}_{len(self.sems)}"))
        self.sems.append(s)
        self.semidx[e] = len(self.sems) - 1
        self.cnt[e] = 0

    def _record(self, o, R, W):
        eng = o.eng
        deps = set()
        for b in R:
            if b.w is not None:
                deps.add(b.w)
            if b.psum:
                for r in b.rd:
                    if r.eng != eng:
                        deps.add(r)
        for b in W:
            if b.w is not None:
                deps.add(b.w)
            deps.update(b.rd)
        deps.discard(o)
        o.deps = deps
        for b in R:
            rd = b.rd
            rd.append(o)
            if len(rd) > 96 and not b.psum:
                last = {}
                keep = []
                for r in rd:
                    if r.ev is None:
                        keep.append(r)
                    else:
                        k = (r.eng, r.ev[0])
                        if k not in last or last[k].ev[1] < r.ev[1]:
                            last[k] = r
                if len(keep) < 64:
                    b.rd = list(last.values()) + keep
        for b in W:
            b.w = o
            b.rd = []
        o.fl = self.nblk
        self.pending.append(o)
        return o

    def op(self, eng, fn, R=(), W=(), est=200.0):
        return self._record(Op(eng, fn, est), R, W)

    def dma(self, out, in_, R=(), W=(), eng="sp", nbytes=None):
        if nbytes is None:
            try:
                nbytes = int(out.nbytes() if callable(out.nbytes) else out.nbytes)
            except Exception:
                nbytes = 65536
        o = Op(eng, lambda e: e.dma_start(out=out, in_=in_), 60.0, True, nbytes)
        return self._record(o, R, W)

    def wait_all(self, eng, ops):
        self.final_ops.extend(ops)

    def _schedule(self, ops):
        ENG = self.ENG
        n = len(ops)
        for i, o in enumerate(ops):
            o.idx = i
            o.ready = 0.0
        npend = [0] * n
        succ = [[] for _ in range(n)]
        cur = self.nblk
        for o in ops:
            for d in o.deps:
                if d.fl == cur and d.ev is None:
                    npend[o.idx] += 1
                    succ[d.idx].append(o)
        out = {e: [] for e in ENG}
        if not self.sched:
            for o in ops:
                out[o.eng].append(o)
            return out
        cand = {e: [] for e in ENG}
        for o in ops:
            if npend[o.idx] == 0:
                cand[o.eng].append(o)
        free = {e: 0.0 for e in ENG}
        dma_bw = 0.0
        cur_set = None
        LAT = self.LAT
        remaining = n
        while remaining:
            bst = None
            bop = None
            for e in ENG:
                c = cand[e]
                if not c:
                    continue
                fa = free[e]
                pick = None
                pk = None
                if e == "act":
                    for o in c:
                        r = o.ready
                        sw = 1 if (o.aset is not None and o.aset != cur_set) else 0
                        k = (0.0, sw, o.idx) if r <= fa else (r + 1300.0 * sw, sw, o.idx)
                        if pk is None or k < pk:
                            pick, pk = o, k
                else:
                    for o in c:
                        r = o.ready
                        k = (0.0, o.idx) if r <= fa else (r, o.idx)
                        if pk is None or k < pk:
                            pick, pk = o, k
                st = fa if pick.ready <= fa else pick.ready
                if bst is None or st < bst or (st == bst and pick.idx < bop.idx):
                    bst, bop = st, pick
            o = bop
            e = o.eng
            cand[e].remove(o)
            if o.is_dma:
                free[e] = bst + 60.0
                t0 = bst if bst > dma_bw else dma_bw
                dur = o.nbytes / 150.0
                dma_bw = t0 + dur
                fin = t0 + dur + 2000.0
            else:
                fin = bst + o.est
                if o.aset is not None and o.aset != cur_set:
                    fin += 1300.0
                    cur_set = o.aset
                free[e] = fin
            o.fin = fin
            out[e].append(o)
            for s_ in succ[o.idx]:
                lat = (0.0 if (e == "pe" and s_.eng == "pe") else LAT) + s_.delay
                if fin + lat > s_.ready:
                    s_.ready = fin + lat
                npend[s_.idx] -= 1
                if npend[s_.idx] == 0:
                    cand[s_.eng].append(s_)
            remaining -= 1
        self.sim_time = max(free.values())
        return out

    def flush(self, drain=True, final=False):
        ops = self.pending
        self.pending = []
        order = self._schedule(ops)
        self.nblk += 1
        for e in ("pe", "act", "dve", "pool"):
            for o in order[e]:
                if self.cnt[e] >= SEM_MAX:
                    self._newsem(e)
                self.cnt[e] += 1
                o.ev = (self.semidx[e], self.cnt[e])
        for o in order["sp"]:
            i = self.dnext
            self.dnext = (i + 1) % NDS
            prev = self.dval[i]
            o.extra = (self.dsem[i], prev) if prev > 0 else None
            self.dval[i] = prev + 16
            o.ev = (self.dsem[i], prev + 16)
        for e in self.ENG:
            kn = self.known[e]
            for o in order[e]:
                d = {}
                for dep in o.deps:
                    if e == "pe" and dep.eng == "pe":
                        continue
                    s, v = dep.ev
                    if d.get(s, 0) < v:
                        d[s] = v
                if o.extra is not None:
                    s, v = o.extra
                    if d.get(s, 0) < v:
                        d[s] = v
                w = []
                for s, v in d.items():
                    if kn.get(s, 0) < v:
                        kn[s] = v
                        w.append((s, v))
                o.waits = w
                o.deps = ()
        tail = []
        if drain or final:
            kn = self.known["sp"]
            for i in range(NDS):
                if self.dval[i] > 0 and kn.get(self.dsem[i], 0) < self.dval[i]:
                    kn[self.dsem[i]] = self.dval[i]
                    tail.append((self.dsem[i], self.dval[i]))
        nc = self.nc
        sems = self.sems

        def mk(lst, tl):
            def body(e):
                fam = None
                for o in lst:
                    if o.aset is not None and o.aset != fam:
                        fam = o.aset
                    for s, v in o.waits:
                        e.wait_ge(sems[s], v)
                    ins = o.fn(e)
                    ins.then_inc(sems[o.ev[0]], o.inc)
                    o.fn = None
                for s, v in tl:
                    e.wait_ge(sems[s], v)
            return body

        with nc.Block() as blk:
            for name, dec in (("pe", blk.tensor), ("act", blk.scalar), ("dve", blk.vector),
                              ("pool", blk.gpsimd), ("sp", blk.sync)):
                tl = tail if name == "sp" else []
                if order[name] or tl:
                    dec(mk(order[name], tl))

    def mm(self, out, lhsT, rhs, start, stop, R, W):
        n = _fsize(rhs)
        est = max(n, 256) / 2.4 + 4.0
        if lhsT.dtype == F32:
            est *= 4.0
        return self.op("pe", lambda e: e.matmul(out, lhsT, rhs, start=start, stop=stop), R, W, est)

    def tr(self, out, in_, ident, R, W):
        return self.op("pe", lambda e: e.transpose(out, in_, ident), R, W, 60.0)

    def act(self, out, in_, func, R, W, bias=None, scale=None):
        kw = {}
        if bias is not None:
            kw["bias"] = bias
        if scale is not None:
            kw["scale"] = scale
        o = self.op("act", lambda e: e.activation(out, in_, func, **kw), R, W, 200.0 + _fsize(in_) / 1.2)
        o.aset = ACT_SET.get(func)
        return o

    def _vest(self, eng, n, f=1.0):
        if eng == "pool":
            return 120.0 + n * 1.9 * f
        return 70.0 + n * f / 0.96

    def tt(self, eng, out, in0, in1, op, R, W):
        return self.op(eng, lambda e: e.tensor_tensor(out, in0, in1, op), R, W, self._vest(eng, _fsize(in0), 1.6))

    def ts(self, eng, out, in0, s1, op0, R, W, s2=None, op1=None):
        est = self._vest(eng, _fsize(in0), 1.0)
        if op1 is None and eng == "pool":
            if op0 == ALU.mult:
                s2, op1 = 0.0, ALU.add
            elif op0 == ALU.add:
                s2, op1 = 1.0, ALU.mult
        if op1 is None:
            return self.op(eng, lambda e: e.tensor_scalar(out, in0, s1, None, op0), R, W, est)
        return self.op(eng, lambda e: e.tensor_scalar(out, in0, s1, s2, op0, op1), R, W, est)

    def stt(self, out, in0, scalar, in1, op0, op1, R, W):
        return self.op("dve", lambda e: e.scalar_tensor_tensor(out, in0, scalar, in1, op0, op1), R, W,
                       self._vest("dve", _fsize(in0), 1.8))

    def copy(self, eng, out, in_, R, W):
        if eng == "act":
            return self.op("act", lambda e: e.activation(out, in_, AF.Copy), R, W, 200.0 + _fsize(in_) / 1.2)
        return self.op(eng, lambda e: e.tensor_copy(out, in_), R, W, self._vest(eng, _fsize(in_), 1.0))

    def memset(self, eng, ap, val, W):
        return self.op(eng, lambda e: e.memset(ap, val), (), W, self._vest(eng, _fsize(ap), 0.5))

    def any3(self):
        self.rr += 1
        return ("pool", "dve", "act")[self.rr % 3]


class Cfg:
    def __init__(self, S, NSEQ):
        self.S = S
        self.NSEQ = NSEQ
        self.T = S * NSEQ


def _vec_layout():
    off = {}
    n = 0

    def add(name, w):
        nonlocal n
        off[name] = n
        n += w
    for sl in range(4):
        add(f"adab{sl}", 24)
    add("e_pool_scale", 4)
    add("e_conv", 32)
    add("e_ln_g", 8)
    add("e_ln_b", 8)
    add("o_ln_g", 8)
    add("o_ln_b", 8)
    add("o_q_norm", 4)
    add("o_kv_norm", 2)
    for l in range(2):
        add(f"f_conv{l}", 66)
        add(f"f_ln_g{l}", 8)
        add(f"f_ln_b{l}", 8)
    add("inv2", 1)
    add("sgn", 1)
    return off, n


VOFF, NV = _vec_layout()


def chunked(v, nch):
    return np.ascontiguousarray(np.asarray(v, np.float32).reshape(nch, 128).T)


class IO:
    pass


def declare_io(nc, cfg):
    io = IO()
    T, S, NSEQ = cfg.T, cfg.S, cfg.NSEQ

    def din(name, shape, dt=F32):
        return nc.dram_tensor(name, list(shape), dt, kind="ExternalInput").ap()

    def scr(name, shape, dt):
        return nc.dram_tensor(name, list(shape), dt, kind="Internal").ap()

    io.x = din("x", [T, D])
    io.pos = din("pos", [NSEQ, S], I32)
    io.cT = din("cT", [128, 8, NSEQ])
    io.vecs = din("vecs", [128, NV])
    io.rows = din("rows", [1, 520])
    io.consts = din("consts", [128, 5, 128])
    io.rc = din("rc", [128, 4, 16])
    io.ada_w = [din(f"ada_w{i}", [D, 3 * D]) for i in range(4)]
    io.e_w_in = din("e_w_in", [D, EVEN_IN])
    io.e_pool_w = din("e_pool_w", [128, 4, 128])
    io.e_w_out = din("e_w_out", [D, D])
    io.o_w_in = din("o_w_in", [D, 832 + 64])
    io.o_w_uq = din("o_w_uq", [512, 1536 + 512])
    io.o_w_ukv = din("o_w_ukv", [256, 2048])
    io.o_w_out = din("o_w_out", [D, D])
    io.f_w_up = [din(f"f_w_up{l}", [D, 2 * DFF]) for l in range(2)]
    io.f_w_down = [din(f"f_w_down{l}", [DFF, D]) for l in range(2)]
    io.out = nc.dram_tensor("out", [T, D], F32, kind="ExternalOutput").ap()
    io.X = [scr(f"X{i}", [8, 128, T], F32) for i in range(4)]
    io.QN = scr("QN", [8, 128, T], BF16)
    io.QR = scr("QR", [8, 64, T], BF16)
    io.KN = scr("KN", [8, 128, T], BF16)
    io.KR = scr("KR", [64, T], BF16)
    io.V = scr("V", [T, 1024], BF16)
    io.OT = scr("OT", [8, 128, T], BF16)
    io.Xb = [Buf() for _ in range(4)]
    io.ab = {k: [Buf()] for k in ("QN", "QR", "KN", "KR", "V", "OT")}
    return io


class Pers:
    pass


def setup_persistent(P, ctx, io, cfg):
    ps = Pers()
    ps.vecs = Tile(P, ctx, "vecs", [128, NV], F32)
    ps.cst = Tile(P, ctx, "cst", [128, 5, 128], F32)
    ps.cstb = Tile(P, ctx, "cstb", [128, 5, 128], BF16)
    ps.mod = Tile(P, ctx, "mod", [128, 4, cfg.NSEQ, 24], F32)
    P.dma(ps.vecs.t[:], io.vecs[:, :], W=ps.vecs.b)
    P.dma(ps.cst.t[:], io.consts[:, :, :], W=ps.cst.b)
    P.copy("dve", ps.cstb.t[:], ps.cst.t[:], R=ps.cst.b, W=ps.cstb.b)
    ps.ident_f = ps.cst.t[:, 0, :]
    ps.tri_f = ps.cst.t[:, 1, :]
    ps.ones_f = ps.cst.t[:, 2, :]
    ps.ident_b = ps.cstb.t[:, 0, :]
    ps.tri_b = ps.cstb.t[:, 1, :]
    ps.ones_b = ps.cstb.t[:, 2, :]
    ps.CR = ps.cst.b + ps.cstb.b + ps.vecs.b
    return ps


def vcol(ps, name, j=0, rows=128):
    c = VOFF[name] + j
    return ps.vecs.t[0:rows, c:c + 1]


def stage_ada(P, cfg, io, ps):
    NSEQ = cfg.NSEQ
    with ExitStack() as st:
        wst = [Tile(P, st, f"adaw{i}", [128, 8, 512], F32) for i in range(2)]
        sc = Tile(P, st, "silu_c", [128, 8, NSEQ], F32)
        res = Tile(P, st, "ada_res", [NSEQ, 3072], F32)
        pa = [Tile(P, st, f"ada_acc{i}", [128, 512], F32, psum=True) for i in range(2)]
        pp = Tile(P, st, "ada_T", [128, 512], F32, psum=True)
        P.dma(sc.t[:], io.cT[:, :, :], W=sc.b)
        P.act(sc.t[:], sc.t[:], AF.Silu, R=sc.b, W=sc.b)
        k = 0
        for sl in range(4):
            for cb in range(6):
                w = wst[k % 2]
                k += 1
                P.dma(w.t[:], io.ada_w[sl][:, cb * 512:(cb + 1) * 512].rearrange("(kc p) n -> p kc n", p=128), W=w.b)
                bank = pa[cb % 2]
                for kc in range(8):
                    P.mm(bank.t[0:NSEQ, :], sc.t[:, kc, :], w.t[:, kc, :], kc == 0, kc == 7, R=w.b + sc.b, W=bank.b)
                P.copy("act" if cb % 2 else "dve", res.t[:, cb * 512:(cb + 1) * 512], bank.t[0:NSEQ, :], R=bank.b, W=res.b)
            for fc in range(24):
                P.tr(pp.t[:, fc * NSEQ:(fc + 1) * NSEQ], res.t[0:NSEQ, fc * 128:(fc + 1) * 128], ps.cst.t[0:NSEQ, 0, 0:NSEQ],
                     R=res.b + ps.CR, W=pp.b)
            pv = pp.t[:, 0:24 * NSEQ].rearrange("p (f b) -> p f b", b=NSEQ)
            bias = ps.vecs.t[:, VOFF[f"adab{sl}"]:VOFF[f"adab{sl}"] + 24]
            for b in range(NSEQ):
                P.tt("dve", ps.mod.t[:, sl, b, :], pv[:, :, b], bias, ALU.add, R=pp.b + ps.vecs.b, W=ps.mod.b)
                P.ts("dve", ps.mod.t[:, sl, b, 8:16], ps.mod.t[:, sl, b, 8:16], 1.0, ALU.add, R=ps.mod.b, W=ps.mod.b)
                P.ts("dve", ps.mod.t[:, sl, b, 16:24], ps.mod.t[:, sl, b, 16:24], 1.0 / ALPHA, ALU.mult,
                     R=ps.mod.b, W=ps.mod.b)
        P.flush()


def mod_shift(ps, sl, b, c):
    return ps.mod.t[:, sl, b, c:c + 1]


def mod_sc1(ps, sl, b, c):
    return ps.mod.t[:, sl, b, 8 + c:9 + c]


def mod_gate(ps, sl, b, c):
    return ps.mod.t[:, sl, b, 16 + c:17 + c]


class LNBufs:
    def __init__(self, P, ctx, TT, tag, merged=False, split=False):
        self.merged = merged
        self.split = split
        if merged:
            self.zz = [Tile(P, ctx, f"zz{tag}{i}", [128, 2, TT], BF16, nsub=2) for i in range(2)]
        else:
            self.zb = [Tile(P, ctx, f"zb{tag}{i}", [128, TT], BF16) for i in range(2)]
            self.zq = [Tile(P, ctx, f"zq{tag}{i}", [128, TT], BF16) for i in range(2)]
        self.m = Tile(P, ctx, f"lnm{tag}", [128, TT], F32)
        self.v = Tile(P, ctx, f"lnv{tag}", [128, TT], F32)
        self.r = Tile(P, ctx, f"lnr{tag}", [128, TT], F32)


def ln_stats_chunk(P, ps, lb, xT, c, TT, S1, S2, nch=8):
    if lb.merged:
        zz = lb.zz[c % 2]
        P.copy("pool", zz.t[:, 0, :], xT.t[:, c, 0:TT], R=[xT.b[c]], W=[zz.b[0]])
        P.act(zz.t[:, 1, :], xT.t[:, c, 0:TT], AF.Square, R=[xT.b[c]], W=[zz.b[1]])
        P.mm(S1.t[:, 0:2 * TT], ps.ones_b, zz.t[:].rearrange("p a t -> p (a t)"), c == 0, c == nch - 1, R=zz.b + ps.CR, W=S1.b)
        return
    zb = lb.zb[c % 2]
    zq = lb.zq[c % 2]
    P.copy("dve" if (lb.split and c % 2 == 0) else "pool", zb.t[:], xT.t[:, c, 0:TT], R=[xT.b[c]], W=zb.b)
    P.act(zq.t[:], xT.t[:, c, 0:TT], AF.Square, R=[xT.b[c]], W=zq.b)
    P.mm(S1.t[:, 0:TT], ps.ones_b, zb.t[:], c == 0, c == nch - 1, R=zb.b + ps.CR, W=S1.b)
    P.mm(S2.t[:, 0:TT], ps.ones_b, zq.t[:], c == 0, c == nch - 1, R=zq.b + ps.CR, W=S2.b)


def ln_finish(P, ps, lb, xT, TT, S1, S2, gname, bname, eps, nfeat=1024.0):
    if lb.merged:
        s1ap, s2ap, s2b = S1.t[:, 0:TT], S1.t[:, TT:2 * TT], S1.b
    else:
        s1ap, s2ap, s2b = S1.t[:, 0:TT], S2.t[:, 0:TT], S2.b
    P.ts("dve", lb.m.t[:], s1ap, 1.0 / nfeat, ALU.mult, R=S1.b, W=lb.m.b)
    P.tt("dve", lb.v.t[:], lb.m.t[:], lb.m.t[:], ALU.mult, R=lb.m.b, W=lb.v.b)
    P.stt(lb.v.t[:], s2ap, 1.0 / nfeat, lb.v.t[:], ALU.mult, ALU.subtract, R=s2b + lb.v.b, W=lb.v.b)
    P.act(lb.r.t[:], lb.v.t[:], AF.Ln, R=lb.v.b, W=lb.r.b, bias=eps)
    P.act(lb.r.t[:], lb.r.t[:], AF.Exp, R=lb.r.b, W=lb.r.b, scale=-0.5)
    for c in range(8):
        P.stt(xT.t[:, c, 0:TT], s1ap, -1.0 / nfeat, xT.t[:, c, 0:TT], ALU.mult, ALU.add, R=[xT.b[c]] + S1.b, W=[xT.b[c]])
        P.tt("dve" if (lb.split and c % 2 == 1) else "pool", xT.t[:, c, 0:TT], xT.t[:, c, 0:TT], lb.r.t[:], ALU.mult,
             R=[xT.b[c]] + lb.r.b, W=[xT.b[c]])
        P.act(xT.t[:, c, 0:TT], xT.t[:, c, 0:TT], AF.Identity, R=[xT.b[c]] + ps.vecs.b, W=[xT.b[c]],
              bias=vcol(ps, bname, c), scale=vcol(ps, gname, c))


def load_cast(P, dst, dst_cols, src_ap_fn, ncols, stg, kchunks, blk=512, engines=None):
    k = 0
    for c0 in range(0, ncols, blk):
        n = min(blk, ncols - c0)
        s = stg[k % len(stg)]
        k += 1
        P.dma(s.t[:, 0:kchunks, 0:n], src_ap_fn(c0, n), W=s.b)
        eng = P.any3() if engines is None else engines[k % len(engines)]
        P.copy(eng, dst.t[:, 0:kchunks, dst_cols + c0:dst_cols + c0 + n], s.t[:, 0:kchunks, 0:n], R=s.b, W=dst.b)


E_NR = 5
E_BANKS = (6, 7, 0, 1)


def wslice(src, p=128):
    return lambda c0, n: src[:, c0:c0 + n].rearrange("(kc p) n -> p kc n", p=p)


def E_alloc(P, wctx):
    return (Tile(P, wctx, "e_win", [128, 8, EVEN_IN], BF16), Tile(P, wctx, "e_wout", [128, 8, 1024], BF16),
            Tile(P, wctx, "e_wpool", [128, 4, 128], BF16), Tile(P, wctx, "e_rows", [128, 520], F32),
            Tile(P, wctx, "e_rc", [128, 4, 16], F32))


def E_load(P, lctx, io, EW):
    win, wout, wpool, rows, rc = EW
    stg = [Tile(P, lctx, f"e_stg{i}", [128, 8, 512], F32) for i in range(2)]
    load_cast(P, win, 0, wslice(io.e_w_in), EVEN_IN, stg, 8)
    load_cast(P, wout, 0, wslice(io.e_w_out), 1024, stg, 8)
    s = stg[0]
    P.dma(s.t[:, 0:4, 0:128], io.e_pool_w[:, :, :], W=s.b)
    P.copy("dve", wpool.t[:], s.t[:, 0:4, 0:128], R=s.b, W=wpool.b)
    P.dma(rows.t[:], io.rows[0, :].partition_broadcast(128), W=rows.b)
    P.dma(rc.t[:], io.rc[:, :, :], W=rc.b)


def stage_E(P, cfg, io, ps, EW=None):
    S, NSEQ = cfg.S, cfg.NSEQ
    TT = 512
    NT = S // TT
    LNS = math.log(128.0 ** -0.5)
    WINS = (2, 4, 8, 16)
    with ExitStack() as wctx:
        if EW is None:
            EW = E_alloc(P, wctx)
            with ExitStack() as lctx:
                E_load(P, lctx, io, EW)
                P.flush()
        win, wout, wpool, rows, rc = EW
        WR = win.b + wout.b + wpool.b + rows.b + rc.b + ps.CR
        with ExitStack() as cx:
            xin = [Tile(P, cx, f"e_xin{i}", [128, 1024], F32) for i in range(2)]
            xTs = [Tile(P, cx, f"e_xT{i}", [128, 8, TT], F32, nsub=8) for i in range(2)]
            hs_ = [Tile(P, cx, f"e_h{i}", [128, 8, TT], BF16, nsub=8) for i in range(1)]
            U = [Tile(P, cx, f"e_U{g}", [128, 16 + TT], F32) for g in range(4)]
            PA = [Tile(P, cx, f"e_pa{i}", [128, 16 + TT], F32) for i in range(1)]
            PB = [Tile(P, cx, f"e_pb_{i}", [128, 16 + TT], F32) for i in range(1)]
            pooled = [Tile(P, cx, f"e_pooled{i}", [128, TT], BF16) for i in range(2)]
            UC = [Tile(P, cx, f"e_uc{j}", [128, 3 + TT], F32) for j in range(8)]
            cacc = [Tile(P, cx, f"e_cacc{i}", [128, TT], F32) for i in range(2)]
            qTs = [Tile(P, cx, f"e_qT{i}", [128, 4, TT], BF16, nsub=4) for i in range(2)]
            kTs = [Tile(P, cx, f"e_kT{i}", [128, 4, TT], BF16, nsub=4) for i in range(2)]
            vaug = Tile(P, cx, "e_vaug", [128, 16, 129], BF16, nsub=4)
            NR = E_NR
            gs = [Tile(P, cx, f"e_gs{i}", [128, 128], F32) for i in range(2)]
            gb4 = Tile(P, cx, "e_gb4", [128, 32], F32)
            ss = [Tile(P, cx, f"e_ss{i}", [128, 20], F32) for i in range(2)]
            hs = [Tile(P, cx, f"e_hs{i}", [128, 16], F32) for i in range(NR)]
            ktok = [Tile(P, cx, f"e_ktok{i}", [128, 4, 128], BF16) for i in range(2)]
            PT = [Tile(P, cx, f"e_PT{i}", [128, 128], BF16) for i in range(NR)]
            vw = [Tile(P, cx, f"e_vw{i}", [128, 129], BF16) for i in range(NR)]
            Cf = Tile(P, cx, "e_Cf", [128, 4, 129], F32, nsub=4)
            Cb = Tile(P, cx, "e_Cb", [128, 4, 129], BF16, nsub=4)
            hm = [Tile(P, cx, f"e_hm{i}", [128, 128], F32) for i in range(NR)]
            ytok = [Tile(P, cx, f"e_ytok{i}", [128, 4, 128], BF16) for i in range(2)]
            go = [Tile(P, cx, f"e_go{i}", [128, 512], F32) for i in range(4)]
            mixT = Tile(P, cx, "e_mix", [128, 8, TT], BF16, nsub=8)
            lb = LNBufs(P, cx, TT, "e")
            pb = [Tile(P, cx, f"e_bank{i}", [128, 512], F32, psum=True) for i in range(8)]
            pb4b = pb[4].t[:].bitcast(BF16)
            P.memset("pool", vaug.t[:, :, 128:129], 1.0, W=vaug.b)
            for sb in range(4):
                P.copy("pool", gb4.t[:, sb * 8:(sb + 1) * 8], rows.t[:, 512:520], R=rows.b, W=gb4.b)
            hcnt = 0
            for q in range(NSEQ):
                for ti in range(NT):
                    tok0 = q * S + ti * TT
                    xT = xTs[(q * NT + ti) % 2]
                    h = hs_[0]
                    qT = qTs[(q * NT + ti) % 2]
                    kT = kTs[(q * NT + ti) % 2]
                    for sb in range(4):
                        xi = xin[sb % 2]
                        P.dma(xi.t[:], io.x[tok0 + sb * 128: tok0 + (sb + 1) * 128, :], W=xi.b)
                        for half in range(2):
                            bank = pb[2 + half]
                            for j in range(4):
                                c = half * 4 + j
                                P.tr(bank.t[:, j * 128:(j + 1) * 128], xi.t[:, c * 128:(c + 1) * 128], ps.ident_f,
                                     R=xi.b + ps.CR, W=bank.b)
                            P.copy("dve" if half == 0 else "act", xT.t[:, half * 4:half * 4 + 4, sb * 128:(sb + 1) * 128],
                                   bank.t[:, :].rearrange("p (a b) -> p a b", a=4), R=bank.b, W=xT.b[half * 4:half * 4 + 4])
                    for c in range(8):
                        P.act(h.t[:, c, :], xT.t[:, c, :], AF.Identity, R=[xT.b[c]] + ps.mod.b, W=[h.b[c]],
                              bias=mod_shift(ps, 0, q, c), scale=mod_sc1(ps, 0, q, c))
                    for oc in range(12):
                        bank = pb[oc % 2]
                        for kc in range(8):
                            P.mm(bank.t[:, :], win.t[:, kc, oc * 128:(oc + 1) * 128], h.t[:, kc, :], kc == 0, kc == 7,
                                 R=[h.b[kc]] + win.b, W=bank.b)
                        if oc < 4:
                            g = oc
                            Wn = WINS[g]
                            if ti == 0:
                                P.memset("pool", U[g].t[:, 0:16], 0.0, W=U[g].b)
                            P.copy("act", U[g].t[:, 16:16 + TT], bank.t[:, :], R=bank.b, W=U[g].b)
                            cur, lo, w, k = U[g], 0, 1, 0
                            while w < Wn:
                                nxt = (PA, PB)[k % 2][0]
                                k += 1
                                P.tt("pool", nxt.t[:, lo + w:16 + TT], cur.t[:, lo + w:16 + TT], cur.t[:, lo:16 + TT - w],
                                     ALU.add, R=cur.b, W=nxt.b)
                                cur, lo, w = nxt, lo + w, 2 * w
                            if ti == 0:
                                P.tt("pool", cur.t[:, 16:32], cur.t[:, 16:32], rc.t[:, g, :], ALU.mult, R=cur.b + rc.b, W=cur.b)
                            pl = pooled[g % 2]
                            P.stt(pl.t[:], cur.t[:, 16:16 + TT], 1.0 / Wn, U[g].t[:, 16:16 + TT], ALU.mult, ALU.subtract,
                                  R=cur.b + U[g].b, W=pl.b)
                            P.copy("pool", U[g].t[:, 0:16], U[g].t[:, TT:TT + 16], R=U[g].b, W=U[g].b)
                            bk = pb[5]
                            P.mm(bk.t[:, :], wpool.t[:, g, :], pl.t[:], True, True, R=pl.b + wpool.b, W=bk.b)
                            P.act(mixT.t[:, g, :], bk.t[:, :], AF.Identity, R=bk.b + ps.vecs.b, W=[mixT.b[g]],
                                  scale=vcol(ps, "e_pool_scale", g))
                        else:
                            j = oc - 4
                            uc = UC[j]
                            if ti == 0:
                                P.memset("pool", uc.t[:, 0:3], 0.0, W=uc.b)
                            P.copy("act", uc.t[:, 3:3 + TT], bank.t[:, :], R=bank.b, W=uc.b)
                            ac = cacc[j % 2]
                            P.ts("pool", ac.t[:], uc.t[:, 0:TT], vcol(ps, "e_conv", j * 4 + 0), ALU.mult, R=uc.b + ps.vecs.b, W=ac.b)
                            for tap in range(1, 4):
                                P.stt(ac.t[:], uc.t[:, tap:tap + TT], vcol(ps, "e_conv", j * 4 + tap), ac.t[:], ALU.mult, ALU.add,
                                      R=uc.b + ac.b + ps.vecs.b, W=ac.b)
                            P.copy("pool", uc.t[:, 0:3], uc.t[:, TT:TT + 3], R=uc.b, W=uc.b)
                            dst, db = (qT.t[:, j, :], qT.b[j]) if j < 4 else (kT.t[:, j - 4, :], kT.b[j - 4])
                            P.act(dst, ac.t[:], AF.Silu, R=ac.b, W=[db])
                    if ti == 0:
                        P.memset("pool", Cf.t[:], 0.0, W=Cf.b)
                        P.memset("pool", Cb.t[:], 0.0, W=Cb.b)
                    gsT = gs[(q * NT + ti) % 2]
                    G = gsT.t
                    sp_all, a_all, eF, w_all, t1a, gta = G[:, 0:16], G[:, 16:32], G[:, 32:64], G[:, 64:80], G[:, 80:96], G[:, 96:128]
                    r3 = lambda ap: ap.rearrange("p (s g) -> p s g", s=4)
                    for sb in range(4):
                        cols = slice(sb * 128, (sb + 1) * 128)
                        for kc in range(8):
                            P.mm(pb[3].t[:, sb * 8:(sb + 1) * 8], h.t[:, kc, cols], win.t[:, kc, 2560:2568], kc == 0, kc == 7,
                                 R=[h.b[kc]] + win.b, W=pb[3].b)
                    P.tt("dve", gta, pb[3].t[:, 0:32], gb4.t[:], ALU.add, R=pb[3].b + gb4.b, W=gsT.b)
                    P.act(r3(sp_all), r3(gta)[:, :, 4:8], AF.Exp, R=gsT.b, W=gsT.b, scale=-1.0)
                    P.act(sp_all, sp_all, AF.Ln, R=gsT.b, W=gsT.b, bias=1.0)
                    for sb in range(4):
                        P.mm(pb[3].t[:, 32 + sb * 4:36 + sb * 4], ps.tri_f, sp_all[:, sb * 4:(sb + 1) * 4], True, True, R=gsT.b + ps.CR, W=pb[3].b)
                        P.mm(pb[3].t[:, 48 + sb * 4:52 + sb * 4], ps.ones_f, sp_all[:, sb * 4:(sb + 1) * 4], True, True, R=gsT.b + ps.CR, W=pb[3].b)
                    P.tt("dve", r3(t1a), r3(gta)[:, :, 0:4], r3(pb[3].t[:, 32:48]), ALU.add, R=gsT.b + pb[3].b, W=gsT.b)
                    P.act(a_all, t1a, AF.Exp, R=gsT.b, W=gsT.b, bias=LNS)
                    P.act(eF, pb[3].t[:, 32:64], AF.Exp, R=pb[3].b, W=gsT.b, scale=-1.0)
                    P.tt("dve", w_all, a_all, eF[:, 16:32], ALU.mult, R=gsT.b, W=gsT.b)
                    for sb in range(4):
                        cols = slice(sb * 128, (sb + 1) * 128)
                        for kc in range(8):
                            P.mm(pb[2].t[:, :], h.t[:, kc, cols], win.t[:, kc, 1536:2048], kc == 0, kc == 7, R=[h.b[kc]] + win.b, W=pb[2].b)
                        P.copy("act", vaug.t[:, sb * 4:sb * 4 + 4, 0:128], pb[2].t[:, :].rearrange("p (a b) -> p a b", a=4),
                               R=pb[2].b, W=[vaug.b[sb]])
                        for kc in range(8):
                            P.mm(pb[2].t[:, :], h.t[:, kc, cols], win.t[:, kc, 2048:2560], kc == 0, kc == 7, R=[h.b[kc]] + win.b, W=pb[2].b)
                        g_ = go[sb % 4]
                        P.act(g_.t[:], pb[2].t[:, :], AF.Sigmoid, R=pb[2].b, W=g_.b)
                        P.tt("pool", g_.t[:], g_.t[:], rows.t[:, 0:512], ALU.mult, R=g_.b + rows.b, W=g_.b)
                        s_ = gsT
                        a_, ebt, Fd, w_ = (G[:, 16 + sb * 4:20 + sb * 4], G[:, 32 + sb * 4:36 + sb * 4], G[:, 48 + sb * 4:52 + sb * 4],
                                           G[:, 64 + sb * 4:68 + sb * 4])
                        kk = ktok[sb % 2]
                        for hh in range(4):
                            P.tr(pb4b[:, hh * 128:(hh + 1) * 128], kT.t[:, hh, cols], ps.ident_b, R=[kT.b[hh]] + ps.CR, W=pb[4].b)
                        P.copy("dve", kk.t[:], pb4b[:, 0:512].rearrange("p (a b) -> p a b", a=4), R=pb[4].b, W=kk.b)
                        yt = ytok[sb % 2]
                        sbs = ss[sb % 2]
                        mv3 = sbs.t[:, 0:8].rearrange("p (a t) -> p a t", a=4)
                        vv4, rs4, nm4 = sbs.t[:, 8:12], sbs.t[:, 12:16], sbs.t[:, 16:20]
                        hms = []
                        for hh in range(4):
                            hcnt += 1
                            bank = pb[E_BANKS[hcnt % len(E_BANKS)]]
                            pt = PT[hcnt % NR]
                            v_ = vw[hcnt % NR]
                            hm_ = hm[hcnt % NR]
                            hms.append(hm_)
                            x_ = hs[hcnt % NR]
                            r_, d_, rec, scl, st6 = (x_.t[:, 0:1], x_.t[:, 1:2], x_.t[:, 2:3], x_.t[:, 3:4], x_.t[:, 4:10])
                            P.mm(bank.t[:, 0:128], kT.t[:, hh, cols], qT.t[:, hh, cols], True, True, R=[kT.b[hh], qT.b[hh]], W=bank.b)
                            P.stt(pt.t[:], bank.t[:, 0:128], a_[:, hh:hh + 1], ps.tri_b, ALU.mult, ALU.mult, R=bank.b + s_.b + ps.CR, W=pt.b)
                            P.mm(bank.t[:, 128:257], pt.t[:], vaug.t[:, sb * 4 + hh, :], True, False, R=pt.b + [vaug.b[sb]], W=bank.b)
                            P.mm(bank.t[:, 128:257], qT.t[:, hh, cols], Cb.t[:, hh, :], False, True, R=[qT.b[hh], Cb.b[hh]], W=bank.b)
                            P.tt("dve", r_, bank.t[:, 256:257], ebt[:, hh:hh + 1], ALU.mult, R=bank.b + s_.b, W=x_.b)
                            P.ts("dve", d_, r_, -1.0, ALU.mult, R=x_.b, W=x_.b)
                            P.tt("dve", d_, d_, r_, ALU.max, R=x_.b, W=x_.b)
                            P.ts("dve", d_, d_, 1.0, ALU.max, R=x_.b, W=x_.b)
                            P.op("dve", lambda e, o=rec, i=d_: e.reciprocal(o, i), R=x_.b, W=x_.b, est=120.0)
                            P.tt("dve", scl, rec, ebt[:, hh:hh + 1], ALU.mult, R=x_.b + s_.b, W=x_.b)
                            P.act(hm_.t[:], bank.t[:, 128:256], AF.Identity, R=bank.b + x_.b, W=hm_.b, scale=scl)
                            P.op("dve", lambda e, o=st6, i=hm_.t[:]: e.bn_stats(o, i), R=hm_.b, W=x_.b, est=250.0)
                            P.op("dve", lambda e, o=mv3[:, hh, :], i=st6: e.bn_aggr(o, i), R=x_.b + sbs.b, W=sbs.b, est=120.0)
                            P.ts("pool", v_.t[:], vaug.t[:, sb * 4 + hh, :], w_[:, hh:hh + 1], ALU.mult, R=[vaug.b[sb]] + s_.b, W=v_.b)
                            P.mm(pb[5].t[:, 0:129], kk.t[:, hh, :], v_.t[:], True, True, R=kk.b + v_.b, W=pb[5].b)
                            P.stt(Cf.t[:, hh, :], Cf.t[:, hh, :], Fd[:, hh:hh + 1], pb[5].t[:, 0:129], ALU.mult, ALU.add,
                                  R=[Cf.b[hh]] + s_.b + pb[5].b, W=[Cf.b[hh]])
                            P.copy("act", Cb.t[:, hh, :], Cf.t[:, hh, :], R=[Cf.b[hh]], W=[Cb.b[hh]])
                        P.act(rs4, mv3[:, :, 1], AF.Ln, R=sbs.b, W=sbs.b, bias=LN_EPS)
                        P.act(rs4, rs4, AF.Exp, R=sbs.b, W=sbs.b, scale=-0.5)
                        P.stt(nm4, mv3[:, :, 0], -1.0, rs4, ALU.mult, ALU.mult, R=sbs.b, W=sbs.b)
                        for hh in range(4):
                            hm_ = hms[hh]
                            P.act(hm_.t[:], hm_.t[:], AF.Identity, R=hm_.b + sbs.b, W=hm_.b, bias=nm4[:, hh:hh + 1], scale=rs4[:, hh:hh + 1])
                            P.tt("dve", yt.t[:, hh, :], hm_.t[:], g_.t[:, hh * 128:(hh + 1) * 128], ALU.mult, R=hm_.b + g_.b, W=yt.b)
                        for hh in range(4):
                            P.tr(pb4b[:, 512 + hh * 128:512 + (hh + 1) * 128], yt.t[:, hh, :], ps.ident_b, R=yt.b + ps.CR, W=pb[4].b)
                        P.copy("dve", mixT.t[:, 4:8, cols], pb4b[:, 512:1024].rearrange("p (a b) -> p a b", a=4), R=pb[4].b, W=mixT.b[4:8])
                    for oc in range(8):
                        bank = pb[oc % 2]
                        for kc in range(8):
                            P.mm(bank.t[:, :], wout.t[:, kc, oc * 128:(oc + 1) * 128], mixT.t[:, kc, :], kc == 0, kc == 7,
                                 R=[mixT.b[kc]] + wout.b, W=bank.b)
                        P.stt(xT.t[:, oc, :], bank.t[:, :], mod_gate(ps, 0, q, oc), xT.t[:, oc, :], ALU.mult, ALU.add,
                              R=bank.b + [xT.b[oc]] + ps.mod.b, W=[xT.b[oc]])
                        ln_stats_chunk(P, ps, lb, xT, oc, TT, pb[2], pb[3])
                    ln_finish(P, ps, lb, xT, TT, pb[2], pb[3], "e_ln_g", "e_ln_b", LN_EPS / (ALPHA * ALPHA))
                    P.dma(io.X[0][:, :, tok0:tok0 + TT].rearrange("c p t -> p c t"), xT.t[:], R=xT.b, W=[io.Xb[0]])
            P.flush()


def Fdn_alloc(P, wctx, layer):
    return Tile(P, wctx, f"f_wdn{layer}", [128, NFC, 1024], BF16)


def Fdn_load(P, lctx, io, wdn, layer, engines=None):
    stg2 = [Tile(P, lctx, f"f_stgd{layer}{i}", [128, NFC, 128], F32) for i in range(2)]
    load_cast(P, wdn, 0, wslice(io.f_w_down[layer]), 1024, stg2, NFC, blk=128, engines=engines)


def stage_F(P, cfg, io, ps, layer, sl, Xin, Xinb, Xout, Xoutb, final, wdn=None):
    S, NSEQ = cfg.S, cfg.NSEQ
    TT = 256
    NT = S // TT
    L = f"{layer}"
    gname, bname, cname = f"f_ln_g{layer}", f"f_ln_b{layer}", f"f_conv{layer}"
    with ExitStack() as wctx:
        wup = Tile(P, wctx, "f_wup" + L, [128, 8, 2 * DFF], BF16)
        pre = wdn is not None
        if not pre:
            wdn = Fdn_alloc(P, wctx, layer)
        with ExitStack() as lctx:
            stg = [Tile(P, lctx, f"f_stg{L}{i}", [128, 8, 512], F32) for i in range(2)]
            load_cast(P, wup, 0, wslice(io.f_w_up[layer]), 2 * DFF, stg, 8)
            if not pre:
                Fdn_load(P, lctx, io, wdn, layer)
            P.flush()
        with ExitStack() as cx:
            xTs = [Tile(P, cx, f"f_xT{L}{i}", [128, 8, TT], F32, nsub=8) for i in range(2)]
            hs_ = [Tile(P, cx, f"f_h{L}{i}", [128, 8, TT], BF16, nsub=8) for i in range(2)]
            ms_ = [Tile(P, cx, f"f_m{L}{i}", [128, NFC, TT], BF16, nsub=NFC) for i in range(2)]
            halo = Tile(P, cx, "f_halo" + L, [128, NFC, 2], F32, nsub=NFC)
            Gw = [Tile(P, cx, f"f_gw{L}{i}", [128, 2 + TT], F32) for i in range(4)]
            ca = [Tile(P, cx, f"f_ca{L}{i}", [128, TT], F32) for i in range(2)]
            gl = [Tile(P, cx, f"f_gl{L}{i}", [128, TT], F32) for i in range(2)]
            lb = LNBufs(P, cx, TT, "f" + L, merged=True)
            pb = [Tile(P, cx, f"f_bank{L}{i}", [128, 512], F32, psum=True) for i in range(8)]
            xo = [Tile(P, cx, f"f_xo{L}{i}", [128, 1024], F32) for i in range(2)] if final else None
            outs = []
            for q in range(NSEQ):
                for ti in range(NT):
                    tok0 = q * S + ti * TT
                    xT = xTs[(q * NT + ti) % 2]
                    h = hs_[(q * NT + ti) % 2]
                    m = ms_[(q * NT + ti) % 2]
                    P.dma(xT.t[:], Xin[:, :, tok0:tok0 + TT].rearrange("c p t -> p c t"), R=[Xinb], W=xT.b)
                    for c in range(8):
                        P.act(h.t[:, c, :], xT.t[:, c, :], AF.Identity, R=[xT.b[c]] + ps.mod.b, W=[h.b[c]],
                              bias=mod_shift(ps, sl, q, c), scale=mod_sc1(ps, sl, q, c))
                    for fc in range(NFC):
                        bg = pb[fc % 2]
                        ba = pb[2 + fc % 3]
                        for kc in range(8):
                            P.mm(bg.t[:, 0:TT], wup.t[:, kc, DFF + fc * 128:DFF + (fc + 1) * 128], h.t[:, kc, :], kc == 0, kc == 7,
                                 R=[h.b[kc]] + wup.b, W=bg.b)
                        for kc in range(8):
                            P.mm(ba.t[:, 0:TT], wup.t[:, kc, fc * 128:(fc + 1) * 128], h.t[:, kc, :], kc == 0, kc == 7,
                                 R=[h.b[kc]] + wup.b, W=ba.b)
                        gw = Gw[fc % 4]
                        if ti == 0:
                            P.memset("pool", gw.t[:, 0:2], 0.0, W=gw.b)
                        else:
                            P.copy("pool", gw.t[:, 0:2], halo.t[:, fc, :], R=[halo.b[fc]], W=gw.b)
                        P.copy("act", gw.t[:, 2:2 + TT], bg.t[:, 0:TT], R=bg.b, W=gw.b)
                        P.copy("pool", halo.t[:, fc, :], gw.t[:, TT:TT + 2], R=gw.b, W=[halo.b[fc]])
                        ac = ca[fc % 2]
                        P.act(ac.t[:], bg.t[:, 0:TT], AF.Identity, R=bg.b + ps.vecs.b, W=ac.b, scale=vcol(ps, cname, fc * 3 + 2))
                        for tap in (0, 1):
                            P.stt(ac.t[:], gw.t[:, tap:tap + TT], vcol(ps, cname, fc * 3 + tap), ac.t[:], ALU.mult, ALU.add,
                                  R=gw.b + ac.b + ps.vecs.b, W=ac.b)
                        g_ = gl[fc % 2]
                        P.act(g_.t[:], ac.t[:], AF.Gelu, R=ac.b, W=g_.b)
                        P.tt("dve", m.t[:, fc, :], ba.t[:, 0:TT], g_.t[:], ALU.mult, R=ba.b + g_.b, W=[m.b[fc]])
                    for oc in range(8):
                        bank = pb[5 + oc % 2]
                        for fc in range(NFC):
                            P.mm(bank.t[:, 0:TT], wdn.t[:, fc, oc * 128:(oc + 1) * 128], m.t[:, fc, :], fc == 0, fc == NFC - 1,
                                 R=[m.b[fc]] + wdn.b, W=bank.b)
                        P.stt(xT.t[:, oc, :], bank.t[:, 0:TT], mod_gate(ps, sl, q, oc), xT.t[:, oc, :], ALU.mult, ALU.add,
                              R=bank.b + [xT.b[oc]] + ps.mod.b, W=[xT.b[oc]])
                        ln_stats_chunk(P, ps, lb, xT, oc, TT, pb[7], None)
                    ln_finish(P, ps, lb, xT, TT, pb[7], None, gname, bname, LN_EPS / (ALPHA * ALPHA))
                    if not final:
                        P.dma(Xout[:, :, tok0:tok0 + TT].rearrange("c p t -> p c t"), xT.t[:], R=xT.b, W=[Xoutb])
                    else:
                        for sb in range(TT // 128):
                            x_ = xo[sb % 2]
                            for half in range(2):
                                bank = pb[5 + half]
                                for j in range(4):
                                    c = half * 4 + j
                                    o_ = P.tr(bank.t[:, j * 128:(j + 1) * 128], xT.t[:, c, sb * 128:(sb + 1) * 128], ps.ident_f,
                                              R=[xT.b[c]] + ps.CR, W=bank.b)
                                    o_.delay = 3000.0
                                    o_.est = 230.0
                                P.copy("dve" if half == 0 else "act", x_.t[:, half * 512:(half + 1) * 512], bank.t[:, :], R=bank.b, W=x_.b)
                            outs.append(P.dma(io.out[tok0 + sb * 128:tok0 + (sb + 1) * 128, :], x_.t[:], R=x_.b))
            P.flush(final=final)


def stage_M1(P, cfg, io, ps, Xin, Xinb):
    S, NSEQ = cfg.S, cfg.NSEQ
    TT = 512
    NT = S // TT
    sl = 2
    PI = math.pi
    C1 = 6.28125
    C2 = TWO_PI - 6.28125
    with ExitStack() as wctx:
        win = Tile(P, wctx, "m_win", [128, 8, 896], BF16)
        wuq = Tile(P, wctx, "m_wuq", [128, 4, 2048], BF16)
        wukv = Tile(P, wctx, "m_wukv", [128, 2, 2048], BF16)
        wv = Tile(P, wctx, "m_wv", [128, 2, 1024], BF16)
        with ExitStack() as lctx:
            stg = [Tile(P, lctx, f"m_stg{i}", [128, 8, 512], F32) for i in range(2)]
            load_cast(P, win, 0, wslice(io.o_w_in), 896, stg, 8)
            k = 0
            for c0 in range(0, 2048, 512):
                s_ = stg[k % 2]
                k += 1
                P.dma(s_.t[:, 0:4, :], io.o_w_uq[:, c0:c0 + 512].rearrange("(kc p) n -> p kc n", p=128), W=s_.b)
                for kc in range(4):
                    P.ts(P.any3() if False else "dve", wuq.t[:, kc, c0:c0 + 512], s_.t[:, kc, :], vcol(ps, "o_q_norm", kc), ALU.mult,
                         R=s_.b + ps.vecs.b, W=wuq.b)
            for c0 in range(0, 2048, 512):
                s_ = stg[k % 2]
                k += 1
                P.dma(s_.t[:, 0:2, :], io.o_w_ukv[:, c0:c0 + 512].rearrange("(kc p) n -> p kc n", p=128), W=s_.b)
                for kc in range(2):
                    P.ts("dve", wukv.t[:, kc, c0:c0 + 512], s_.t[:, kc, :], vcol(ps, "o_kv_norm", kc), ALU.mult,
                         R=s_.b + ps.vecs.b, W=wukv.b)
            for hh in range(8):
                P.copy("pool", wv.t[:, :, hh * 128:(hh + 1) * 128], wukv.t[:, :, hh * 256 + 128:hh * 256 + 256], R=wukv.b, W=wv.b)
            P.flush()
        with ExitStack() as cx:
            xTs = [Tile(P, cx, f"m_xT{i}", [128, 8, TT], F32, nsub=8) for i in range(2)]
            hs_ = [Tile(P, cx, f"m_h{i}", [128, 8, TT], BF16, nsub=8) for i in range(2)]
            cq = Tile(P, cx, "m_cq", [128, 6, TT], F32, nsub=6)
            sq = [Tile(P, cx, f"m_sq{i}", [128, TT], BF16) for i in range(2)]
            rr = [Tile(P, cx, f"m_rr{i}", [128, TT], F32) for i in range(2)]
            nq = Tile(P, cx, "m_nq", [128, 6, TT], BF16, nsub=6)
            posi = Tile(P, cx, "m_posi", [64, TT], I32)
            ang = Tile(P, cx, "m_ang", [64, TT], F32)
            kf = Tile(P, cx, "m_kf", [64, TT], F32)
            ki = Tile(P, cx, "m_ki", [64, TT], I32)
            msk = Tile(P, cx, "m_msk", [64, TT], F32)
            rs = Tile(P, cx, "m_rs", [64, TT], F32)
            rcs = Tile(P, cx, "m_rcs", [64, TT], F32)
            cos2 = Tile(P, cx, "m_cos2", [64, TT], F32)
            sin2 = Tile(P, cx, "m_sin2", [64, TT], F32)
            t1 = [Tile(P, cx, f"m_t1{i}", [64, TT], F32) for i in range(2)]
            t2 = [Tile(P, cx, f"m_t2{i}", [64, TT], F32) for i in range(2)]
            qn_t = Tile(P, cx, "m_qn", [128, 8, TT], BF16)
            qr_t = Tile(P, cx, "m_qr", [64, 8, TT], BF16)
            kn_t = Tile(P, cx, "m_kn", [128, 8, TT], BF16)
            kr_t = Tile(P, cx, "m_kr", [64, TT], BF16)
            v_t = Tile(P, cx, "m_v", [128, 4, 1024], BF16)
            pb = [Tile(P, cx, f"m_bank{i}", [128, 512], F32, psum=True) for i in range(8)]
            ab = io.ab
            rcnt = 0

            def rope(bA, bB, dst, dstb):
                nonlocal rcnt
                rcnt += 1
                a_, b_ = t1[rcnt % 2], t2[rcnt % 2]
                P.tt("dve", a_.t[:], bA.t[0:64, :], cos2.t[:], ALU.mult, R=bA.b + cos2.b, W=a_.b)
                P.tt("dve", b_.t[:], bB.t[0:64, :], sin2.t[:], ALU.mult, R=bB.b + sin2.b, W=b_.b)
                P.tt("pool", dst, a_.t[:], b_.t[:], ALU.add, R=a_.b + b_.b, W=dstb)

            for q in range(NSEQ):
                for ti in range(NT):
                    tok0 = q * S + ti * TT
                    xT = xTs[(q * NT + ti) % 2]
                    h = hs_[(q * NT + ti) % 2]
                    P.dma(xT.t[:], Xin[:, :, tok0:tok0 + TT].rearrange("c p t -> p c t"), R=[Xinb], W=xT.b)
                    P.dma(posi.t[:], io.pos[q, ti * TT:(ti + 1) * TT].partition_broadcast(64), W=posi.b)
                    for c in range(8):
                        P.act(h.t[:, c, :], xT.t[:, c, :], AF.Identity, R=[xT.b[c]] + ps.mod.b, W=[h.b[c]],
                              bias=mod_shift(ps, sl, q, c), scale=mod_sc1(ps, sl, q, c))
                    if DBG < 2:
                        continue
                    P.copy("dve", ang.t[:], posi.t[:], R=posi.b, W=ang.b)
                    P.ts("dve", ang.t[:], ang.t[:], vcol(ps, "inv2", 0, 64), ALU.mult, R=ang.b + ps.vecs.b, W=ang.b)
                    P.ts("dve", kf.t[:], ang.t[:], 1.0 / TWO_PI, ALU.mult, R=ang.b, W=kf.b)
                    P.copy("dve", ki.t[:], kf.t[:], R=kf.b, W=ki.b)
                    P.copy("dve", kf.t[:], ki.t[:], R=ki.b, W=kf.b)
                    P.stt(rs.t[:], kf.t[:], -C1, ang.t[:], ALU.mult, ALU.add, R=kf.b + ang.b, W=rs.b)
                    P.stt(rs.t[:], kf.t[:], -C2, rs.t[:], ALU.mult, ALU.add, R=kf.b + rs.b, W=rs.b)
                    for tgt, shift in ((rs, 0.0), (rcs, PI / 2)):
                        if tgt is rcs:
                            P.ts("dve", rcs.t[:], rs.t[:], shift, ALU.add, R=rs.b, W=rcs.b)
                        P.ts("dve", msk.t[:], tgt.t[:], PI, ALU.is_gt, R=tgt.b, W=msk.b)
                        P.stt(tgt.t[:], msk.t[:], -TWO_PI, tgt.t[:], ALU.mult, ALU.add, R=msk.b + tgt.b, W=tgt.b)
                        P.ts("dve", msk.t[:], tgt.t[:], -PI, ALU.is_lt, R=tgt.b, W=msk.b)
                        P.stt(tgt.t[:], msk.t[:], TWO_PI, tgt.t[:], ALU.mult, ALU.add, R=msk.b + tgt.b, W=tgt.b)
                    P.act(sin2.t[:], rs.t[:], AF.Sin, R=rs.b + ps.vecs.b, W=sin2.b, scale=vcol(ps, "sgn", 0, 64))
                    P.act(cos2.t[:], rcs.t[:], AF.Sin, R=rcs.b, W=cos2.b)
                    if DBG < 3:
                        continue
                    for oc in range(6):
                        bank = pb[oc % 2]
                        for kc in range(8):
                            P.mm(bank.t[:, :], win.t[:, kc, oc * 128:(oc + 1) * 128], h.t[:, kc, :], kc == 0, kc == 7,
                                 R=[h.b[kc]] + win.b, W=bank.b)
                        P.copy("dve", cq.t[:, oc, :], bank.t[:, :], R=bank.b, W=[cq.b[oc]])
                        s_ = sq[oc % 2]
                        P.act(s_.t[:], bank.t[:, :], AF.Square, R=bank.b, W=s_.b)
                        if oc < 4:
                            P.mm(pb[6].t[:, :], ps.ones_b, s_.t[:], oc == 0, oc == 3, R=s_.b + ps.CR, W=pb[6].b)
                        else:
                            P.mm(pb[7].t[:, :], ps.ones_b, s_.t[:], oc == 4, oc == 5, R=s_.b + ps.CR, W=pb[7].b)
                    for which, (bk, n, ocs) in enumerate(((pb[6], 512.0, range(0, 4)), (pb[7], 256.0, range(4, 6)))):
                        r_ = rr[which]
                        P.ts("dve", r_.t[:], bk.t[:, :], 1.0 / n, ALU.mult, R=bk.b, W=r_.b, s2=RMS_EPS, op1=ALU.add)
                        P.act(r_.t[:], r_.t[:], AF.Ln, R=r_.b, W=r_.b)
                        P.act(r_.t[:], r_.t[:], AF.Exp, R=r_.b, W=r_.b, scale=-0.5)
                        for oc in ocs:
                            P.tt("pool" if oc % 2 else "dve", nq.t[:, oc, :], cq.t[:, oc, :], r_.t[:], ALU.mult,
                                 R=[cq.b[oc]] + r_.b, W=[nq.b[oc]])
                    if DBG < 4:
                        continue
                    for kc in range(8):
                        P.mm(pb[2].t[0:64, :], win.t[:, kc, 768:832], h.t[:, kc, :], kc == 0, kc == 7, R=[h.b[kc]] + win.b, W=pb[2].b)
                    for kc in range(8):
                        P.mm(pb[3].t[0:64, :], win.t[:, kc, 832:896], h.t[:, kc, :], kc == 0, kc == 7, R=[h.b[kc]] + win.b, W=pb[3].b)
                    rope(pb[2], pb[3], kr_t.t[:], kr_t.b)
                    if DBG < 5:
                        continue
                    for hh in range(8):
                        bank = pb[hh % 2]
                        for kc in range(4):
                            P.mm(bank.t[:, :], wuq.t[:, kc, hh * 192:hh * 192 + 128], nq.t[:, kc, :], kc == 0, kc == 3,
                                 R=[nq.b[kc]] + wuq.b, W=bank.b)
                        P.copy("act", qn_t.t[:, hh, :], bank.t[:, :], R=bank.b, W=qn_t.b)
                        bA, bB = pb[2 + 2 * (hh % 2)], pb[3 + 2 * (hh % 2)]
                        for kc in range(4):
                            P.mm(bA.t[0:64, :], wuq.t[:, kc, hh * 192 + 128:hh * 192 + 192], nq.t[:, kc, :], kc == 0, kc == 3,
                                 R=[nq.b[kc]] + wuq.b, W=bA.b)
                        for kc in range(4):
                            P.mm(bB.t[0:64, :], wuq.t[:, kc, 1536 + hh * 64:1536 + (hh + 1) * 64], nq.t[:, kc, :], kc == 0, kc == 3,
                                 R=[nq.b[kc]] + wuq.b, W=bB.b)
                        rope(bA, bB, qr_t.t[:, hh, :], qr_t.b)
                        bank = pb[6 + hh % 2]
                        for kc in range(2):
                            P.mm(bank.t[:, :], wukv.t[:, kc, hh * 256:hh * 256 + 128], nq.t[:, 4 + kc, :], kc == 0, kc == 1,
                                 R=[nq.b[4 + kc]] + wukv.b, W=bank.b)
                        P.copy("act", kn_t.t[:, hh, :], bank.t[:, :], R=bank.b, W=kn_t.b)
                    if DBG < 6:
                        continue
                    for sb in range(4):
                        for cb in range(2):
                            bank = pb[cb]
                            for kc in range(2):
                                P.mm(bank.t[:, :], nq.t[:, 4 + kc, sb * 128:(sb + 1) * 128], wv.t[:, kc, cb * 512:(cb + 1) * 512],
                                     kc == 0, kc == 1, R=[nq.b[4 + kc]] + wv.b, W=bank.b)
                            P.copy("act" if cb else "dve", v_t.t[:, sb, cb * 512:(cb + 1) * 512], bank.t[:, :], R=bank.b, W=v_t.b)
                    if DBG < 7:
                        continue
                    tk = slice(tok0, tok0 + TT)
                    P.dma(io.QN[:, :, tk].rearrange("h p t -> p h t"), qn_t.t[:], R=qn_t.b, W=ab["QN"])
                    P.dma(io.QR[:, :, tk].rearrange("h p t -> p h t"), qr_t.t[:], R=qr_t.b, W=ab["QR"])
                    P.dma(io.KN[:, :, tk].rearrange("h p t -> p h t"), kn_t.t[:], R=kn_t.b, W=ab["KN"])
                    P.dma(io.KR[:, tk], kr_t.t[:], R=kr_t.b, W=ab["KR"])
                    P.dma(io.V[tk, :].rearrange("(sb p) e -> p sb e", p=128), v_t.t[:], R=v_t.b, W=ab["V"])
            P.flush()


def stage_M2(P, cfg, io, ps, prefetch=None):
    S, NSEQ = cfg.S, cfg.NSEQ
    NB = S // 128
    NSB = NB // 4
    SCALE = 192.0 ** -0.5
    ab = io.ab
    with ExitStack() as cx:
        kn = [Tile(P, cx, f"a_kn{i}", [128, S], BF16) for i in range(2)]
        kr = [Tile(P, cx, f"a_kr{i}", [128, S], BF16) for i in range(2)]
        qn = [Tile(P, cx, f"a_qn{i}", [128, S], BF16) for i in range(2)]
        qr = [Tile(P, cx, f"a_qr{i}", [128, S], BF16) for i in range(2)]
        va = [Tile(P, cx, f"a_va{i}", [128, NB, 129], BF16) for i in range(2)]
        NPT = 4
        PT = [Tile(P, cx, f"a_PT{i}", [128, 4, 128], BF16) for i in range(NPT)]
        otok = [Tile(P, cx, f"a_otok{i}", [128, 128], BF16) for i in range(2)]
        rec = [Tile(P, cx, f"a_rec{i}", [128, 1], F32) for i in range(2)]
        oT = [Tile(P, cx, f"a_oT{i}", [128, 512], BF16) for i in range(2)]
        stb = [Tile(P, cx, f"a_st{i}", [128, 512], F32, psum=True) for i in range(3)]
        acc = [Tile(P, cx, f"a_acc{i}", [128, 512], F32, psum=True) for i in range(4)]
        trb = Tile(P, cx, "a_tr", [128, 512], F32, psum=True)
        trbb = trb.t[:].bitcast(BF16)
        if prefetch is not None:
            prefetch(cx)
        for i in range(2):
            P.memset("pool", va[i].t[:, :, 128:129], 1.0, W=va[i].b)
            P.memset("pool", kr[i].t[64:128, :], 0.0, W=kr[i].b)
            P.memset("pool", qr[i].t[64:128, :], 0.0, W=qr[i].b)
        it = 0
        gcnt = 0
        qcnt = 0
        for q in range(NSEQ):
            sq = slice(q * S, (q + 1) * S)
            for hh in range(8):
                i = it % 2
                it += 1
                P.dma(kn[i].t[:], io.KN[hh, :, sq], R=ab["KN"], W=kn[i].b)
                P.dma(kr[i].t[0:64, :], io.KR[:, sq], R=ab["KR"], W=kr[i].b)
                P.dma(qn[i].t[:], io.QN[hh, :, sq], R=ab["QN"], W=qn[i].b)
                P.dma(qr[i].t[0:64, :], io.QR[hh, :, sq], R=ab["QR"], W=qr[i].b)
                for k0 in range(0, NB, 8):
                    P.dma(va[i].t[:, k0:k0 + 8, 0:128],
                          io.V[q * S + k0 * 128:q * S + (k0 + 8) * 128, hh * 128:(hh + 1) * 128].rearrange("(kb p) e -> p kb e", p=128),
                          R=ab["V"], W=va[i].b)
                for sb in range(NSB):
                    o4 = oT[sb % 2]
                    for kb in range(4 * sb + 4):
                        r = max(kb - 4 * sb, 0)
                        n = 4 - r
                        ks = slice(kb * 128, (kb + 1) * 128)
                        qs = slice((4 * sb + r) * 128, (4 * sb + 4) * 128)
                        st = stb[gcnt % 3]
                        pt = PT[gcnt % NPT]
                        gcnt += 1
                        P.mm(st.t[:, 0:n * 128], kn[i].t[:, ks], qn[i].t[:, qs], True, False, R=kn[i].b + qn[i].b, W=st.b)
                        P.mm(st.t[:, 0:n * 128], kr[i].t[:, ks], qr[i].t[:, qs], False, True, R=kr[i].b + qr[i].b, W=st.b)
                        P.act(pt.t[:, r:4, :], st.t[:, 0:n * 128].rearrange("p (a b) -> p a b", a=n), AF.Exp, R=st.b, W=pt.b, scale=SCALE)
                        if kb >= 4 * sb:
                            P.memset("pool", pt.t[64:128, r, 0:64], 0.0, W=pt.b)
                        for j in range(r, 4):
                            last = (kb == 4 * sb + j)
                            P.mm(acc[j].t[:, 0:129], pt.t[:, j, :], va[i].t[:, kb, :], kb == 0, last, R=pt.b + va[i].b, W=acc[j].b)
                            if last:
                                ac = acc[j]
                                r_ = rec[qcnt % 2]
                                ot = otok[qcnt % 2]
                                qcnt += 1
                                P.op("dve", lambda e, o=r_.t[:], a=ac.t[:, 128:129]: e.reciprocal(o, a), R=ac.b, W=r_.b, est=150.0)
                                P.ts("dve", ot.t[:], ac.t[:, 0:128], r_.t[:, 0:1], ALU.mult, R=ac.b + r_.b, W=ot.b)
                                P.tr(trbb[:, j * 128:(j + 1) * 128], ot.t[:], ps.ident_b, R=ot.b + ps.CR, W=trb.b)
                    P.copy("dve", o4.t[:], trbb[:, 0:512], R=trb.b, W=o4.b)
                    c0 = q * S + sb * 512
                    P.dma(io.OT[hh, :, c0:c0 + 512], o4.t[:], R=o4.b, W=ab["OT"])
        P.flush()


def M3_alloc(P, wctx):
    return Tile(P, wctx, "o_wout", [128, 8, 1024], BF16)


def M3_load(P, lctx, io, wout, engines=None):
    stg = [Tile(P, lctx, f"o_stg{i}", [128, 8, 256], F32) for i in range(2)]
    load_cast(P, wout, 0, wslice(io.o_w_out), 1024, stg, 8, blk=256, engines=engines)


def stage_M3(P, cfg, io, ps, Xin, Xinb, Xout, Xoutb, wout=None):
    S, NSEQ = cfg.S, cfg.NSEQ
    TT = 512
    NT = S // TT
    sl = 2
    with ExitStack() as wctx:
        if wout is None:
            wout = M3_alloc(P, wctx)
            with ExitStack() as lctx:
                M3_load(P, lctx, io, wout)
                P.flush()
        with ExitStack() as cx:
            xTs = [Tile(P, cx, f"o_xT{i}", [128, 8, TT], F32, nsub=8) for i in range(2)]
            oTs = [Tile(P, cx, f"o_oT{i}", [128, 8, TT], BF16) for i in range(2)]
            lb = LNBufs(P, cx, TT, "o", split=True)
            pb = [Tile(P, cx, f"o_bank{i}", [128, 512], F32, psum=True) for i in range(6)]
            for q in range(NSEQ):
                for ti in range(NT):
                    tok0 = q * S + ti * TT
                    tk = slice(tok0, tok0 + TT)
                    xT = xTs[(q * NT + ti) % 2]
                    oT = oTs[(q * NT + ti) % 2]
                    P.dma(xT.t[:], Xin[:, :, tk].rearrange("c p t -> p c t"), R=[Xinb], W=xT.b)
                    P.dma(oT.t[:], io.OT[:, :, tk].rearrange("h p t -> p h t"), R=io.ab["OT"], W=oT.b)
                    for oc in range(8):
                        bank = pb[oc % 4]
                        for kc in range(8):
                            P.mm(bank.t[:, :], wout.t[:, kc, oc * 128:(oc + 1) * 128], oT.t[:, kc, :], kc == 0, kc == 7,
                                 R=oT.b + wout.b, W=bank.b)
                        P.stt(xT.t[:, oc, :], bank.t[:, :], mod_gate(ps, sl, q, oc), xT.t[:, oc, :], ALU.mult, ALU.add,
                              R=bank.b + [xT.b[oc]] + ps.mod.b, W=[xT.b[oc]])
                        ln_stats_chunk(P, ps, lb, xT, oc, TT, pb[4], pb[5])
                    ln_finish(P, ps, lb, xT, TT, pb[4], pb[5], "o_ln_g", "o_ln_b", LN_EPS / (ALPHA * ALPHA))
                    P.dma(Xout[:, :, tk].rearrange("c p t -> p c t"), xT.t[:], R=xT.b, W=[Xoutb])
            P.flush()


def build_program(cfg, upto=99, dbg_x=None):
    nc = bass.Bass("TRN2", target_bir_lowering=False)
    io = declare_io(nc, cfg)
    P = Prog(nc)
    dbg = None
    if dbg_x is not None:
        dbg = [nc.dram_tensor(f"dbg{i}", [8, 128, cfg.T], F32, kind="ExternalOutput").ap() for i in dbg_x]
    with ExitStack() as ctx:
        ps = setup_persistent(P, ctx, io, cfg)
        with ExitStack() as ectx:
            EW = None
            if upto >= 1:
                EW = E_alloc(P, ectx)
            with ExitStack() as lctx:
                if upto >= 1:
                    E_load(P, lctx, io, EW)
                stage_ada(P, cfg, io, ps)
            if upto >= 1:
                stage_E(P, cfg, io, ps, EW)
        if upto >= 2:
            stage_F(P, cfg, io, ps, 0, 1, io.X[0], io.Xb[0], io.X[1], io.Xb[1], False)
        if upto >= 3:
            stage_M1(P, cfg, io, ps, io.X[1], io.Xb[1])
        if upto >= 4:
            stage_M2(P, cfg, io, ps)
        if upto >= 5:
            stage_M3(P, cfg, io, ps, io.X[1], io.Xb[1], io.X[2], io.Xb[2])
        if upto >= 6:
            stage_F(P, cfg, io, ps, 1, 3, io.X[2], io.Xb[2], io.X[3], io.Xb[3], True)
        if dbg is not None:
            for d, i in zip(dbg, dbg_x):
                P.dma(d[:, :, :], io.X[i][:, :, :], R=[io.Xb[i]])
            P.flush(final=True)
    return nc, P


def make_consts():
    c = np.zeros((128, 5, 128), np.float32)
    c[:, 0, :] = np.eye(128, dtype=np.float32)
    c[:, 1, :] = np.triu(np.ones((128, 128), np.float32))
    c[:, 2, :] = 1.0
    rc = np.zeros((128, 4, 16), np.float32)
    for g, w in enumerate((2, 4, 8, 16)):
        t = np.arange(16)
        rc[:, g, :] = (w / np.minimum(t + 1, w)).astype(np.float32)[None, :]
    return c, rc


def pack_shared(inp):
    f = lambda a: np.ascontiguousarray(np.asarray(a, np.float32))
    sh = {}
    vecs = np.zeros((128, NV), np.float32)

    def put(name, arr):
        arr = np.asarray(arr, np.float32)
        vecs[:arr.shape[0], VOFF[name]:VOFF[name] + arr.shape[1]] = arr
    adab = [inp["e_ada_b"][0], inp["f_ada_b"][0], inp["o_ada_b"][0], inp["f_ada_b"][1]]
    for sl in range(4):
        put(f"adab{sl}", chunked(adab[sl], 24))
    put("e_pool_scale", chunked(inp["e_pool_scale"][0], 4))
    cq = np.asarray(inp["e_conv_qk"][0], np.float32)
    put("e_conv", cq.reshape(4, 8, 128).transpose(2, 1, 0).reshape(128, 32))
    put("e_ln_g", chunked(inp["e_ln_g"][0], 8))
    put("e_ln_b", chunked(inp["e_ln_b"][0], 8))
    put("o_ln_g", chunked(inp["o_ln_g"][0], 8))
    put("o_ln_b", chunked(inp["o_ln_b"][0], 8))
    put("o_q_norm", chunked(inp["o_q_norm"][0], 4))
    put("o_kv_norm", chunked(inp["o_kv_norm"][0], 2))
    for l in range(2):
        fcv = np.asarray(inp["f_conv"][l], np.float32)
        put(f"f_conv{l}", fcv.reshape(3, NFC, 128).transpose(2, 1, 0).reshape(128, 66))
        put(f"f_ln_g{l}", chunked(inp["f_ln_g"][l], 8))
        put(f"f_ln_b{l}", chunked(inp["f_ln_b"][l], 8))
    half = 32
    inv = (10000.0 ** (-np.arange(half, dtype=np.float32) / half)).astype(np.float32)
    inv2 = np.zeros((128, 1), np.float32)
    inv2[:64, 0] = np.concatenate([inv, inv])
    put("inv2", inv2)
    sgn = np.ones((128, 1), np.float32)
    sgn[:32] = -1.0
    put("sgn", sgn)
    sh["vecs"] = vecs
    sh["rows"] = f(np.concatenate([inp["e_head_norm"][0], inp["e_gate_b"][0]])[None, :])
    sh["consts"], sh["rc"] = make_consts()
    ada = [inp["e_ada_w"][0], inp["f_ada_w"][0], inp["o_ada_w"][0], inp["f_ada_w"][1]]
    for i in range(4):
        sh[f"ada_w{i}"] = f(ada[i])
    sh["e_w_in"] = f(inp["e_w_in"][0])
    sh["e_pool_w"] = f(np.asarray(inp["e_pool_w"][0]).transpose(1, 0, 2))
    sh["e_w_out"] = f(inp["e_w_out"][0])
    perm = (np.arange(64) + 32) % 64
    owin = np.asarray(inp["o_w_in"][0], np.float32)
    sh["o_w_in"] = f(np.concatenate([owin, owin[:, 768:832][:, perm]], axis=1))
    wuq = np.asarray(inp["o_w_uq"][0], np.float32)
    rp = [wuq[:, hh * 192 + 128: hh * 192 + 192][:, perm] for hh in range(8)]
    sh["o_w_uq"] = f(np.concatenate([wuq] + rp, axis=1))
    sh["o_w_ukv"] = f(inp["o_w_ukv"][0])
    sh["o_w_out"] = f(inp["o_w_out"][0])
    for l in range(2):
        sh[f"f_w_up{l}"] = f(inp["f_w_up"][l])
        sh[f"f_w_down{l}"] = f(inp["f_w_down"][l])
    return sh


def pack_core(inp, core, cfg):
    NSEQ, S = cfg.NSEQ, cfg.S
    b0 = core * NSEQ
    m = {}
    m["x"] = np.ascontiguousarray(np.asarray(inp["x"][b0:b0 + NSEQ], np.float32).reshape(NSEQ * S, D))
    m["pos"] = np.ascontiguousarray(np.asarray(inp["positions"][b0:b0 + NSEQ], np.int32))
    c = np.asarray(inp["c"][b0:b0 + NSEQ], np.float32)
    m["cT"] = np.ascontiguousarray(c.reshape(NSEQ, 8, 128).transpose(2, 1, 0))
    return m


N_CORES = 8
_CACHE = {}


def kernel(**inputs):
    B, S = inputs["x"].shape[0], inputs["x"].shape[1]
    nseq = B // N_CORES
    cfg = Cfg(S, nseq)
    key = (S, nseq)
    if key not in _CACHE:
        _CACHE[key] = build_program(cfg)[0]
    nc = _CACHE[key]
    sh = pack_shared(inputs)
    in_maps = []
    for core in range(N_CORES):
        m = dict(sh)
        m.update(pack_core(inputs, core, cfg))
        in_maps.append(m)
    res = run_bass_kernel_spmd(nc, in_maps, core_ids=list(range(N_CORES)))
    out = np.concatenate([np.asarray(r["out"], np.float32).reshape(nseq, S, D) for r in res.results], axis=0)
    return out
```

```python
import math
DBG = 99
from contextlib import ExitStack
import numpy as np
import concourse.bass as bass
import concourse.mybir as mybir
from concourse.bass_utils import run_bass_kernel_spmd

F32 = mybir.dt.float32
BF16 = mybir.dt.bfloat16
I32 = mybir.dt.int32
ALU = mybir.AluOpType
AF = mybir.ActivationFunctionType

D = 1024
DFF = 2816
NFC = 22
EVEN_IN = 2568
ALPHA = 4.0 ** 0.25
LN_EPS = 1e-5
RMS_EPS = 1e-6
SEM_MAX = 3500
NDS = 40
TWO_PI = 2.0 * math.pi


ACT_SET = {AF.Exp: 'exp', AF.Ln: 'ln', AF.Sigmoid: 'sig', AF.Silu: 'silu', AF.Gelu: 'gelu', AF.Sin: 'sin'}


class Buf:
    __slots__ = ("w", "rd", "psum")

    def __init__(self, psum=False):
        self.w = None
        self.rd = []
        self.psum = psum


class Op:
    __slots__ = ("idx", "eng", "fn", "est", "deps", "ev", "inc", "is_dma", "nbytes", "waits", "ready", "fin", "fl", "extra", "aset", "delay")

    def __init__(self, eng, fn, est, is_dma=False, nbytes=0):
        self.eng = eng
        self.fn = fn
        self.est = est
        self.is_dma = is_dma
        self.nbytes = nbytes
        self.ev = None
        self.inc = 16 if is_dma else 1
        self.deps = ()
        self.waits = []
        self.ready = 0.0
        self.fin = 0.0
        self.fl = -1
        self.idx = -1
        self.extra = None
        self.aset = None
        self.delay = 0.0


class Tile:
    def __init__(self, P, ctx, name, shape, dt, nsub=1, psum=False):
        if psum:
            self.t = ctx.enter_context(P.nc.psum_tensor("ps_" + name, list(shape), dt))
        else:
            self.t = ctx.enter_context(P.nc.sbuf_tensor("sb_" + name, list(shape), dt))
        self.b = [Buf(psum) for _ in range(nsub)]


def _fsize(ap):
    n = 1
    for d in list(ap.shape)[1:]:
        n *= int(d)
    return n


class Prog:
    ENG = ("pe", "act", "dve", "pool", "sp")
    LAT = 220.0

    def __init__(self, nc):
        self.nc = nc
        self.top = ExitStack()
        self.sems = []
        self.pending = []
        self.semidx = {}
        self.cnt = {}
        for e in ("pe", "act", "dve", "pool"):
            self._newsem(e)
        self.known = {e: {} for e in self.ENG}
        self.dsem = []
        for i in range(NDS):
            self.sems.append(self.top.enter_context(nc.semaphore(f"dq{i}")))
            self.dsem.append(len(self.sems) - 1)
        self.dval = [0] * NDS
        self.dnext = 0
        self.nblk = 0
        self.rr = 0
        self.final_ops = []
        self.sched = True

    def _newsem(self, e):
        s = self.top.enter_context(self.nc.semaphore(f"s_{e}_{len(self.sems)}"))
        self.sems.append(s)
        self.semidx[e] = len(self.sems) - 1
        self.cnt[e] = 0

    def _record(self, o, R, W):
        eng = o.eng
        deps = set()
        for b in R:
            if b.w is not None:
                deps.add(b.w)
            if b.psum:
                for r in b.rd:
                    if r.eng != eng:
                        deps.add(r)
        for b in W:
            if b.w is not None:
                deps.add(b.w)
            deps.update(b.rd)
        deps.discard(o)
        o.deps = deps
        for b in R:
            rd = b.rd
            rd.append(o)
            if len(rd) > 96 and not b.psum:
                last = {}
                keep = []
                for r in rd:
                    if r.ev is None:
                        keep.append(r)
                    else:
                        k = (r.eng, r.ev[0])
                        if k not in last or last[k].ev[1] < r.ev[1]:
                            last[k] = r
                if len(keep) < 64:
                    b.rd = list(last.values()) + keep
        for b in W:
            b.w = o
            b.rd = []
        o.fl = self.nblk
        self.pending.append(o)
        return o

    def op(self, eng, fn, R=(), W=(), est=200.0):
        return self._record(Op(eng, fn, est), R, W)

    def dma(self, out, in_, R=(), W=(), eng="sp", nbytes=None):
        if nbytes is None:
            try:
                nbytes = int(out.nbytes() if callable(out.nbytes) else out.nbytes)
            except Exception:
                nbytes = 65536
        o = Op(eng, lambda e: e.dma_start(out=out, in_=in_), 60.0, True, nbytes)
        return self._record(o, R, W)

    def wait_all(self, eng, ops):
        self.final_ops.extend(ops)

    def _schedule(self, ops):
        ENG = self.ENG
        n = len(ops)
        for i, o in enumerate(ops):
            o.idx = i
            o.ready = 0.0
        npend = [0] * n
        succ = [[] for _ in range(n)]
        cur = self.nblk
        for o in ops:
            for d in o.deps:
                if d.fl == cur and d.ev is None:
                    npend[o.idx] += 1
                    succ[d.idx].append(o)
        out = {e: [] for e in ENG}
        if not self.sched:
            for o in ops:
                out[o.eng].append(o)
            return out
        cand = {e: [] for e in ENG}
        for o in ops:
            if npend[o.idx] == 0:
                cand[o.eng].append(o)
        free = {e: 0.0 for e in ENG}
        dma_bw = 0.0
        cur_set = None
        LAT = self.LAT
        remaining = n
        while remaining:
            bst = None
            bop = None
            for e in ENG:
                c = cand[e]
                if not c:
                    continue
                fa = free[e]
                pick = None
                pk = None
                if e == "act":
                    for o in c:
                        r = o.ready
                        sw = 1 if (o.aset is not None and o.aset != cur_set) else 0
                        k = (0.0, sw, o.idx) if r <= fa else (r + 1300.0 * sw, sw, o.idx)
                        if pk is None or k < pk:
                            pick, pk = o, k
                else:
                    for o in c:
                        r = o.ready
                        k = (0.0, o.idx) if r <= fa else (r, o.idx)
                        if pk is None or k < pk:
                            pick, pk = o, k
                st = fa if pick.ready <= fa else pick.ready
                if bst is None or st < bst or (st == bst and pick.idx < bop.idx):
                    bst, bop = st, pick
            o = bop
            e = o.eng
            cand[e].remove(o)
            if o.is_dma:
                free[e] = bst + 60.0
                t0 = bst if bst > dma_bw else dma_bw
                dur = o.nbytes / 150.0
                dma_bw = t0 + dur
                fin = t0 + dur + 2000.0
            else:
                fin = bst + o.est
                if o.aset is not None and o.aset != cur_set:
                    fin += 1300.0
                    cur_set = o.aset
                free[e] = fin
            o.fin = fin
            out[e].append(o)
            for s_ in succ[o.idx]:
                lat = (0.0 if (e == "pe" and s_.eng == "pe") else LAT) + s_.delay
                if fin + lat > s_.ready:
                    s_.ready = fin + lat
                npend[s_.idx] -= 1
                if npend[s_.idx] == 0:
                    cand[s_.eng].append(s_)
            remaining -= 1
        self.sim_time = max(free.values())
        return out

    def flush(self, drain=True, final=False):
        ops = self.pending
        self.pending = []
        order = self._schedule(ops)
        self.nblk += 1
        for e in ("pe", "act", "dve", "pool"):
            for o in order[e]:
                if self.cnt[e] >= SEM_MAX:
                    self._newsem(e)
                self.cnt[e] += 1
                o.ev = (self.semidx[e], self.cnt[e])
        for o in order["sp"]:
            i = self.dnext
            self.dnext = (i + 1) % NDS
            prev = self.dval[i]
            o.extra = (self.dsem[i], prev) if prev > 0 else None
            self.dval[i] = prev + 16
            o.ev = (self.dsem[i], prev + 16)
        for e in self.ENG:
            kn = self.known[e]
            for o in order[e]:
                d = {}
                for dep in o.deps:
                    if e == "pe" and dep.eng == "pe":
                        continue
                    s, v = dep.ev
                    if d.get(s, 0) < v:
                        d[s] = v
                if o.extra is not None:
                    s, v = o.extra
                    if d.get(s, 0) < v:
                        d[s] = v
                w = []
                for s, v in d.items():
                    if kn.get(s, 0) < v:
                        kn[s] = v
                        w.append((s, v))
                o.waits = w
                o.deps = ()
        tail = []
        if drain or final:
            kn = self.known["sp"]
            for i in range(NDS):
                if self.dval[i] > 0 and kn.get(self.dsem[i], 0) < self.dval[i]:
                    kn[self.dsem[i]] = self.dval[i]
                    tail.append((self.dsem[i], self.dval[i]))
        nc = self.nc
        sems = self.sems

        def mk(lst, tl):
            def body(e):
                fam = None
                for o in lst:
                    if o.aset is not None and o.aset != fam:
                        fam = o.aset
                    for s, v in o.waits:
                        e.wait_ge(sems[s], v)
                    ins = o.fn(e)
                    ins.then_inc(sems[o.ev[0]], o.inc)
                    o.fn = None
                for s, v in tl:
                    e.wait_ge(sems[s], v)
            return body

        with nc.Block() as blk:
            for name, dec in (("pe", blk.tensor), ("act", blk.scalar), ("dve", blk.vector),
                              ("pool", blk.gpsimd), ("sp", blk.sync)):
                tl = tail if name == "sp" else []
                if order[name] or tl:
                    dec(mk(order[name], tl))

    def mm(self, out, lhsT, rhs, start, stop, R, W):
        n = _fsize(rhs)
        est = max(n, 256) / 2.4 + 4.0
        if lhsT.dtype == F32:
            est *= 4.0
        return self.op("pe", lambda e: e.matmul(out, lhsT, rhs, start=start, stop=stop), R, W, est)

    def tr(self, out, in_, ident, R, W):
        return self.op("pe", lambda e: e.transpose(out, in_, ident), R, W, 60.0)

    def act(self, out, in_, func, R, W, bias=None, scale=None):
        kw = {}
        if bias is not None:
            kw["bias"] = bias
        if scale is not None:
            kw["scale"] = scale
        o = self.op("act", lambda e: e.activation(out, in_, func, **kw), R, W, 200.0 + _fsize(in_) / 1.2)
        o.aset = ACT_SET.get(func)
        return o

    def _vest(self, eng, n, f=1.0):
        if eng == "pool":
            return 120.0 + n * 1.9 * f
        return 70.0 + n * f / 0.96

    def tt(self, eng, out, in0, in1, op, R, W):
        return self.op(eng, lambda e: e.tensor_tensor(out, in0, in1, op), R, W, self._vest(eng, _fsize(in0), 1.6))

    def ts(self, eng, out, in0, s1, op0, R, W, s2=None, op1=None):
        est = self._vest(eng, _fsize(in0), 1.0)
        if op1 is None and eng == "pool":
            if op0 == ALU.mult:
                s2, op1 = 0.0, ALU.add
            elif op0 == ALU.add:
                s2, op1 = 1.0, ALU.mult
        if op1 is None:
            return self.op(eng, lambda e: e.tensor_scalar(out, in0, s1, None, op0), R, W, est)
        return self.op(eng, lambda e: e.tensor_scalar(out, in0, s1, s2, op0, op1), R, W, est)

    def stt(self, out, in0, scalar, in1, op0, op1, R, W):
        return self.op("dve", lambda e: e.scalar_tensor_tensor(out, in0, scalar, in1, op0, op1), R, W,
                       self._vest("dve", _fsize(in0), 1.8))

    def copy(self, eng, out, in_, R, W):
        if eng == "act":
            return self.op("act", lambda e: e.activation(out, in_, AF.Copy), R, W, 200.0 + _fsize(in_) / 1.2)
        return self.op(eng, lambda e: e.tensor_copy(out, in_), R, W, self._vest(eng, _fsize(in_), 1.0))

    def memset(self, eng, ap, val, W):
        return self.op(eng, lambda e: e.memset(ap, val), (), W, self._vest(eng, _fsize(ap), 0.5))

    def any3(self):
        self.rr += 1
        return ("pool", "dve", "act")[self.rr % 3]


class Cfg:
    def __init__(self, S, NSEQ):
        self.S = S
        self.NSEQ = NSEQ
        self.T = S * NSEQ


def _vec_layout():
    off = {}
    n = 0

    def add(name, w):
        nonlocal n
        off[name] = n
        n += w
    for sl in range(4):
        add(f"adab{sl}", 24)
    add("e_pool_scale", 4)
    add("e_conv", 32)
    add("e_ln_g", 8)
    add("e_ln_b", 8)
    add("o_ln_g", 8)
    add("o_ln_b", 8)
    add("o_q_norm", 4)
    add("o_kv_norm", 2)
    for l in range(2):
        add(f"f_conv{l}", 66)
        add(f"f_ln_g{l}", 8)
        add(f"f_ln_b{l}", 8)
    add("inv2", 1)
    add("sgn", 1)
    return off, n


VOFF, NV = _vec_layout()


def chunked(v, nch):
    return np.ascontiguousarray(np.asarray(v, np.float32).reshape(nch, 128).T)


class IO:
    pass


def declare_io(nc, cfg):
    io = IO()
    T, S, NSEQ = cfg.T, cfg.S, cfg.NSEQ

    def din(name, shape, dt=F32):
        return nc.dram_tensor(name, list(shape), dt, kind="ExternalInput").ap()

    def scr(name, shape, dt):
        return nc.dram_tensor(name, list(shape), dt, kind="Internal").ap()

    io.x = din("x", [T, D])
    io.pos = din("pos", [NSEQ, S], I32)
    io.cT = din("cT", [128, 8, NSEQ])
    io.vecs = din("vecs", [128, NV])
    io.rows = din("rows", [1, 520])
    io.consts = din("consts", [128, 5, 128])
    io.rc = din("rc", [128, 4, 16])
    io.ada_w = [din(f"ada_w{i}", [D, 3 * D]) for i in range(4)]
    io.e_w_in = din("e_w_in", [D, EVEN_IN])
    io.e_pool_w = din("e_pool_w", [128, 4, 128])
    io.e_w_out = din("e_w_out", [D, D])
    io.o_w_in = din("o_w_in", [D, 832 + 64])
    io.o_w_uq = din("o_w_uq", [512, 1536 + 512])
    io.o_w_ukv = din("o_w_ukv", [256, 2048])
    io.o_w_out = din("o_w_out", [D, D])
    io.f_w_up = [din(f"f_w_up{l}", [D, 2 * DFF]) for l in range(2)]
    io.f_w_down = [din(f"f_w_down{l}", [DFF, D]) for l in range(2)]
    io.out = nc.dram_tensor("out", [T, D], F32, kind="ExternalOutput").ap()
    io.X = [scr(f"X{i}", [8, 128, T], F32) for i in range(4)]
    io.QN = scr("QN", [8, 128, T], BF16)
    io.QR = scr("QR", [8, 64, T], BF16)
    io.KN = scr("KN", [8, 128, T], BF16)
    io.KR = scr("KR", [64, T], BF16)
    io.V = scr("V", [T, 1024], BF16)
    io.OT = scr("OT", [8, 128, T], BF16)
    io.Xb = [Buf() for _ in range(4)]
    io.ab = {k: [Buf()] for k in ("QN", "QR", "KN", "KR", "V", "OT")}
    return io


class Pers:
    pass


def setup_persistent(P, ctx, io, cfg):
    ps = Pers()
    ps.vecs = Tile(P, ctx, "vecs", [128, NV], F32)
    ps.cst = Tile(P, ctx, "cst", [128, 5, 128], F32)
    ps.cstb = Tile(P, ctx, "cstb", [128, 5, 128], BF16)
    ps.mod = Tile(P, ctx, "mod", [128, 4, cfg.NSEQ, 24], F32)
    P.dma(ps.vecs.t[:], io.vecs[:, :], W=ps.vecs.b)
    P.dma(ps.cst.t[:], io.consts[:, :, :], W=ps.cst.b)
    P.copy("dve", ps.cstb.t[:], ps.cst.t[:], R=ps.cst.b, W=ps.cstb.b)
    ps.ident_f = ps.cst.t[:, 0, :]
    ps.tri_f = ps.cst.t[:, 1, :]
    ps.ones_f = ps.cst.t[:, 2, :]
    ps.ident_b = ps.cstb.t[:, 0, :]
    ps.tri_b = ps.cstb.t[:, 1, :]
    ps.ones_b = ps.cstb.t[:, 2, :]
    ps.CR = ps.cst.b + ps.cstb.b + ps.vecs.b
    return ps


def vcol(ps, name, j=0, rows=128):
    c = VOFF[name] + j
    return ps.vecs.t[0:rows, c:c + 1]


def stage_ada(P, cfg, io, ps):
    NSEQ = cfg.NSEQ
    with ExitStack() as st:
        wst = [Tile(P, st, f"adaw{i}", [128, 8, 512], F32) for i in range(2)]
        sc = Tile(P, st, "silu_c", [128, 8, NSEQ], F32)
        res = Tile(P, st, "ada_res", [NSEQ, 3072], F32)
        pa = [Tile(P, st, f"ada_acc{i}", [128, 512], F32, psum=True) for i in range(2)]
        pp = Tile(P, st, "ada_T", [128, 512], F32, psum=True)
        P.dma(sc.t[:], io.cT[:, :, :], W=sc.b)
        P.act(sc.t[:], sc.t[:], AF.Silu, R=sc.b, W=sc.b)
        k = 0
        for sl in range(4):
            for cb in range(6):
                w = wst[k % 2]
                k += 1
                P.dma(w.t[:], io.ada_w[sl][:, cb * 512:(cb + 1) * 512].rearrange("(kc p) n -> p kc n", p=128), W=w.b)
                bank = pa[cb % 2]
                for kc in range(8):
                    P.mm(bank.t[0:NSEQ, :], sc.t[:, kc, :], w.t[:, kc, :], kc == 0, kc == 7, R=w.b + sc.b, W=bank.b)
                P.copy("act" if cb % 2 else "dve", res.t[:, cb * 512:(cb + 1) * 512], bank.t[0:NSEQ, :], R=bank.b, W=res.b)
            for fc in range(24):
                P.tr(pp.t[:, fc * NSEQ:(fc + 1) * NSEQ], res.t[0:NSEQ, fc * 128:(fc + 1) * 128], ps.cst.t[0:NSEQ, 0, 0:NSEQ],
                     R=res.b + ps.CR, W=pp.b)
            pv = pp.t[:, 0:24 * NSEQ].rearrange("p (f b) -> p f b", b=NSEQ)
            bias = ps.vecs.t[:, VOFF[f"adab{sl}"]:VOFF[f"adab{sl}"] + 24]
            for b in range(NSEQ):
                P.tt("dve", ps.mod.t[:, sl, b, :], pv[:, :, b], bias, ALU.add, R=pp.b + ps.vecs.b, W=ps.mod.b)
                P.ts("dve", ps.mod.t[:, sl, b, 8:16], ps.mod.t[:, sl, b, 8:16], 1.0, ALU.add, R=ps.mod.b, W=ps.mod.b)
                P.ts("dve", ps.mod.t[:, sl, b, 16:24], ps.mod.t[:, sl, b, 16:24], 1.0 / ALPHA, ALU.mult,
                     R=ps.mod.b, W=ps.mod.b)
        P.flush()


def mod_shift(ps, sl, b, c):
    return ps.mod.t[:, sl, b, c:c + 1]


def mod_sc1(ps, sl, b, c):
    return ps.mod.t[:, sl, b, 8 + c:9 + c]


def mod_gate(ps, sl, b, c):
    return ps.mod.t[:, sl, b, 16 + c:17 + c]


class LNBufs:
    def __init__(self, P, ctx, TT, tag, merged=False, split=False):
        self.merged = merged
        self.split = split
        if merged:
            self.zz = [Tile(P, ctx, f"zz{tag}{i}", [128, 2, TT], BF16, nsub=2) for i in range(2)]
        else:
            self.zb = [Tile(P, ctx, f"zb{tag}{i}", [128, TT], BF16) for i in range(2)]
            self.zq = [Tile(P, ctx, f"zq{tag}{i}", [128, TT], BF16) for i in range(2)]
        self.m = Tile(P, ctx, f"lnm{tag}", [128, TT], F32)
        self.v = Tile(P, ctx, f"lnv{tag}", [128, TT], F32)
        self.r = Tile(P, ctx, f"lnr{tag}", [128, TT], F32)


def ln_stats_chunk(P, ps, lb, xT, c, TT, S1, S2, nch=8):
    if lb.merged:
        zz = lb.zz[c % 2]
        P.copy("pool", zz.t[:, 0, :], xT.t[:, c, 0:TT], R=[xT.b[c]], W=[zz.b[0]])
        P.act(zz.t[:, 1, :], xT.t[:, c, 0:TT], AF.Square, R=[xT.b[c]], W=[zz.b[1]])
        P.mm(S1.t[:, 0:2 * TT], ps.ones_b, zz.t[:].rearrange("p a t -> p (a t)"), c == 0, c == nch - 1, R=zz.b + ps.CR, W=S1.b)
        return
    zb = lb.zb[c % 2]
    zq = lb.zq[c % 2]
    P.copy("dve" if (lb.split and c % 2 == 0) else "pool", zb.t[:], xT.t[:, c, 0:TT], R=[xT.b[c]], W=zb.b)
    P.act(zq.t[:], xT.t[:, c, 0:TT], AF.Square, R=[xT.b[c]], W=zq.b)
    P.mm(S1.t[:, 0:TT], ps.ones_b, zb.t[:], c == 0, c == nch - 1, R=zb.b + ps.CR, W=S1.b)
    P.mm(S2.t[:, 0:TT], ps.ones_b, zq.t[:], c == 0, c == nch - 1, R=zq.b + ps.CR, W=S2.b)


def ln_finish(P, ps, lb, xT, TT, S1, S2, gname, bname, eps, nfeat=1024.0):
    if lb.merged:
        s1ap, s2ap, s2b = S1.t[:, 0:TT], S1.t[:, TT:2 * TT], S1.b
    else:
        s1ap, s2ap, s2b = S1.t[:, 0:TT], S2.t[:, 0:TT], S2.b
    P.ts("dve", lb.m.t[:], s1ap, 1.0 / nfeat, ALU.mult, R=S1.b, W=lb.m.b)
    P.tt("dve", lb.v.t[:], lb.m.t[:], lb.m.t[:], ALU.mult, R=lb.m.b, W=lb.v.b)
    P.stt(lb.v.t[:], s2ap, 1.0 / nfeat, lb.v.t[:], ALU.mult, ALU.subtract, R=s2b + lb.v.b, W=lb.v.b)
    P.act(lb.r.t[:], lb.v.t[:], AF.Ln, R=lb.v.b, W=lb.r.b, bias=eps)
    P.act(lb.r.t[:], lb.r.t[:], AF.Exp, R=lb.r.b, W=lb.r.b, scale=-0.5)
    for c in range(8):
        P.stt(xT.t[:, c, 0:TT], s1ap, -1.0 / nfeat, xT.t[:, c, 0:TT], ALU.mult, ALU.add, R=[xT.b[c]] + S1.b, W=[xT.b[c]])
        P.tt("dve" if (lb.split and c % 2 == 1) else "pool", xT.t[:, c, 0:TT], xT.t[:, c, 0:TT], lb.r.t[:], ALU.mult,
             R=[xT.b[c]] + lb.r.b, W=[xT.b[c]])
        P.act(xT.t[:, c, 0:TT], xT.t[:, c, 0:TT], AF.Identity, R=[xT.b[c]] + ps.vecs.b, W=[xT.b[c]],
              bias=vcol(ps, bname, c), scale=vcol(ps, gname, c))


def load_cast(P, dst, dst_cols, src_ap_fn, ncols, stg, kchunks, blk=512, engines=None):
    k = 0
    for c0 in range(0, ncols, blk):
        n = min(blk, ncols - c0)
        s = stg[k % len(stg)]
        k += 1
        P.dma(s.t[:, 0:kchunks, 0:n], src_ap_fn(c0, n), W=s.b)
        eng = P.any3() if engines is None else engines[k % len(engines)]
        P.copy(eng, dst.t[:, 0:kchunks, dst_cols + c0:dst_cols + c0 + n], s.t[:, 0:kchunks, 0:n], R=s.b, W=dst.b)


E_NR = 5
E_BANKS = (6, 7, 0, 1)


def wslice(src, p=128):
    return lambda c0, n: src[:, c0:c0 + n].rearrange("(kc p) n -> p kc n", p=p)


def E_alloc(P, wctx):
    return (Tile(P, wctx, "e_win", [128, 8, EVEN_IN], BF16), Tile(P, wctx, "e_wout", [128, 8, 1024], BF16),
            Tile(P, wctx, "e_wpool", [128, 4, 128], BF16), Tile(P, wctx, "e_rows", [128, 520], F32),
            Tile(P, wctx, "e_rc", [128, 4, 16], F32))


def E_load(P, lctx, io, EW):
    win, wout, wpool, rows, rc = EW
    stg = [Tile(P, lctx, f"e_stg{i}", [128, 8, 512], F32) for i in range(2)]
    load_cast(P, win, 0, wslice(io.e_w_in), EVEN_IN, stg, 8)
    load_cast(P, wout, 0, wslice(io.e_w_out), 1024, stg, 8)
    s = stg[0]
    P.dma(s.t[:, 0:4, 0:128], io.e_pool_w[:, :, :], W=s.b)
    P.copy("dve", wpool.t[:], s.t[:, 0:4, 0:128], R=s.b, W=wpool.b)
    P.dma(rows.t[:], io.rows[0, :].partition_broadcast(128), W=rows.b)
    P.dma(rc.t[:], io.rc[:, :, :], W=rc.b)


def stage_E(P, cfg, io, ps, EW=None):
    S, NSEQ = cfg.S, cfg.NSEQ
    TT = 512
    NT = S // TT
    LNS = math.log(128.0 ** -0.5)
    WINS = (2, 4, 8, 16)
    with ExitStack() as wctx:
        if EW is None:
            EW = E_alloc(P, wctx)
            with ExitStack() as lctx:
                E_load(P, lctx, io, EW)
                P.flush()
        win, wout, wpool, rows, rc = EW
        WR = win.b + wout.b + wpool.b + rows.b + rc.b + ps.CR
        with ExitStack() as cx:
            xin = [Tile(P, cx, f"e_xin{i}", [128, 1024], F32) for i in range(2)]
            xTs = [Tile(P, cx, f"e_xT{i}", [128, 8, TT], F32, nsub=8) for i in range(2)]
            hs_ = [Tile(P, cx, f"e_h{i}", [128, 8, TT], BF16, nsub=8) for i in range(1)]
            U = [Tile(P, cx, f"e_U{g}", [128, 16 + TT], F32) for g in range(4)]
            PA = [Tile(P, cx, f"e_pa{i}", [128, 16 + TT], F32) for i in range(1)]
            PB = [Tile(P, cx, f"e_pb_{i}", [128, 16 + TT], F32) for i in range(1)]
            pooled = [Tile(P, cx, f"e_pooled{i}", [128, TT], BF16) for i in range(2)]
            UC = [Tile(P, cx, f"e_uc{j}", [128, 3 + TT], F32) for j in range(8)]
            cacc = [Tile(P, cx, f"e_cacc{i}", [128, TT], F32) for i in range(2)]
            qTs = [Tile(P, cx, f"e_qT{i}", [128, 4, TT], BF16, nsub=4) for i in range(2)]
            kTs = [Tile(P, cx, f"e_kT{i}", [128, 4, TT], BF16, nsub=4) for i in range(2)]
            vaug = Tile(P, cx, "e_vaug", [128, 16, 129], BF16, nsub=4)
            NR = E_NR
            gs = [Tile(P, cx, f"e_gs{i}", [128, 128], F32) for i in range(2)]
            gb4 = Tile(P, cx, "e_gb4", [128, 32], F32)
            ss = [Tile(P, cx, f"e_ss{i}", [128, 20], F32) for i in range(2)]
            hs = [Tile(P, cx, f"e_hs{i}", [128, 16], F32) for i in range(NR)]
            ktok = [Tile(P, cx, f"e_ktok{i}", [128, 4, 128], BF16) for i in range(2)]
            PT = [Tile(P, cx, f"e_PT{i}", [128, 128], BF16) for i in range(NR)]
            vw = [Tile(P, cx, f"e_vw{i}", [128, 129], BF16) for i in range(NR)]
            Cf = Tile(P, cx, "e_Cf", [128, 4, 129], F32, nsub=4)
            Cb = Tile(P, cx, "e_Cb", [128, 4, 129], BF16, nsub=4)
            hm = [Tile(P, cx, f"e_hm{i}", [128, 128], F32) for i in range(NR)]
            ytok = [Tile(P, cx, f"e_ytok{i}", [128, 4, 128], BF16) for i in range(2)]
            go = [Tile(P, cx, f"e_go{i}", [128, 512], F32) for i in range(4)]
            mixT = Tile(P, cx, "e_mix", [128, 8, TT], BF16, nsub=8)
            lb = LNBufs(P, cx, TT, "e")
            pb = [Tile(P, cx, f"e_bank{i}", [128, 512], F32, psum=True) for i in range(8)]
            pb4b = pb[4].t[:].bitcast(BF16)
            P.memset("pool", vaug.t[:, :, 128:129], 1.0, W=vaug.b)
            for sb in range(4):
                P.copy("pool", gb4.t[:, sb * 8:(sb + 1) * 8], rows.t[:, 512:520], R=rows.b, W=gb4.b)
            hcnt = 0
            for q in range(NSEQ):
                for ti in range(NT):
                    tok0 = q * S + ti * TT
                    xT = xTs[(q * NT + ti) % 2]
                    h = hs_[0]
                    qT = qTs[(q * NT + ti) % 2]
                    kT = kTs[(q * NT + ti) % 2]
                    for sb in range(4):
                        xi = xin[sb % 2]
                        P.dma(xi.t[:], io.x[tok0 + sb * 128: tok0 + (sb + 1) * 128, :], W=xi.b)
                        for half in range(2):
                            bank = pb[2 + half]
                            for j in range(4):
                                c = half * 4 + j
                                P.tr(bank.t[:, j * 128:(j + 1) * 128], xi.t[:, c * 128:(c + 1) * 128], ps.ident_f,
                                     R=xi.b + ps.CR, W=bank.b)
                            P.copy("dve" if half == 0 else "act", xT.t[:, half * 4:half * 4 + 4, sb * 128:(sb + 1) * 128],
                                   bank.t[:, :].rearrange("p (a b) -> p a b", a=4), R=bank.b, W=xT.b[half * 4:half * 4 + 4])
                    for c in range(8):
                        P.act(h.t[:, c, :], xT.t[:, c, :], AF.Identity, R=[xT.b[c]] + ps.mod.b, W=[h.b[c]],
                              bias=mod_shift(ps, 0, q, c), scale=mod_sc1(ps, 0, q, c))
                    for oc in range(12):
                        bank = pb[oc % 2]
                        for kc in range(8):
                            P.mm(bank.t[:, :], win.t[:, kc, oc * 128:(oc + 1) * 128], h.t[:, kc, :], kc == 0, kc == 7,
                                 R=[h.b[kc]] + win.b, W=bank.b)
                        if oc < 4:
                            g = oc
                            Wn = WINS[g]
                            if ti == 0:
                                P.memset("pool", U[g].t[:, 0:16], 0.0, W=U[g].b)
                            P.copy("act", U[g].t[:, 16:16 + TT], bank.t[:, :], R=bank.b, W=U[g].b)
                            cur, lo, w, k = U[g], 0, 1, 0
                            while w < Wn:
                                nxt = (PA, PB)[k % 2][0]
                                k += 1
                                P.tt("pool", nxt.t[:, lo + w:16 + TT], cur.t[:, lo + w:16 + TT], cur.t[:, lo:16 + TT - w],
                                     ALU.add, R=cur.b, W=nxt.b)
                                cur, lo, w = nxt, lo + w, 2 * w
                            if ti == 0:
                                P.tt("pool", cur.t[:, 16:32], cur.t[:, 16:32], rc.t[:, g, :], ALU.mult, R=cur.b + rc.b, W=cur.b)
                            pl = pooled[g % 2]
                            P.stt(pl.t[:], cur.t[:, 16:16 + TT], 1.0 / Wn, U[g].t[:, 16:16 + TT], ALU.mult, ALU.subtract,
                                  R=cur.b + U[g].b, W=pl.b)
                            P.copy("pool", U[g].t[:, 0:16], U[g].t[:, TT:TT + 16], R=U[g].b, W=U[g].b)
                            bk = pb[5]
                            P.mm(bk.t[:, :], wpool.t[:, g, :], pl.t[:], True, True, R=pl.b + wpool.b, W=bk.b)
                            P.act(mixT.t[:, g, :], bk.t[:, :], AF.Identity, R=bk.b + ps.vecs.b, W=[mixT.b[g]],
                                  scale=vcol(ps, "e_pool_scale", g))
                        else:
                            j = oc - 4
                            uc = UC[j]
                            if ti == 0:
                                P.memset("pool", uc.t[:, 0:3], 0.0, W=uc.b)
                            P.copy("act", uc.t[:, 3:3 + TT], bank.t[:, :], R=bank.b, W=uc.b)
                            ac = cacc[j % 2]
                            P.ts("pool", ac.t[:], uc.t[:, 0:TT], vcol(ps, "e_conv", j * 4 + 0), ALU.mult, R=uc.b + ps.vecs.b, W=ac.b)
                            for tap in range(1, 4):
                                P.stt(ac.t[:], uc.t[:, tap:tap + TT], vcol(ps, "e_conv", j * 4 + tap), ac.t[:], ALU.mult, ALU.add,
                                      R=uc.b + ac.b + ps.vecs.b, W=ac.b)
                            P.copy("pool", uc.t[:, 0:3], uc.t[:, TT:TT + 3], R=uc.b, W=uc.b)
                            dst, db = (qT.t[:, j, :], qT.b[j]) if j < 4 else (kT.t[:, j - 4, :], kT.b[j - 4])
                            P.act(dst, ac.t[:], AF.Silu, R=ac.b, W=[db])
                    if ti == 0:
                        P.memset("pool", Cf.t[:], 0.0, W=Cf.b)
                        P.memset("pool", Cb.t[:], 0.0, W=Cb.b)
                    gsT = gs[(q * NT + ti) % 2]
                    G = gsT.t
                    sp_all, a_all, eF, w_all, t1a, gta = G[:, 0:16], G[:, 16:32], G[:, 32:64], G[:, 64:80], G[:, 80:96], G[:, 96:128]
                    r3 = lambda ap: ap.rearrange("p (s g) -> p s g", s=4)
                    for sb in range(4):
                        cols = slice(sb * 128, (sb + 1) * 128)
                        for kc in range(8):
                            P.mm(pb[3].t[:, sb * 8:(sb + 1) * 8], h.t[:, kc, cols], win.t[:, kc, 2560:2568], kc == 0, kc == 7,
                                 R=[h.b[kc]] + win.b, W=pb[3].b)
                    P.tt("dve", gta, pb[3].t[:, 0:32], gb4.t[:], ALU.add, R=pb[3].b + gb4.b, W=gsT.b)
                    P.act(r3(sp_all), r3(gta)[:, :, 4:8], AF.Exp, R=gsT.b, W=gsT.b, scale=-1.0)
                    P.act(sp_all, sp_all, AF.Ln, R=gsT.b, W=gsT.b, bias=1.0)
                    for sb in range(4):
                        P.mm(pb[3].t[:, 32 + sb * 4:36 + sb * 4], ps.tri_f, sp_all[:, sb * 4:(sb + 1) * 4], True, True, R=gsT.b + ps.CR, W=pb[3].b)
                        P.mm(pb[3].t[:, 48 + sb * 4:52 + sb * 4], ps.ones_f, sp_all[:, sb * 4:(sb + 1) * 4], True, True, R=gsT.b + ps.CR, W=pb[3].b)
                    P.tt("dve", r3(t1a), r3(gta)[:, :, 0:4], r3(pb[3].t[:, 32:48]), ALU.add, R=gsT.b + pb[3].b, W=gsT.b)
                    P.act(a_all, t1a, AF.Exp, R=gsT.b, W=gsT.b, bias=LNS)
                    P.act(eF, pb[3].t[:, 32:64], AF.Exp, R=pb[3].b, W=gsT.b, scale=-1.0)
                    P.tt("dve", w_all, a_all, eF[:, 16:32], ALU.mult, R=gsT.b, W=gsT.b)
                    for sb in range(4):
                        cols = slice(sb * 128, (sb + 1) * 128)
                        for kc in range(8):
                            P.mm(pb[2].t[:, :], h.t[:, kc, cols], win.t[:, kc, 1536:2048], kc == 0, kc == 7, R=[h.b[kc]] + win.b, W=pb[2].b)
                        P.copy("act", vaug.t[:, sb * 4:sb * 4 + 4, 0:128], pb[2].t[:, :].rearrange("p (a b) -> p a b", a=4),
                               R=pb[2].b, W=[vaug.b[sb]])
                        for kc in range(8):
                            P.mm(pb[2].t[:, :], h.t[:, kc, cols], win.t[:, kc, 2048:2560], kc == 0, kc == 7, R=[h.b[kc]] + win.b, W=pb[2].b)
                        g_ = go[sb % 4]
                        P.act(g_.t[:], pb[2].t[:, :], AF.Sigmoid, R=pb[2].b, W=g_.b)
                        P.tt("pool", g_.t[:], g_.t[:], rows.t[:, 0:512], ALU.mult, R=g_.b + rows.b, W=g_.b)
                        s_ = gsT
                        a_, ebt, Fd, w_ = (G[:, 16 + sb * 4:20 + sb * 4], G[:, 32 + sb * 4:36 + sb * 4], G[:, 48 + sb * 4:52 + sb * 4],
                                           G[:, 64 + sb * 4:68 + sb * 4])
                        kk = ktok[sb % 2]
                        for hh in range(4):
                            P.tr(pb4b[:, hh * 128:(hh + 1) * 128], kT.t[:, hh, cols], ps.ident_b, R=[kT.b[hh]] + ps.CR, W=pb[4].b)
                        P.copy("dve", kk.t[:], pb4b[:, 0:512].rearrange("p (a b) -> p a b", a=4), R=pb[4].b, W=kk.b)
                        yt = ytok[sb % 2]
                        sbs = ss[sb % 2]
                        mv3 = sbs.t[:, 0:8].rearrange("p (a t) -> p a t", a=4)
                        vv4, rs4, nm4 = sbs.t[:, 8:12], sbs.t[:, 12:16], sbs.t[:, 16:20]
                        hms = []
                        for hh in range(4):
                            hcnt += 1
                            bank = pb[E_BANKS[hcnt % len(E_BANKS)]]
                            pt = PT[hcnt % NR]
                            v_ = vw[hcnt % NR]
                            hm_ = hm[hcnt % NR]
                            hms.append(hm_)
                            x_ = hs[hcnt % NR]
                            r_, d_, rec, scl, st6 = (x_.t[:, 0:1], x_.t[:, 1:2], x_.t[:, 2:3], x_.t[:, 3:4], x_.t[:, 4:10])
                            P.mm(bank.t[:, 0:128], kT.t[:, hh, cols], qT.t[:, hh, cols], True, True, R=[kT.b[hh], qT.b[hh]], W=bank.b)
                            P.stt(pt.t[:], bank.t[:, 0:128], a_[:, hh:hh + 1], ps.tri_b, ALU.mult, ALU.mult, R=bank.b + s_.b + ps.CR, W=pt.b)
                            P.mm(bank.t[:, 128:257], pt.t[:], vaug.t[:, sb * 4 + hh, :], True, False, R=pt.b + [vaug.b[sb]], W=bank.b)
                            P.mm(bank.t[:, 128:257], qT.t[:, hh, cols], Cb.t[:, hh, :], False, True, R=[qT.b[hh], Cb.b[hh]], W=bank.b)
                            P.tt("dve", r_, bank.t[:, 256:257], ebt[:, hh:hh + 1], ALU.mult, R=bank.b + s_.b, W=x_.b)
                            P.ts("dve", d_, r_, -1.0, ALU.mult, R=x_.b, W=x_.b)
                            P.tt("dve", d_, d_, r_, ALU.max, R=x_.b, W=x_.b)
                            P.ts("dve", d_, d_, 1.0, ALU.max, R=x_.b, W=x_.b)
                            P.op("dve", lambda e, o=rec, i=d_: e.reciprocal(o, i), R=x_.b, W=x_.b, est=120.0)
                            P.tt("dve", scl, rec, ebt[:, hh:hh + 1], ALU.mult, R=x_.b + s_.b, W=x_.b)
                            P.act(hm_.t[:], bank.t[:, 128:256], AF.Identity, R=bank.b + x_.b, W=hm_.b, scale=scl)
                            P.op("dve", lambda e, o=st6, i=hm_.t[:]: e.bn_stats(o, i), R=hm_.b, W=x_.b, est=250.0)
                            P.op("dve", lambda e, o=mv3[:, hh, :], i=st6: e.bn_aggr(o, i), R=x_.b + sbs.b, W=sbs.b, est=120.0)
                            P.ts("pool", v_.t[:], vaug.t[:, sb * 4 + hh, :], w_[:, hh:hh + 1], ALU.mult, R=[vaug.b[sb]] + s_.b, W=v_.b)
                            P.mm(pb[5].t[:, 0:129], kk.t[:, hh, :], v_.t[:], True, True, R=kk.b + v_.b, W=pb[5].b)
                            P.stt(Cf.t[:, hh, :], Cf.t[:, hh, :], Fd[:, hh:hh + 1], pb[5].t[:, 0:129], ALU.mult, ALU.add,
                                  R=[Cf.b[hh]] + s_.b + pb[5].b, W=[Cf.b[hh]])
                            P.copy("act", Cb.t[:, hh, :], Cf.t[:, hh, :], R=[Cf.b[hh]], W=[Cb.b[hh]])
                        P.act(rs4, mv3[:, :, 1], AF.Ln, R=sbs.b, W=sbs.b, bias=LN_EPS)
                        P.act(rs4, rs4, AF.Exp, R=sbs.b, W=sbs.b, scale=-0.5)
                        P.stt(nm4, mv3[:, :, 0], -1.0, rs4, ALU.mult, ALU.mult, R=sbs.b, W=sbs.b)
                        for hh in range(4):
                            hm_ = hms[hh]
                            P.act(hm_.t[:], hm_.t[:], AF.Identity, R=hm_.b + sbs.b, W=hm_.b, bias=nm4[:, hh:hh + 1], scale=rs4[:, hh:hh + 1])
                            P.tt("dve", yt.t[:, hh, :], hm_.t[:], g_.t[:, hh * 128:(hh + 1) * 128], ALU.mult, R=hm_.b + g_.b, W=yt.b)
                        for hh in range(4):
                            P.tr(pb4b[:, 512 + hh * 128:512 + (hh + 1) * 128], yt.t[:, hh, :], ps.ident_b, R=yt.b + ps.CR, W=pb[4].b)
                        P.copy("dve", mixT.t[:, 4:8, cols], pb4b[:, 512:1024].rearrange("p (a b) -> p a b", a=4), R=pb[4].b, W=mixT.b[4:8])
                    for oc in range(8):
                        bank = pb[(0, 1, 6, 7)[oc % 4]]
                        for kc in range(8):
                            P.mm(bank.t[:, :], wout.t[:, kc, oc * 128:(oc + 1) * 128], mixT.t[:, kc, :], kc == 0, kc == 7,
                                 R=[mixT.b[kc]] + wout.b, W=bank.b)
                        P.stt(xT.t[:, oc, :], bank.t[:, :], mod_gate(ps, 0, q, oc), xT.t[:, oc, :], ALU.mult, ALU.add,
                              R=bank.b + [xT.b[oc]] + ps.mod.b, W=[xT.b[oc]])
                        ln_stats_chunk(P, ps, lb, xT, oc, TT, pb[2], pb[3])
                    ln_finish(P, ps, lb, xT, TT, pb[2], pb[3], "e_ln_g", "e_ln_b", LN_EPS / (ALPHA * ALPHA))
                    P.dma(io.X[0][:, :, tok0:tok0 + TT].rearrange("c p t -> p c t"), xT.t[:], R=xT.b, W=[io.Xb[0]])
            P.flush()


def Fdn_alloc(P, wctx, layer):
    return Tile(P, wctx, f"f_wdn{layer}", [128, NFC, 1024], BF16)


def Fdn_load(P, lctx, io, wdn, layer, engines=None):
    stg2 = [Tile(P, lctx, f"f_stgd{layer}{i}", [128, NFC, 128], F32) for i in range(2)]
    load_cast(P, wdn, 0, wslice(io.f_w_down[layer]), 1024, stg2, NFC, blk=128, engines=engines)


def stage_F(P, cfg, io, ps, layer, sl, Xin, Xinb, Xout, Xoutb, final, wdn=None):
    S, NSEQ = cfg.S, cfg.NSEQ
    TT = 256
    NT = S // TT
    L = f"{layer}"
    gname, bname, cname = f"f_ln_g{layer}", f"f_ln_b{layer}", f"f_conv{layer}"
    with ExitStack() as wctx:
        wup = Tile(P, wctx, "f_wup" + L, [128, 8, 2 * DFF], BF16)
        pre = wdn is not None
        if not pre:
            wdn = Fdn_alloc(P, wctx, layer)
        with ExitStack() as lctx:
            stg = [Tile(P, lctx, f"f_stg{L}{i}", [128, 8, 512], F32) for i in range(2)]
            load_cast(P, wup, 0, wslice(io.f_w_up[layer]), 2 * DFF, stg, 8)
            if not pre:
                Fdn_load(P, lctx, io, wdn, layer)
            P.flush()
        with ExitStack() as cx:
            xTs = [Tile(P, cx, f"f_xT{L}{i}", [128, 8, TT], F32, nsub=8) for i in range(2)]
            hs_ = [Tile(P, cx, f"f_h{L}{i}", [128, 8, TT], BF16, nsub=8) for i in range(2)]
            ms_ = [Tile(P, cx, f"f_m{L}{i}", [128, NFC, TT], BF16, nsub=NFC) for i in range(2)]
            halo = Tile(P, cx, "f_halo" + L, [128, NFC, 2], F32, nsub=NFC)
            Gw = [Tile(P, cx, f"f_gw{L}{i}", [128, 2 + TT], F32) for i in range(4)]
            ca = [Tile(P, cx, f"f_ca{L}{i}", [128, TT], F32) for i in range(2)]
            gl = [Tile(P, cx, f"f_gl{L}{i}", [128, TT], F32) for i in range(2)]
            lb = LNBufs(P, cx, TT, "f" + L, merged=True)
            pb = [Tile(P, cx, f"f_bank{L}{i}", [128, 512], F32, psum=True) for i in range(8)]
            xo = [Tile(P, cx, f"f_xo{L}{i}", [128, 1024], F32) for i in range(2)] if final else None
            outs = []
            for q in range(NSEQ):
                for ti in range(NT):
                    tok0 = q * S + ti * TT
                    xT = xTs[(q * NT + ti) % 2]
                    h = hs_[(q * NT + ti) % 2]
                    m = ms_[(q * NT + ti) % 2]
                    P.dma(xT.t[:], Xin[:, :, tok0:tok0 + TT].rearrange("c p t -> p c t"), R=[Xinb], W=xT.b)
                    for c in range(8):
                        P.act(h.t[:, c, :], xT.t[:, c, :], AF.Identity, R=[xT.b[c]] + ps.mod.b, W=[h.b[c]],
                              bias=mod_shift(ps, sl, q, c), scale=mod_sc1(ps, sl, q, c))
                    for fc in range(NFC):
                        bg = pb[fc % 2]
                        ba = pb[2 + fc % 3]
                        for kc in range(8):
                            P.mm(bg.t[:, 0:TT], wup.t[:, kc, DFF + fc * 128:DFF + (fc + 1) * 128], h.t[:, kc, :], kc == 0, kc == 7,
                                 R=[h.b[kc]] + wup.b, W=bg.b)
                        for kc in range(8):
                            P.mm(ba.t[:, 0:TT], wup.t[:, kc, fc * 128:(fc + 1) * 128], h.t[:, kc, :], kc == 0, kc == 7,
                                 R=[h.b[kc]] + wup.b, W=ba.b)
                        gw = Gw[fc % 4]
                        if ti == 0:
                            P.memset("pool", gw.t[:, 0:2], 0.0, W=gw.b)
                        else:
                            P.copy("pool", gw.t[:, 0:2], halo.t[:, fc, :], R=[halo.b[fc]], W=gw.b)
                        P.copy("act", gw.t[:, 2:2 + TT], bg.t[:, 0:TT], R=bg.b, W=gw.b)
                        P.copy("pool", halo.t[:, fc, :], gw.t[:, TT:TT + 2], R=gw.b, W=[halo.b[fc]])
                        ac = ca[fc % 2]
                        P.act(ac.t[:], bg.t[:, 0:TT], AF.Identity, R=bg.b + ps.vecs.b, W=ac.b, scale=vcol(ps, cname, fc * 3 + 2))
                        for tap in (0, 1):
                            P.stt(ac.t[:], gw.t[:, tap:tap + TT], vcol(ps, cname, fc * 3 + tap), ac.t[:], ALU.mult, ALU.add,
                                  R=gw.b + ac.b + ps.vecs.b, W=ac.b)
                        g_ = gl[fc % 2]
                        P.act(g_.t[:], ac.t[:], AF.Gelu, R=ac.b, W=g_.b)
                        P.tt("dve", m.t[:, fc, :], ba.t[:, 0:TT], g_.t[:], ALU.mult, R=ba.b + g_.b, W=[m.b[fc]])
                    for oc in range(8):
                        bank = pb[5 + oc % 2]
                        for fc in range(NFC):
                            P.mm(bank.t[:, 0:TT], wdn.t[:, fc, oc * 128:(oc + 1) * 128], m.t[:, fc, :], fc == 0, fc == NFC - 1,
                                 R=[m.b[fc]] + wdn.b, W=bank.b)
                        P.stt(xT.t[:, oc, :], bank.t[:, 0:TT], mod_gate(ps, sl, q, oc), xT.t[:, oc, :], ALU.mult, ALU.add,
                              R=bank.b + [xT.b[oc]] + ps.mod.b, W=[xT.b[oc]])
                        ln_stats_chunk(P, ps, lb, xT, oc, TT, pb[7], None)
                    ln_finish(P, ps, lb, xT, TT, pb[7], None, gname, bname, LN_EPS / (ALPHA * ALPHA))
                    if not final:
                        P.dma(Xout[:, :, tok0:tok0 + TT].rearrange("c p t -> p c t"), xT.t[:], R=xT.b, W=[Xoutb])
                    else:
                        for sb in range(TT // 128):
                            x_ = xo[sb % 2]
                            for half in range(2):
                                bank = pb[5 + half]
                                for j in range(4):
                                    c = half * 4 + j
                                    o_ = P.tr(bank.t[:, j * 128:(j + 1) * 128], xT.t[:, c, sb * 128:(sb + 1) * 128], ps.ident_f,
                                              R=[xT.b[c]] + ps.CR, W=bank.b)
                                    o_.delay = 3000.0
                                    o_.est = 230.0
                                P.copy("dve" if half == 0 else "act", x_.t[:, half * 512:(half + 1) * 512], bank.t[:, :], R=bank.b, W=x_.b)
                            outs.append(P.dma(io.out[tok0 + sb * 128:tok0 + (sb + 1) * 128, :], x_.t[:], R=x_.b))
            P.flush(final=final)


def stage_M1(P, cfg, io, ps, Xin, Xinb):
    S, NSEQ = cfg.S, cfg.NSEQ
    TT = 512
    NT = S // TT
    sl = 2
    PI = math.pi
    C1 = 6.28125
    C2 = TWO_PI - 6.28125
    with ExitStack() as wctx:
        win = Tile(P, wctx, "m_win", [128, 8, 896], BF16)
        wuq = Tile(P, wctx, "m_wuq", [128, 4, 2048], BF16)
        wukv = Tile(P, wctx, "m_wukv", [128, 2, 2048], BF16)
        wv = Tile(P, wctx, "m_wv", [128, 2, 1024], BF16)
        with ExitStack() as lctx:
            stg = [Tile(P, lctx, f"m_stg{i}", [128, 8, 512], F32) for i in range(2)]
            load_cast(P, win, 0, wslice(io.o_w_in), 896, stg, 8)
            k = 0
            for c0 in range(0, 2048, 512):
                s_ = stg[k % 2]
                k += 1
                P.dma(s_.t[:, 0:4, :], io.o_w_uq[:, c0:c0 + 512].rearrange("(kc p) n -> p kc n", p=128), W=s_.b)
                for kc in range(4):
                    P.ts(P.any3() if False else "dve", wuq.t[:, kc, c0:c0 + 512], s_.t[:, kc, :], vcol(ps, "o_q_norm", kc), ALU.mult,
                         R=s_.b + ps.vecs.b, W=wuq.b)
            for c0 in range(0, 2048, 512):
                s_ = stg[k % 2]
                k += 1
                P.dma(s_.t[:, 0:2, :], io.o_w_ukv[:, c0:c0 + 512].rearrange("(kc p) n -> p kc n", p=128), W=s_.b)
                for kc in range(2):
                    P.ts("dve", wukv.t[:, kc, c0:c0 + 512], s_.t[:, kc, :], vcol(ps, "o_kv_norm", kc), ALU.mult,
                         R=s_.b + ps.vecs.b, W=wukv.b)
            for hh in range(8):
                P.copy("pool", wv.t[:, :, hh * 128:(hh + 1) * 128], wukv.t[:, :, hh * 256 + 128:hh * 256 + 256], R=wukv.b, W=wv.b)
            P.flush()
        with ExitStack() as cx:
            xTs = [Tile(P, cx, f"m_xT{i}", [128, 8, TT], F32, nsub=8) for i in range(2)]
            hs_ = [Tile(P, cx, f"m_h{i}", [128, 8, TT], BF16, nsub=8) for i in range(2)]
            cq = Tile(P, cx, "m_cq", [128, 6, TT], F32, nsub=6)
            sq = [Tile(P, cx, f"m_sq{i}", [128, TT], BF16) for i in range(2)]
            rr = [Tile(P, cx, f"m_rr{i}", [128, TT], F32) for i in range(2)]
            nq = Tile(P, cx, "m_nq", [128, 6, TT], BF16, nsub=6)
            posi = Tile(P, cx, "m_posi", [64, TT], I32)
            ang = Tile(P, cx, "m_ang", [64, TT], F32)
            kf = Tile(P, cx, "m_kf", [64, TT], F32)
            ki = Tile(P, cx, "m_ki", [64, TT], I32)
            msk = Tile(P, cx, "m_msk", [64, TT], F32)
            rs = Tile(P, cx, "m_rs", [64, TT], F32)
            rcs = Tile(P, cx, "m_rcs", [64, TT], F32)
            cos2 = Tile(P, cx, "m_cos2", [64, TT], F32)
            sin2 = Tile(P, cx, "m_sin2", [64, TT], F32)
            t1 = [Tile(P, cx, f"m_t1{i}", [64, TT], F32) for i in range(2)]
            t2 = [Tile(P, cx, f"m_t2{i}", [64, TT], F32) for i in range(2)]
            qn_t = Tile(P, cx, "m_qn", [128, 8, TT], BF16)
            qr_t = Tile(P, cx, "m_qr", [64, 8, TT], BF16)
            kn_t = Tile(P, cx, "m_kn", [128, 8, TT], BF16)
            kr_t = Tile(P, cx, "m_kr", [64, TT], BF16)
            v_t = Tile(P, cx, "m_v", [128, 4, 1024], BF16)
            pb = [Tile(P, cx, f"m_bank{i}", [128, 512], F32, psum=True) for i in range(8)]
            ab = io.ab
            rcnt = 0

            def rope(bA, bB, dst, dstb):
                nonlocal rcnt
                rcnt += 1
                a_, b_ = t1[rcnt % 2], t2[rcnt % 2]
                P.tt("dve", a_.t[:], bA.t[0:64, :], cos2.t[:], ALU.mult, R=bA.b + cos2.b, W=a_.b)
                P.tt("dve", b_.t[:], bB.t[0:64, :], sin2.t[:], ALU.mult, R=bB.b + sin2.b, W=b_.b)
                P.tt("pool", dst, a_.t[:], b_.t[:], ALU.add, R=a_.b + b_.b, W=dstb)

            for q in range(NSEQ):
                for ti in range(NT):
                    tok0 = q * S + ti * TT
                    xT = xTs[(q * NT + ti) % 2]
                    h = hs_[(q * NT + ti) % 2]
                    P.dma(xT.t[:], Xin[:, :, tok0:tok0 + TT].rearrange("c p t -> p c t"), R=[Xinb], W=xT.b)
                    P.dma(posi.t[:], io.pos[q, ti * TT:(ti + 1) * TT].partition_broadcast(64), W=posi.b)
                    for c in range(8):
                        P.act(h.t[:, c, :], xT.t[:, c, :], AF.Identity, R=[xT.b[c]] + ps.mod.b, W=[h.b[c]],
                              bias=mod_shift(ps, sl, q, c), scale=mod_sc1(ps, sl, q, c))
                    if DBG < 2:
                        continue
                    P.copy("dve", ang.t[:], posi.t[:], R=posi.b, W=ang.b)
                    P.ts("dve", ang.t[:], ang.t[:], vcol(ps, "inv2", 0, 64), ALU.mult, R=ang.b + ps.vecs.b, W=ang.b)
                    P.ts("dve", kf.t[:], ang.t[:], 1.0 / TWO_PI, ALU.mult, R=ang.b, W=kf.b)
                    P.copy("dve", ki.t[:], kf.t[:], R=kf.b, W=ki.b)
                    P.copy("dve", kf.t[:], ki.t[:], R=ki.b, W=kf.b)
                    P.stt(rs.t[:], kf.t[:], -C1, ang.t[:], ALU.mult, ALU.add, R=kf.b + ang.b, W=rs.b)
                    P.stt(rs.t[:], kf.t[:], -C2, rs.t[:], ALU.mult, ALU.add, R=kf.b + rs.b, W=rs.b)
                    for tgt, shift in ((rs, 0.0), (rcs, PI / 2)):
                        if tgt is rcs:
                            P.ts("dve", rcs.t[:], rs.t[:], shift, ALU.add, R=rs.b, W=rcs.b)
                        P.ts("dve", msk.t[:], tgt.t[:], PI, ALU.is_gt, R=tgt.b, W=msk.b)
                        P.stt(tgt.t[:], msk.t[:], -TWO_PI, tgt.t[:], ALU.mult, ALU.add, R=msk.b + tgt.b, W=tgt.b)
                        P.ts("dve", msk.t[:], tgt.t[:], -PI, ALU.is_lt, R=tgt.b, W=msk.b)
                        P.stt(tgt.t[:], msk.t[:], TWO_PI, tgt.t[:], ALU.mult, ALU.add, R=msk.b + tgt.b, W=tgt.b)
                    P.act(sin2.t[:], rs.t[:], AF.Sin, R=rs.b + ps.vecs.b, W=sin2.b, scale=vcol(ps, "sgn", 0, 64))
                    P.act(cos2.t[:], rcs.t[:], AF.Sin, R=rcs.b, W=cos2.b)
                    if DBG < 3:
                        continue
                    for oc in range(6):
                        bank = pb[oc % 2]
                        for kc in range(8):
                            P.mm(bank.t[:, :], win.t[:, kc, oc * 128:(oc + 1) * 128], h.t[:, kc, :], kc == 0, kc == 7,
                                 R=[h.b[kc]] + win.b, W=bank.b)
                        P.copy("dve", cq.t[:, oc, :], bank.t[:, :], R=bank.b, W=[cq.b[oc]])
                        s_ = sq[oc % 2]
                        P.act(s_.t[:], bank.t[:, :], AF.Square, R=bank.b, W=s_.b)
                        if oc < 4:
                            P.mm(pb[6].t[:, :], ps.ones_b, s_.t[:], oc == 0, oc == 3, R=s_.b + ps.CR, W=pb[6].b)
                        else:
                            P.mm(pb[7].t[:, :], ps.ones_b, s_.t[:], oc == 4, oc == 5, R=s_.b + ps.CR, W=pb[7].b)
                    for which, (bk, n, ocs) in enumerate(((pb[6], 512.0, range(0, 4)), (pb[7], 256.0, range(4, 6)))):
                        r_ = rr[which]
                        P.ts("dve", r_.t[:], bk.t[:, :], 1.0 / n, ALU.mult, R=bk.b, W=r_.b, s2=RMS_EPS, op1=ALU.add)
                        P.act(r_.t[:], r_.t[:], AF.Ln, R=r_.b, W=r_.b)
                        P.act(r_.t[:], r_.t[:], AF.Exp, R=r_.b, W=r_.b, scale=-0.5)
                        for oc in ocs:
                            P.tt("pool" if oc % 2 else "dve", nq.t[:, oc, :], cq.t[:, oc, :], r_.t[:], ALU.mult,
                                 R=[cq.b[oc]] + r_.b, W=[nq.b[oc]])
                    if DBG < 4:
                        continue
                    for kc in range(8):
                        P.mm(pb[2].t[0:64, :], win.t[:, kc, 768:832], h.t[:, kc, :], kc == 0, kc == 7, R=[h.b[kc]] + win.b, W=pb[2].b)
                    for kc in range(8):
                        P.mm(pb[3].t[0:64, :], win.t[:, kc, 832:896], h.t[:, kc, :], kc == 0, kc == 7, R=[h.b[kc]] + win.b, W=pb[3].b)
                    rope(pb[2], pb[3], kr_t.t[:], kr_t.b)
                    if DBG < 5:
                        continue
                    for hh in range(8):
                        bank = pb[hh % 2]
                        for kc in range(4):
                            P.mm(bank.t[:, :], wuq.t[:, kc, hh * 192:hh * 192 + 128], nq.t[:, kc, :], kc == 0, kc == 3,
                                 R=[nq.b[kc]] + wuq.b, W=bank.b)
                        P.copy("act", qn_t.t[:, hh, :], bank.t[:, :], R=bank.b, W=qn_t.b)
                        bA, bB = pb[2 + 2 * (hh % 2)], pb[3 + 2 * (hh % 2)]
                        for kc in range(4):
                            P.mm(bA.t[0:64, :], wuq.t[:, kc, hh * 192 + 128:hh * 192 + 192], nq.t[:, kc, :], kc == 0, kc == 3,
                                 R=[nq.b[kc]] + wuq.b, W=bA.b)
                        for kc in range(4):
                            P.mm(bB.t[0:64, :], wuq.t[:, kc, 1536 + hh * 64:1536 + (hh + 1) * 64], nq.t[:, kc, :], kc == 0, kc == 3,
                                 R=[nq.b[kc]] + wuq.b, W=bB.b)
                        rope(bA, bB, qr_t.t[:, hh, :], qr_t.b)
                        bank = pb[6 + hh % 2]
                        for kc in range(2):
                            P.mm(bank.t[:, :], wukv.t[:, kc, hh * 256:hh * 256 + 128], nq.t[:, 4 + kc, :], kc == 0, kc == 1,
                                 R=[nq.b[4 + kc]] + wukv.b, W=bank.b)
                        P.copy("act", kn_t.t[:, hh, :], bank.t[:, :], R=bank.b, W=kn_t.b)
                    if DBG < 6:
                        continue
                    for sb in range(4):
                        for cb in range(2):
                            bank = pb[cb]
                            for kc in range(2):
                                P.mm(bank.t[:, :], nq.t[:, 4 + kc, sb * 128:(sb + 1) * 128], wv.t[:, kc, cb * 512:(cb + 1) * 512],
                                     kc == 0, kc == 1, R=[nq.b[4 + kc]] + wv.b, W=bank.b)
                            P.copy("act" if cb else "dve", v_t.t[:, sb, cb * 512:(cb + 1) * 512], bank.t[:, :], R=bank.b, W=v_t.b)
                    if DBG < 7:
                        continue
                    tk = slice(tok0, tok0 + TT)
                    P.dma(io.QN[:, :, tk].rearrange("h p t -> p h t"), qn_t.t[:], R=qn_t.b, W=ab["QN"])
                    P.dma(io.QR[:, :, tk].rearrange("h p t -> p h t"), qr_t.t[:], R=qr_t.b, W=ab["QR"])
                    P.dma(io.KN[:, :, tk].rearrange("h p t -> p h t"), kn_t.t[:], R=kn_t.b, W=ab["KN"])
                    P.dma(io.KR[:, tk], kr_t.t[:], R=kr_t.b, W=ab["KR"])
                    P.dma(io.V[tk, :].rearrange("(sb p) e -> p sb e", p=128), v_t.t[:], R=v_t.b, W=ab["V"])
            P.flush()


def stage_M2(P, cfg, io, ps, prefetch=None):
    S, NSEQ = cfg.S, cfg.NSEQ
    NB = S // 128
    NSB = NB // 4
    SCALE = 192.0 ** -0.5
    ab = io.ab
    with ExitStack() as cx:
        kn = [Tile(P, cx, f"a_kn{i}", [128, S], BF16) for i in range(2)]
        kr = [Tile(P, cx, f"a_kr{i}", [128, S], BF16) for i in range(2)]
        qn = [Tile(P, cx, f"a_qn{i}", [128, S], BF16) for i in range(2)]
        qr = [Tile(P, cx, f"a_qr{i}", [128, S], BF16) for i in range(2)]
        va = [Tile(P, cx, f"a_va{i}", [128, NB, 129], BF16) for i in range(2)]
        NPT = 4
        PT = [Tile(P, cx, f"a_PT{i}", [128, 4, 128], BF16) for i in range(NPT)]
        otok = [Tile(P, cx, f"a_otok{i}", [128, 128], BF16) for i in range(2)]
        rec = [Tile(P, cx, f"a_rec{i}", [128, 1], F32) for i in range(2)]
        oT = [Tile(P, cx, f"a_oT{i}", [128, 512], BF16) for i in range(2)]
        stb = [Tile(P, cx, f"a_st{i}", [128, 512], F32, psum=True) for i in range(3)]
        acc = [Tile(P, cx, f"a_acc{i}", [128, 512], F32, psum=True) for i in range(4)]
        trb = Tile(P, cx, "a_tr", [128, 512], F32, psum=True)
        trbb = trb.t[:].bitcast(BF16)
        if prefetch is not None:
            prefetch(cx)
        for i in range(2):
            P.memset("pool", va[i].t[:, :, 128:129], 1.0, W=va[i].b)
            P.memset("pool", kr[i].t[64:128, :], 0.0, W=kr[i].b)
            P.memset("pool", qr[i].t[64:128, :], 0.0, W=qr[i].b)
        it = 0
        gcnt = 0
        qcnt = 0
        for q in range(NSEQ):
            sq = slice(q * S, (q + 1) * S)
            for hh in range(8):
                i = it % 2
                it += 1
                P.dma(kn[i].t[:], io.KN[hh, :, sq], R=ab["KN"], W=kn[i].b)
                P.dma(kr[i].t[0:64, :], io.KR[:, sq], R=ab["KR"], W=kr[i].b)
                P.dma(qn[i].t[:], io.QN[hh, :, sq], R=ab["QN"], W=qn[i].b)
                P.dma(qr[i].t[0:64, :], io.QR[hh, :, sq], R=ab["QR"], W=qr[i].b)
                for k0 in range(0, NB, 8):
                    P.dma(va[i].t[:, k0:k0 + 8, 0:128],
                          io.V[q * S + k0 * 128:q * S + (k0 + 8) * 128, hh * 128:(hh + 1) * 128].rearrange("(kb p) e -> p kb e", p=128),
                          R=ab["V"], W=va[i].b)
                for sb in range(NSB):
                    o4 = oT[sb % 2]
                    for kb in range(4 * sb + 4):
                        r = max(kb - 4 * sb, 0)
                        n = 4 - r
                        ks = slice(kb * 128, (kb + 1) * 128)
                        qs = slice((4 * sb + r) * 128, (4 * sb + 4) * 128)
                        st = stb[gcnt % 3]
                        pt = PT[gcnt % NPT]
                        gcnt += 1
                        P.mm(st.t[:, 0:n * 128], kn[i].t[:, ks], qn[i].t[:, qs], True, False, R=kn[i].b + qn[i].b, W=st.b)
                        P.mm(st.t[:, 0:n * 128], kr[i].t[:, ks], qr[i].t[:, qs], False, True, R=kr[i].b + qr[i].b, W=st.b)
                        P.act(pt.t[:, r:4, :], st.t[:, 0:n * 128].rearrange("p (a b) -> p a b", a=n), AF.Exp, R=st.b, W=pt.b, scale=SCALE)
                        if kb >= 4 * sb:
                            P.memset("pool", pt.t[64:128, r, 0:64], 0.0, W=pt.b)
                        for j in range(r, 4):
                            last = (kb == 4 * sb + j)
                            P.mm(acc[j].t[:, 0:129], pt.t[:, j, :], va[i].t[:, kb, :], kb == 0, last, R=pt.b + va[i].b, W=acc[j].b)
                            if last:
                                ac = acc[j]
                                r_ = rec[qcnt % 2]
                                ot = otok[qcnt % 2]
                                qcnt += 1
                                P.op("dve", lambda e, o=r_.t[:], a=ac.t[:, 128:129]: e.reciprocal(o, a), R=ac.b, W=r_.b, est=150.0)
                                P.ts("dve", ot.t[:], ac.t[:, 0:128], r_.t[:, 0:1], ALU.mult, R=ac.b + r_.b, W=ot.b)
                                P.tr(trbb[:, j * 128:(j + 1) * 128], ot.t[:], ps.ident_b, R=ot.b + ps.CR, W=trb.b)
                    P.copy("dve", o4.t[:], trbb[:, 0:512], R=trb.b, W=o4.b)
                    c0 = q * S + sb * 512
                    P.dma(io.OT[hh, :, c0:c0 + 512], o4.t[:], R=o4.b, W=ab["OT"])
        P.flush()


def M3_alloc(P, wctx):
    return Tile(P, wctx, "o_wout", [128, 8, 1024], BF16)


def M3_load(P, lctx, io, wout, engines=None):
    stg = [Tile(P, lctx, f"o_stg{i}", [128, 8, 256], F32) for i in range(2)]
    load_cast(P, wout, 0, wslice(io.o_w_out), 1024, stg, 8, blk=256, engines=engines)


def stage_M3(P, cfg, io, ps, Xin, Xinb, Xout, Xoutb, wout=None):
    S, NSEQ = cfg.S, cfg.NSEQ
    TT = 512
    NT = S // TT
    sl = 2
    with ExitStack() as wctx:
        if wout is None:
            wout = M3_alloc(P, wctx)
            with ExitStack() as lctx:
                M3_load(P, lctx, io, wout)
                P.flush()
        with ExitStack() as cx:
            xTs = [Tile(P, cx, f"o_xT{i}", [128, 8, TT], F32, nsub=8) for i in range(2)]
            oTs = [Tile(P, cx, f"o_oT{i}", [128, 8, TT], BF16) for i in range(2)]
            lb = LNBufs(P, cx, TT, "o", split=True)
            pb = [Tile(P, cx, f"o_bank{i}", [128, 512], F32, psum=True) for i in range(6)]
            for q in range(NSEQ):
                for ti in range(NT):
                    tok0 = q * S + ti * TT
                    tk = slice(tok0, tok0 + TT)
                    xT = xTs[(q * NT + ti) % 2]
                    oT = oTs[(q * NT + ti) % 2]
                    P.dma(xT.t[:], Xin[:, :, tk].rearrange("c p t -> p c t"), R=[Xinb], W=xT.b)
                    P.dma(oT.t[:], io.OT[:, :, tk].rearrange("h p t -> p h t"), R=io.ab["OT"], W=oT.b)
                    for oc in range(8):
                        bank = pb[oc % 4]
                        for kc in range(8):
                            P.mm(bank.t[:, :], wout.t[:, kc, oc * 128:(oc + 1) * 128], oT.t[:, kc, :], kc == 0, kc == 7,
                                 R=oT.b + wout.b, W=bank.b)
                        P.stt(xT.t[:, oc, :], bank.t[:, :], mod_gate(ps, sl, q, oc), xT.t[:, oc, :], ALU.mult, ALU.add,
                              R=bank.b + [xT.b[oc]] + ps.mod.b, W=[xT.b[oc]])
                        ln_stats_chunk(P, ps, lb, xT, oc, TT, pb[4], pb[5])
                    ln_finish(P, ps, lb, xT, TT, pb[4], pb[5], "o_ln_g", "o_ln_b", LN_EPS / (ALPHA * ALPHA))
                    P.dma(Xout[:, :, tk].rearrange("c p t -> p c t"), xT.t[:], R=xT.b, W=[Xoutb])
            P.flush()


def build_program(cfg, upto=99, dbg_x=None):
    nc = bass.Bass("TRN2", target_bir_lowering=False)
    io = declare_io(nc, cfg)
    P = Prog(nc)
    dbg = None
    if dbg_x is not None:
        dbg = [nc.dram_tensor(f"dbg{i}", [8, 128, cfg.T], F32, kind="ExternalOutput").ap() for i in dbg_x]
    with ExitStack() as ctx:
        ps = setup_persistent(P, ctx, io, cfg)
        with ExitStack() as ectx:
            EW = None
            if upto >= 1:
                EW = E_alloc(P, ectx)
            with ExitStack() as lctx:
                if upto >= 1:
                    E_load(P, lctx, io, EW)
                stage_ada(P, cfg, io, ps)
            if upto >= 1:
                stage_E(P, cfg, io, ps, EW)
        if upto >= 2:
            stage_F(P, cfg, io, ps, 0, 1, io.X[0], io.Xb[0], io.X[1], io.Xb[1], False)
        if upto >= 3:
            stage_M1(P, cfg, io, ps, io.X[1], io.Xb[1])
        if upto >= 4:
            stage_M2(P, cfg, io, ps)
        if upto >= 5:
            stage_M3(P, cfg, io, ps, io.X[1], io.Xb[1], io.X[2], io.Xb[2])
        if upto >= 6:
            stage_F(P, cfg, io, ps, 1, 3, io.X[2], io.Xb[2], io.X[3], io.Xb[3], True)
        if dbg is not None:
            for d, i in zip(dbg, dbg_x):
                P.dma(d[:, :, :], io.X[i][:, :, :], R=[io.Xb[i]])
            P.flush(final=True)
    return nc, P


def make_consts():
    c = np.zeros((128, 5, 128), np.float32)
    c[:, 0, :] = np.eye(128, dtype=np.float32)
    c[:, 1, :] = np.triu(np.ones((128, 128), np.float32))
    c[:, 2, :] = 1.0
    rc = np.zeros((128, 4, 16), np.float32)
    for g, w in enumerate((2, 4, 8, 16)):
        t = np.arange(16)
        rc[:, g, :] = (w / np.minimum(t + 1, w)).astype(np.float32)[None, :]
    return c, rc


def pack_shared(inp):
    f = lambda a: np.ascontiguousarray(np.asarray(a, np.float32))
    sh = {}
    vecs = np.zeros((128, NV), np.float32)

    def put(name, arr):
        arr = np.asarray(arr, np.float32)
        vecs[:arr.shape[0], VOFF[name]:VOFF[name] + arr.shape[1]] = arr
    adab = [inp["e_ada_b"][0], inp["f_ada_b"][0], inp["o_ada_b"][0], inp["f_ada_b"][1]]
    for sl in range(4):
        put(f"adab{sl}", chunked(adab[sl], 24))
    put("e_pool_scale", chunked(inp["e_pool_scale"][0], 4))
    cq = np.asarray(inp["e_conv_qk"][0], np.float32)
    put("e_conv", cq.reshape(4, 8, 128).transpose(2, 1, 0).reshape(128, 32))
    put("e_ln_g", chunked(inp["e_ln_g"][0], 8))
    put("e_ln_b", chunked(inp["e_ln_b"][0], 8))
    put("o_ln_g", chunked(inp["o_ln_g"][0], 8))
    put("o_ln_b", chunked(inp["o_ln_b"][0], 8))
    put("o_q_norm", chunked(inp["o_q_norm"][0], 4))
    put("o_kv_norm", chunked(inp["o_kv_norm"][0], 2))
    for l in range(2):
        fcv = np.asarray(inp["f_conv"][l], np.float32)
        put(f"f_conv{l}", fcv.reshape(3, NFC, 128).transpose(2, 1, 0).reshape(128, 66))
        put(f"f_ln_g{l}", chunked(inp["f_ln_g"][l], 8))
        put(f"f_ln_b{l}", chunked(inp["f_ln_b"][l], 8))
    half = 32
    inv = (10000.0 ** (-np.arange(half, dtype=np.float32) / half)).astype(np.float32)
    inv2 = np.zeros((128, 1), np.float32)
    inv2[:64, 0] = np.concatenate([inv, inv])
    put("inv2", inv2)
    sgn = np.ones((128, 1), np.float32)
    sgn[:32] = -1.0
    put("sgn", sgn)
    sh["vecs"] = vecs
    sh["rows"] = f(np.concatenate([inp["e_head_norm"][0], inp["e_gate_b"][0]])[None, :])
    sh["consts"], sh["rc"] = make_consts()
    ada = [inp["e_ada_w"][0], inp["f_ada_w"][0], inp["o_ada_w"][0], inp["f_ada_w"][1]]
    for i in range(4):
        sh[f"ada_w{i}"] = f(ada[i])
    sh["e_w_in"] = f(inp["e_w_in"][0])
    sh["e_pool_w"] = f(np.asarray(inp["e_pool_w"][0]).transpose(1, 0, 2))
    sh["e_w_out"] = f(inp["e_w_out"][0])
    perm = (np.arange(64) + 32) % 64
    owin = np.asarray(inp["o_w_in"][0], np.float32)
    sh["o_w_in"] = f(np.concatenate([owin, owin[:, 768:832][:, perm]], axis=1))
    wuq = np.asarray(inp["o_w_uq"][0], np.float32)
    rp = [wuq[:, hh * 192 + 128: hh * 192 + 192][:, perm] for hh in range(8)]
    sh["o_w_uq"] = f(np.concatenate([wuq] + rp, axis=1))
    sh["o_w_ukv"] = f(inp["o_w_ukv"][0])
    sh["o_w_out"] = f(inp["o_w_out"][0])
    for l in range(2):
        sh[f"f_w_up{l}"] = f(inp["f_w_up"][l])
        sh[f"f_w_down{l}"] = f(inp["f_w_down"][l])
    return sh


def pack_core(inp, core, cfg):
    NSEQ, S = cfg.NSEQ, cfg.S
    b0 = core * NSEQ
    m = {}
    m["x"] = np.ascontiguousarray(np.asarray(inp["x"][b0:b0 + NSEQ], np.float32).reshape(NSEQ * S, D))
    m["pos"] = np.ascontiguousarray(np.asarray(inp["positions"][b0:b0 + NSEQ], np.int32))
    c = np.asarray(inp["c"][b0:b0 + NSEQ], np.float32)
    m["cT"] = np.ascontiguousarray(c.reshape(NSEQ, 8, 128).transpose(2, 1, 0))
    return m


N_CORES = 8
_CACHE = {}


def kernel(**inputs):
    B, S = inputs["x"].shape[0], inputs["x"].shape[1]
    nseq = B // N_CORES
    cfg = Cfg(S, nseq)
    key = (S, nseq)
    if key not in _CACHE:
        _CACHE[key] = build_program(cfg)[0]
    nc = _CACHE[key]
    sh = pack_shared(inputs)
    in_maps = []
    for core in range(N_CORES):
        m = dict(sh)
        m.update(pack_core(inputs, core, cfg))
        in_maps.append(m)
    res = run_bass_kernel_spmd(nc, in_maps, core_ids=list(range(N_CORES)))
    out = np.concatenate([np.asarray(r["out"], np.float32).reshape(nseq, S, D) for r in res.results], axis=0)
    return out
```

```python
import math
DBG = 99
from contextlib import ExitStack
import numpy as np
import concourse.bass as bass
import concourse.mybir as mybir
from concourse.bass_utils import run_bass_kernel_spmd

F32 = mybir.dt.float32
BF16 = mybir.dt.bfloat16
I32 = mybir.dt.int32
ALU = mybir.AluOpType
AF = mybir.ActivationFunctionType

D = 1024
DFF = 2816
NFC = 22
EVEN_IN = 2568
ALPHA = 4.0 ** 0.25
LN_EPS = 1e-5
RMS_EPS = 1e-6
SEM_MAX = 3500
NDS = 40
TWO_PI = 2.0 * math.pi


ACT_SET = {AF.Exp: 'exp', AF.Ln: 'ln', AF.Sigmoid: 'sig', AF.Silu: 'silu', AF.Gelu: 'gelu', AF.Sin: 'sin'}


class Buf:
    __slots__ = ("w", "rd", "psum")

    def __init__(self, psum=False):
        self.w = None
        self.rd = []
        self.psum = psum


class Op:
    __slots__ = ("idx", "eng", "fn", "est", "deps", "ev", "inc", "is_dma", "nbytes", "waits", "ready", "fin", "fl", "extra", "aset", "delay")

    def __init__(self, eng, fn, est, is_dma=False, nbytes=0):
        self.eng = eng
        self.fn = fn
        self.est = est
        self.is_dma = is_dma
        self.nbytes = nbytes
        self.ev = None
        self.inc = 16 if is_dma else 1
        self.deps = ()
        self.waits = []
        self.ready = 0.0
        self.fin = 0.0
        self.fl = -1
        self.idx = -1
        self.extra = None
        self.aset = None
        self.delay = 0.0


class Tile:
    def __init__(self, P, ctx, name, shape, dt, nsub=1, psum=False):
        if psum:
            self.t = ctx.enter_context(P.nc.psum_tensor("ps_" + name, list(shape), dt))
        else:
            self.t = ctx.enter_context(P.nc.sbuf_tensor("sb_" + name, list(shape), dt))
        self.b = [Buf(psum) for _ in range(nsub)]


def _fsize(ap):
    n = 1
    for d in list(ap.shape)[1:]:
        n *= int(d)
    return n


class Prog:
    ENG = ("pe", "act", "dve", "pool", "sp")
    LAT = 220.0

    def __init__(self, nc):
        self.nc = nc
        self.top = ExitStack()
        self.sems = []
        self.pending = []
        self.semidx = {}
        self.cnt = {}
        for e in ("pe", "act", "dve", "pool"):
            self._newsem(e)
        self.known = {e: {} for e in self.ENG}
        self.dsem = []
        for i in range(NDS):
            self.sems.append(self.top.enter_context(nc.semaphore(f"dq{i}")))
            self.dsem.append(len(self.sems) - 1)
        self.dval = [0] * NDS
        self.dnext = 0
        self.nblk = 0
        self.rr = 0
        self.final_ops = []
        self.sched = True

    def _newsem(self, e):
        s = self.top.enter_context(self.nc.semaphore(f"s_{e}_{len(self.sems)}"))
        self.sems.append(s)
        self.semidx[e] = len(self.sems) - 1
        self.cnt[e] = 0

    def _record(self, o, R, W):
        eng = o.eng
        deps = set()
        for b in R:
            if b.w is not None:
                deps.add(b.w)
            if b.psum:
                for r in b.rd:
                    if r.eng != eng:
                        deps.add(r)
        for b in W:
            if b.w is not None:
                deps.add(b.w)
            deps.update(b.rd)
        deps.discard(o)
        o.deps = deps
        for b in R:
            rd = b.rd
            rd.append(o)
            if len(rd) > 96 and not b.psum:
                last = {}
                keep = []
                for r in rd:
                    if r.ev is None:
                        keep.append(r)
                    else:
                        k = (r.eng, r.ev[0])
                        if k not in last or last[k].ev[1] < r.ev[1]:
                            last[k] = r
                if len(keep) < 64:
                    b.rd = list(last.values()) + keep
        for b in W:
            b.w = o
            b.rd = []
        o.fl = self.nblk
        self.pending.append(o)
        return o

    def op(self, eng, fn, R=(), W=(), est=200.0):
        return self._record(Op(eng, fn, est), R, W)

    def dma(self, out, in_, R=(), W=(), eng="sp", nbytes=None):
        if nbytes is None:
            try:
                nbytes = int(out.nbytes() if callable(out.nbytes) else out.nbytes)
            except Exception:
                nbytes = 65536
        o = Op(eng, lambda e: e.dma_start(out=out, in_=in_), 60.0, True, nbytes)
        return self._record(o, R, W)

    def wait_all(self, eng, ops):
        self.final_ops.extend(ops)

    def _schedule(self, ops):
        ENG = self.ENG
        n = len(ops)
        for i, o in enumerate(ops):
            o.idx = i
            o.ready = 0.0
        npend = [0] * n
        succ = [[] for _ in range(n)]
        cur = self.nblk
        for o in ops:
            for d in o.deps:
                if d.fl == cur and d.ev is None:
                    npend[o.idx] += 1
                    succ[d.idx].append(o)
        out = {e: [] for e in ENG}
        if not self.sched:
            for o in ops:
                out[o.eng].append(o)
            return out
        cand = {e: [] for e in ENG}
        for o in ops:
            if npend[o.idx] == 0:
                cand[o.eng].append(o)
        free = {e: 0.0 for e in ENG}
        dma_bw = 0.0
        cur_set = None
        LAT = self.LAT
        remaining = n
        while remaining:
            bst = None
            bop = None
            for e in ENG:
                c = cand[e]
                if not c:
                    continue
                fa = free[e]
                pick = None
                pk = None
                if e == "act":
                    for o in c:
                        r = o.ready
                        sw = 1 if (o.aset is not None and o.aset != cur_set) else 0
                        k = (0.0, sw, o.idx) if r <= fa else (r + 1300.0 * sw, sw, o.idx)
                        if pk is None or k < pk:
                            pick, pk = o, k
                else:
                    for o in c:
                        r = o.ready
                        k = (0.0, o.idx) if r <= fa else (r, o.idx)
                        if pk is None or k < pk:
                            pick, pk = o, k
                st = fa if pick.ready <= fa else pick.ready
                if bst is None or st < bst or (st == bst and pick.idx < bop.idx):
                    bst, bop = st, pick
            o = bop
            e = o.eng
            cand[e].remove(o)
            if o.is_dma:
                free[e] = bst + 60.0
                t0 = bst if bst > dma_bw else dma_bw
                dur = o.nbytes / 150.0
                dma_bw = t0 + dur
                fin = t0 + dur + 2000.0
            else:
                fin = bst + o.est
                if o.aset is not None and o.aset != cur_set:
                    fin += 1300.0
                    cur_set = o.aset
                free[e] = fin
            o.fin = fin
            out[e].append(o)
            for s_ in succ[o.idx]:
                lat = (0.0 if (e == "pe" and s_.eng == "pe") else LAT) + s_.delay
                if fin + lat > s_.ready:
                    s_.ready = fin + lat
                npend[s_.idx] -= 1
                if npend[s_.idx] == 0:
                    cand[s_.eng].append(s_)
            remaining -= 1
        self.sim_time = max(free.values())
        return out

    def flush(self, drain=True, final=False):
        ops = self.pending
        self.pending = []
        order = self._schedule(ops)
        self.nblk += 1
        for e in ("pe", "act", "dve", "pool"):
            for o in order[e]:
                if self.cnt[e] >= SEM_MAX:
                    self._newsem(e)
                self.cnt[e] += 1
                o.ev = (self.semidx[e], self.cnt[e])
        for o in order["sp"]:
            i = self.dnext
            self.dnext = (i + 1) % NDS
            prev = self.dval[i]
            o.extra = (self.dsem[i], prev) if prev > 0 else None
            self.dval[i] = prev + 16
            o.ev = (self.dsem[i], prev + 16)
        for e in self.ENG:
            kn = self.known[e]
            for o in order[e]:
                d = {}
                for dep in o.deps:
                    if e == "pe" and dep.eng == "pe":
                        continue
                    s, v = dep.ev
                    if d.get(s, 0) < v:
                        d[s] = v
                if o.extra is not None:
                    s, v = o.extra
                    if d.get(s, 0) < v:
                        d[s] = v
                w = []
                for s, v in d.items():
                    if kn.get(s, 0) < v:
                        kn[s] = v
                        w.append((s, v))
                o.waits = w
                o.deps = ()
        tail = []
        if drain or final:
            kn = self.known["sp"]
            for i in range(NDS):
                if self.dval[i] > 0 and kn.get(self.dsem[i], 0) < self.dval[i]:
                    kn[self.dsem[i]] = self.dval[i]
                    tail.append((self.dsem[i], self.dval[i]))
        nc = self.nc
        sems = self.sems

        def mk(lst, tl):
            def body(e):
                fam = None
                for o in lst:
                    if o.aset is not None and o.aset != fam:
                        fam = o.aset
                    for s, v in o.waits:
                        e.wait_ge(sems[s], v)
                    ins = o.fn(e)
                    ins.then_inc(sems[o.ev[0]], o.inc)
                    o.fn = None
                for s, v in tl:
                    e.wait_ge(sems[s], v)
            return body

        with nc.Block() as blk:
            for name, dec in (("pe", blk.tensor), ("act", blk.scalar), ("dve", blk.vector),
                              ("pool", blk.gpsimd), ("sp", blk.sync)):
                tl = tail if name == "sp" else []
                if order[name] or tl:
                    dec(mk(order[name], tl))

    def mm(self, out, lhsT, rhs, start, stop, R, W):
        n = _fsize(rhs)
        est = max(n, 256) / 2.4 + 4.0
        if lhsT.dtype == F32:
            est *= 4.0
        return self.op("pe", lambda e: e.matmul(out, lhsT, rhs, start=start, stop=stop), R, W, est)

    def tr(self, out, in_, ident, R, W):
        return self.op("pe", lambda e: e.transpose(out, in_, ident), R, W, 60.0)

    def act(self, out, in_, func, R, W, bias=None, scale=None):
        kw = {}
        if bias is not None:
            kw["bias"] = bias
        if scale is not None:
            kw["scale"] = scale
        o = self.op("act", lambda e: e.activation(out, in_, func, **kw), R, W, 200.0 + _fsize(in_) / 1.2)
        o.aset = ACT_SET.get(func)
        return o

    def _vest(self, eng, n, f=1.0):
        if eng == "pool":
            return 120.0 + n * 1.9 * f
        return 70.0 + n * f / 0.96

    def tt(self, eng, out, in0, in1, op, R, W):
        return self.op(eng, lambda e: e.tensor_tensor(out, in0, in1, op), R, W, self._vest(eng, _fsize(in0), 1.6))

    def ts(self, eng, out, in0, s1, op0, R, W, s2=None, op1=None):
        est = self._vest(eng, _fsize(in0), 1.0)
        if op1 is None and eng == "pool":
            if op0 == ALU.mult:
                s2, op1 = 0.0, ALU.add
            elif op0 == ALU.add:
                s2, op1 = 1.0, ALU.mult
        if op1 is None:
            return self.op(eng, lambda e: e.tensor_scalar(out, in0, s1, None, op0), R, W, est)
        return self.op(eng, lambda e: e.tensor_scalar(out, in0, s1, s2, op0, op1), R, W, est)

    def stt(self, out, in0, scalar, in1, op0, op1, R, W):
        return self.op("dve", lambda e: e.scalar_tensor_tensor(out, in0, scalar, in1, op0, op1), R, W,
                       self._vest("dve", _fsize(in0), 1.8))

    def copy(self, eng, out, in_, R, W):
        if eng == "act":
            return self.op("act", lambda e: e.activation(out, in_, AF.Copy), R, W, 200.0 + _fsize(in_) / 1.2)
        return self.op(eng, lambda e: e.tensor_copy(out, in_), R, W, self._vest(eng, _fsize(in_), 1.0))

    def memset(self, eng, ap, val, W):
        return self.op(eng, lambda e: e.memset(ap, val), (), W, self._vest(eng, _fsize(ap), 0.5))

    def any3(self):
        self.rr += 1
        return ("pool", "dve", "act")[self.rr % 3]


class Cfg:
    def __init__(self, S, NSEQ):
        self.S = S
        self.NSEQ = NSEQ
        self.T = S * NSEQ


def _vec_layout():
    off = {}
    n = 0

    def add(name, w):
        nonlocal n
        off[name] = n
        n += w
    for sl in range(4):
        add(f"adab{sl}", 24)
    add("e_pool_scale", 4)
    add("e_conv", 32)
    add("e_ln_g", 8)
    add("e_ln_b", 8)
    add("o_ln_g", 8)
    add("o_ln_b", 8)
    add("o_q_norm", 4)
    add("o_kv_norm", 2)
    for l in range(2):
        add(f"f_conv{l}", 66)
        add(f"f_ln_g{l}", 8)
        add(f"f_ln_b{l}", 8)
    add("inv2", 1)
    add("sgn", 1)
    return off, n


VOFF, NV = _vec_layout()


def chunked(v, nch):
    return np.ascontiguousarray(np.asarray(v, np.float32).reshape(nch, 128).T)


class IO:
    pass


def declare_io(nc, cfg):
    io = IO()
    T, S, NSEQ = cfg.T, cfg.S, cfg.NSEQ

    def din(name, shape, dt=F32):
        return nc.dram_tensor(name, list(shape), dt, kind="ExternalInput").ap()

    def scr(name, shape, dt):
        return nc.dram_tensor(name, list(shape), dt, kind="Internal").ap()

    io.x = din("x", [T, D])
    io.pos = din("pos", [NSEQ, S], I32)
    io.cT = din("cT", [128, 8, NSEQ])
    io.vecs = din("vecs", [128, NV])
    io.rows = din("rows", [1, 520])
    io.consts = din("consts", [128, 5, 128])
    io.rc = din("rc", [128, 4, 16])
    io.ada_w = [din(f"ada_w{i}", [D, 3 * D]) for i in range(4)]
    io.e_w_in = din("e_w_in", [D, EVEN_IN])
    io.e_pool_w = din("e_pool_w", [128, 4, 128])
    io.e_w_out = din("e_w_out", [D, D])
    io.o_w_in = din("o_w_in", [D, 832 + 64])
    io.o_w_uq = din("o_w_uq", [512, 1536 + 512])
    io.o_w_ukv = din("o_w_ukv", [256, 2048])
    io.o_w_out = din("o_w_out", [D, D])
    io.f_w_up = [din(f"f_w_up{l}", [D, 2 * DFF]) for l in range(2)]
    io.f_w_down = [din(f"f_w_down{l}", [DFF, D]) for l in range(2)]
    io.out = nc.dram_tensor("out", [T, D], F32, kind="ExternalOutput").ap()
    io.X = [scr(f"X{i}", [8, 128, T], F32) for i in range(4)]
    io.QN = scr("QN", [8, 128, T], BF16)
    io.QR = scr("QR", [8, 64, T], BF16)
    io.KN = scr("KN", [8, 128, T], BF16)
    io.KR = scr("KR", [64, T], BF16)
    io.V = scr("V", [T, 1024], BF16)
    io.OT = scr("OT", [8, 128, T], BF16)
    io.Xb = [Buf() for _ in range(4)]
    io.ab = {k: [Buf()] for k in ("QN", "QR", "KN", "KR", "V", "OT")}
    return io


class Pers:
    pass


def setup_persistent(P, ctx, io, cfg):
    ps = Pers()
    ps.vecs = Tile(P, ctx, "vecs", [128, NV], F32)
    ps.cst = Tile(P, ctx, "cst", [128, 5, 128], F32)
    ps.cstb = Tile(P, ctx, "cstb", [128, 5, 128], BF16)
    ps.mod = Tile(P, ctx, "mod", [128, 4, cfg.NSEQ, 24], F32)
    P.dma(ps.vecs.t[:], io.vecs[:, :], W=ps.vecs.b)
    P.dma(ps.cst.t[:], io.consts[:, :, :], W=ps.cst.b)
    P.copy("dve", ps.cstb.t[:], ps.cst.t[:], R=ps.cst.b, W=ps.cstb.b)
    ps.ident_f = ps.cst.t[:, 0, :]
    ps.tri_f = ps.cst.t[:, 1, :]
    ps.ones_f = ps.cst.t[:, 2, :]
    ps.ident_b = ps.cstb.t[:, 0, :]
    ps.tri_b = ps.cstb.t[:, 1, :]
    ps.ones_b = ps.cstb.t[:, 2, :]
    ps.CR = ps.cst.b + ps.cstb.b + ps.vecs.b
    return ps


def vcol(ps, name, j=0, rows=128):
    c = VOFF[name] + j
    return ps.vecs.t[0:rows, c:c + 1]


def stage_ada(P, cfg, io, ps):
    NSEQ = cfg.NSEQ
    with ExitStack() as st:
        wst = [Tile(P, st, f"adaw{i}", [128, 8, 512], F32) for i in range(2)]
        sc = Tile(P, st, "silu_c", [128, 8, NSEQ], F32)
        res = Tile(P, st, "ada_res", [NSEQ, 3072], F32)
        pa = [Tile(P, st, f"ada_acc{i}", [128, 512], F32, psum=True) for i in range(2)]
        pp = Tile(P, st, "ada_T", [128, 512], F32, psum=True)
        P.dma(sc.t[:], io.cT[:, :, :], W=sc.b)
        P.act(sc.t[:], sc.t[:], AF.Silu, R=sc.b, W=sc.b)
        k = 0
        for sl in range(4):
            for cb in range(6):
                w = wst[k % 2]
                k += 1
                P.dma(w.t[:], io.ada_w[sl][:, cb * 512:(cb + 1) * 512].rearrange("(kc p) n -> p kc n", p=128), W=w.b)
                bank = pa[cb % 2]
                for kc in range(8):
                    P.mm(bank.t[0:NSEQ, :], sc.t[:, kc, :], w.t[:, kc, :], kc == 0, kc == 7, R=w.b + sc.b, W=bank.b)
                P.copy("act" if cb % 2 else "dve", res.t[:, cb * 512:(cb + 1) * 512], bank.t[0:NSEQ, :], R=bank.b, W=res.b)
            for fc in range(24):
                P.tr(pp.t[:, fc * NSEQ:(fc + 1) * NSEQ], res.t[0:NSEQ, fc * 128:(fc + 1) * 128], ps.cst.t[0:NSEQ, 0, 0:NSEQ],
                     R=res.b + ps.CR, W=pp.b)
            pv = pp.t[:, 0:24 * NSEQ].rearrange("p (f b) -> p f b", b=NSEQ)
            bias = ps.vecs.t[:, VOFF[f"adab{sl}"]:VOFF[f"adab{sl}"] + 24]
            for b in range(NSEQ):
                P.tt("dve", ps.mod.t[:, sl, b, :], pv[:, :, b], bias, ALU.add, R=pp.b + ps.vecs.b, W=ps.mod.b)
                P.ts("dve", ps.mod.t[:, sl, b, 8:16], ps.mod.t[:, sl, b, 8:16], 1.0, ALU.add, R=ps.mod.b, W=ps.mod.b)
                P.ts("dve", ps.mod.t[:, sl, b, 16:24], ps.mod.t[:, sl, b, 16:24], 1.0 / ALPHA, ALU.mult,
                     R=ps.mod.b, W=ps.mod.b)
        P.flush()


def mod_shift(ps, sl, b, c):
    return ps.mod.t[:, sl, b, c:c + 1]


def mod_sc1(ps, sl, b, c):
    return ps.mod.t[:, sl, b, 8 + c:9 + c]


def mod_gate(ps, sl, b, c):
    return ps.mod.t[:, sl, b, 16 + c:17 + c]


class LNBufs:
    def __init__(self, P, ctx, TT, tag, merged=False, split=False):
        self.merged = merged
        self.split = split
        if merged:
            self.zz = [Tile(P, ctx, f"zz{tag}{i}", [128, 2, TT], BF16, nsub=2) for i in range(2)]
        else:
            self.zb = [Tile(P, ctx, f"zb{tag}{i}", [128, TT], BF16) for i in range(2)]
            self.zq = [Tile(P, ctx, f"zq{tag}{i}", [128, TT], BF16) for i in range(2)]
        self.m = Tile(P, ctx, f"lnm{tag}", [128, TT], F32)
        self.v = Tile(P, ctx, f"lnv{tag}", [128, TT], F32)
        self.r = Tile(P, ctx, f"lnr{tag}", [128, TT], F32)


def ln_stats_chunk(P, ps, lb, xT, c, TT, S1, S2, nch=8):
    if lb.merged:
        zz = lb.zz[c % 2]
        P.copy("pool", zz.t[:, 0, :], xT.t[:, c, 0:TT], R=[xT.b[c]], W=[zz.b[0]])
        P.act(zz.t[:, 1, :], xT.t[:, c, 0:TT], AF.Square, R=[xT.b[c]], W=[zz.b[1]])
        P.mm(S1.t[:, 0:2 * TT], ps.ones_b, zz.t[:].rearrange("p a t -> p (a t)"), c == 0, c == nch - 1, R=zz.b + ps.CR, W=S1.b)
        return
    zb = lb.zb[c % 2]
    zq = lb.zq[c % 2]
    P.copy("dve" if (lb.split and c % 2 == 0) else "pool", zb.t[:], xT.t[:, c, 0:TT], R=[xT.b[c]], W=zb.b)
    P.act(zq.t[:], xT.t[:, c, 0:TT], AF.Square, R=[xT.b[c]], W=zq.b)
    P.mm(S1.t[:, 0:TT], ps.ones_b, zb.t[:], c == 0, c == nch - 1, R=zb.b + ps.CR, W=S1.b)
    P.mm(S2.t[:, 0:TT], ps.ones_b, zq.t[:], c == 0, c == nch - 1, R=zq.b + ps.CR, W=S2.b)


def ln_finish(P, ps, lb, xT, TT, S1, S2, gname, bname, eps, nfeat=1024.0):
    if lb.merged:
        s1ap, s2ap, s2b = S1.t[:, 0:TT], S1.t[:, TT:2 * TT], S1.b
    else:
        s1ap, s2ap, s2b = S1.t[:, 0:TT], S2.t[:, 0:TT], S2.b
    P.ts("dve", lb.m.t[:], s1ap, 1.0 / nfeat, ALU.mult, R=S1.b, W=lb.m.b)
    P.tt("dve", lb.v.t[:], lb.m.t[:], lb.m.t[:], ALU.mult, R=lb.m.b, W=lb.v.b)
    P.stt(lb.v.t[:], s2ap, 1.0 / nfeat, lb.v.t[:], ALU.mult, ALU.subtract, R=s2b + lb.v.b, W=lb.v.b)
    P.act(lb.r.t[:], lb.v.t[:], AF.Ln, R=lb.v.b, W=lb.r.b, bias=eps)
    P.act(lb.r.t[:], lb.r.t[:], AF.Exp, R=lb.r.b, W=lb.r.b, scale=-0.5)
    for c in range(8):
        P.stt(xT.t[:, c, 0:TT], s1ap, -1.0 / nfeat, xT.t[:, c, 0:TT], ALU.mult, ALU.add, R=[xT.b[c]] + S1.b, W=[xT.b[c]])
        P.tt("dve" if (lb.split and c % 2 == 1) else "pool", xT.t[:, c, 0:TT], xT.t[:, c, 0:TT], lb.r.t[:], ALU.mult,
             R=[xT.b[c]] + lb.r.b, W=[xT.b[c]])
        P.act(xT.t[:, c, 0:TT], xT.t[:, c, 0:TT], AF.Identity, R=[xT.b[c]] + ps.vecs.b, W=[xT.b[c]],
              bias=vcol(ps, bname, c), scale=vcol(ps, gname, c))


def load_cast(P, dst, dst_cols, src_ap_fn, ncols, stg, kchunks, blk=512, engines=None):
    k = 0
    for c0 in range(0, ncols, blk):
        n = min(blk, ncols - c0)
        s = stg[k % len(stg)]
        k += 1
        P.dma(s.t[:, 0:kchunks, 0:n], src_ap_fn(c0, n), W=s.b)
        eng = P.any3() if engines is None else engines[k % len(engines)]
        P.copy(eng, dst.t[:, 0:kchunks, dst_cols + c0:dst_cols + c0 + n], s.t[:, 0:kchunks, 0:n], R=s.b, W=dst.b)


E_NR = 5
E_BANKS = (6, 7, 0, 1)


def wslice(src, p=128):
    return lambda c0, n: src[:, c0:c0 + n].rearrange("(kc p) n -> p kc n", p=p)


def E_alloc(P, wctx):
    return (Tile(P, wctx, "e_win", [128, 8, EVEN_IN], BF16), Tile(P, wctx, "e_wout", [128, 8, 1024], BF16),
            Tile(P, wctx, "e_wpool", [128, 4, 128], BF16), Tile(P, wctx, "e_rows", [128, 520], F32),
            Tile(P, wctx, "e_rc", [128, 4, 16], F32))


def E_load(P, lctx, io, EW):
    win, wout, wpool, rows, rc = EW
    stg = [Tile(P, lctx, f"e_stg{i}", [128, 8, 512], F32) for i in range(2)]
    load_cast(P, win, 0, wslice(io.e_w_in), EVEN_IN, stg, 8)
    load_cast(P, wout, 0, wslice(io.e_w_out), 1024, stg, 8)
    s = stg[0]
    P.dma(s.t[:, 0:4, 0:128], io.e_pool_w[:, :, :], W=s.b)
    P.copy("dve", wpool.t[:], s.t[:, 0:4, 0:128], R=s.b, W=wpool.b)
    P.dma(rows.t[:], io.rows[0, :].partition_broadcast(128), W=rows.b)
    P.dma(rc.t[:], io.rc[:, :, :], W=rc.b)


def stage_E(P, cfg, io, ps, EW=None):
    S, NSEQ = cfg.S, cfg.NSEQ
    TT = 512
    NT = S // TT
    LNS = math.log(128.0 ** -0.5)
    WINS = (2, 4, 8, 16)
    with ExitStack() as wctx:
        if EW is None:
            EW = E_alloc(P, wctx)
            with ExitStack() as lctx:
                E_load(P, lctx, io, EW)
                P.flush()
        win, wout, wpool, rows, rc = EW
        WR = win.b + wout.b + wpool.b + rows.b + rc.b + ps.CR
        with ExitStack() as cx:
            xin = [Tile(P, cx, f"e_xin{i}", [128, 1024], F32) for i in range(2)]
            xTs = [Tile(P, cx, f"e_xT{i}", [128, 8, TT], F32, nsub=8) for i in range(2)]
            hs_ = [Tile(P, cx, f"e_h{i}", [128, 8, TT], BF16, nsub=8) for i in range(1)]
            U = [Tile(P, cx, f"e_U{g}", [128, 16 + TT], F32) for g in range(4)]
            PA = [Tile(P, cx, f"e_pa{i}", [128, 16 + TT], F32) for i in range(1)]
            PB = [Tile(P, cx, f"e_pb_{i}", [128, 16 + TT], F32) for i in range(1)]
            pooled = [Tile(P, cx, f"e_pooled{i}", [128, TT], BF16) for i in range(2)]
            UC = [Tile(P, cx, f"e_uc{j}", [128, 3 + TT], F32) for j in range(8)]
            cacc = [Tile(P, cx, f"e_cacc{i}", [128, TT], F32) for i in range(2)]
            qTs = [Tile(P, cx, f"e_qT{i}", [128, 4, TT], BF16, nsub=4) for i in range(2)]
            kTs = [Tile(P, cx, f"e_kT{i}", [128, 4, TT], BF16, nsub=4) for i in range(2)]
            vaug = Tile(P, cx, "e_vaug", [128, 16, 129], BF16, nsub=4)
            NR = E_NR
            gs = [Tile(P, cx, f"e_gs{i}", [128, 128], F32) for i in range(2)]
            gb4 = Tile(P, cx, "e_gb4", [128, 32], F32)
            ss = [Tile(P, cx, f"e_ss{i}", [128, 20], F32) for i in range(2)]
            hs = [Tile(P, cx, f"e_hs{i}", [128, 16], F32) for i in range(NR)]
            ktok = [Tile(P, cx, f"e_ktok{i}", [128, 4, 128], BF16) for i in range(2)]
            PT = [Tile(P, cx, f"e_PT{i}", [128, 128], BF16) for i in range(NR)]
            vw = [Tile(P, cx, f"e_vw{i}", [128, 129], BF16) for i in range(NR)]
            Cf = Tile(P, cx, "e_Cf", [128, 4, 129], F32, nsub=4)
            Cb = Tile(P, cx, "e_Cb", [128, 4, 129], BF16, nsub=4)
            hm = [Tile(P, cx, f"e_hm{i}", [128, 128], F32) for i in range(NR)]
            ytok = [Tile(P, cx, f"e_ytok{i}", [128, 4, 128], BF16) for i in range(2)]
            go = [Tile(P, cx, f"e_go{i}", [128, 512], F32) for i in range(4)]
            mixT = Tile(P, cx, "e_mix", [128, 8, TT], BF16, nsub=8)
            lb = LNBufs(P, cx, TT, "e")
            pb = [Tile(P, cx, f"e_bank{i}", [128, 512], F32, psum=True) for i in range(8)]
            pb4b = pb[4].t[:].bitcast(BF16)
            P.memset("pool", vaug.t[:, :, 128:129], 1.0, W=vaug.b)
            for sb in range(4):
                P.copy("pool", gb4.t[:, sb * 8:(sb + 1) * 8], rows.t[:, 512:520], R=rows.b, W=gb4.b)
            hcnt = 0
            for q in range(NSEQ):
                for ti in range(NT):
                    tok0 = q * S + ti * TT
                    xT = xTs[(q * NT + ti) % 2]
                    h = hs_[0]
                    qT = qTs[(q * NT + ti) % 2]
                    kT = kTs[(q * NT + ti) % 2]
                    for sb in range(4):
                        xi = xin[sb % 2]
                        P.dma(xi.t[:], io.x[tok0 + sb * 128: tok0 + (sb + 1) * 128, :], W=xi.b)
                        for half in range(2):
                            bank = pb[2 + half]
                            for j in range(4):
                                c = half * 4 + j
                                P.tr(bank.t[:, j * 128:(j + 1) * 128], xi.t[:, c * 128:(c + 1) * 128], ps.ident_f,
                                     R=xi.b + ps.CR, W=bank.b)
                            P.copy("dve" if half == 0 else "act", xT.t[:, half * 4:half * 4 + 4, sb * 128:(sb + 1) * 128],
                                   bank.t[:, :].rearrange("p (a b) -> p a b", a=4), R=bank.b, W=xT.b[half * 4:half * 4 + 4])
                    for c in range(8):
                        P.act(h.t[:, c, :], xT.t[:, c, :], AF.Identity, R=[xT.b[c]] + ps.mod.b, W=[h.b[c]],
                              bias=mod_shift(ps, 0, q, c), scale=mod_sc1(ps, 0, q, c))
                    for oc in range(12):
                        bank = pb[oc % 2]
                        for kc in range(8):
                            P.mm(bank.t[:, :], win.t[:, kc, oc * 128:(oc + 1) * 128], h.t[:, kc, :], kc == 0, kc == 7,
                                 R=[h.b[kc]] + win.b, W=bank.b)
                        if oc < 4:
                            g = oc
                            Wn = WINS[g]
                            if ti == 0:
                                P.memset("pool", U[g].t[:, 0:16], 0.0, W=U[g].b)
                            P.copy("act", U[g].t[:, 16:16 + TT], bank.t[:, :], R=bank.b, W=U[g].b)
                            cur, lo, w, k = U[g], 0, 1, 0
                            while w < Wn:
                                nxt = (PA, PB)[k % 2][0]
                                k += 1
                                P.tt("pool", nxt.t[:, lo + w:16 + TT], cur.t[:, lo + w:16 + TT], cur.t[:, lo:16 + TT - w],
                                     ALU.add, R=cur.b, W=nxt.b)
                                cur, lo, w = nxt, lo + w, 2 * w
                            if ti == 0:
                                P.tt("pool", cur.t[:, 16:32], cur.t[:, 16:32], rc.t[:, g, :], ALU.mult, R=cur.b + rc.b, W=cur.b)
                            pl = pooled[g % 2]
                            P.stt(pl.t[:], cur.t[:, 16:16 + TT], 1.0 / Wn, U[g].t[:, 16:16 + TT], ALU.mult, ALU.subtract,
                                  R=cur.b + U[g].b, W=pl.b)
                            P.copy("pool", U[g].t[:, 0:16], U[g].t[:, TT:TT + 16], R=U[g].b, W=U[g].b)
                            bk = pb[5]
                            P.mm(bk.t[:, :], wpool.t[:, g, :], pl.t[:], True, True, R=pl.b + wpool.b, W=bk.b)
                            P.act(mixT.t[:, g, :], bk.t[:, :], AF.Identity, R=bk.b + ps.vecs.b, W=[mixT.b[g]],
                                  scale=vcol(ps, "e_pool_scale", g))
                        else:
                            j = oc - 4
                            uc = UC[j]
                            if ti == 0:
                                P.memset("pool", uc.t[:, 0:3], 0.0, W=uc.b)
                            P.copy("act", uc.t[:, 3:3 + TT], bank.t[:, :], R=bank.b, W=uc.b)
                            ac = cacc[j % 2]
                            P.ts("pool", ac.t[:], uc.t[:, 0:TT], vcol(ps, "e_conv", j * 4 + 0), ALU.mult, R=uc.b + ps.vecs.b, W=ac.b)
                            for tap in range(1, 4):
                                P.stt(ac.t[:], uc.t[:, tap:tap + TT], vcol(ps, "e_conv", j * 4 + tap), ac.t[:], ALU.mult, ALU.add,
                                      R=uc.b + ac.b + ps.vecs.b, W=ac.b)
                            P.copy("pool", uc.t[:, 0:3], uc.t[:, TT:TT + 3], R=uc.b, W=uc.b)
                            dst, db = (qT.t[:, j, :], qT.b[j]) if j < 4 else (kT.t[:, j - 4, :], kT.b[j - 4])
                            P.act(dst, ac.t[:], AF.Silu, R=ac.b, W=[db])
                    if ti == 0:
                        P.memset("pool", Cf.t[:], 0.0, W=Cf.b)
                        P.memset("pool", Cb.t[:], 0.0, W=Cb.b)
                    gsT = gs[(q * NT + ti) % 2]
                    G = gsT.t
                    sp_all, a_all, eF, w_all, t1a, gta = G[:, 0:16], G[:, 16:32], G[:, 32:64], G[:, 64:80], G[:, 80:96], G[:, 96:128]
                    r3 = lambda ap: ap.rearrange("p (s g) -> p s g", s=4)
                    for sb in range(4):
                        cols = slice(sb * 128, (sb + 1) * 128)
                        for kc in range(8):
                            P.mm(pb[3].t[:, sb * 8:(sb + 1) * 8], h.t[:, kc, cols], win.t[:, kc, 2560:2568], kc == 0, kc == 7,
                                 R=[h.b[kc]] + win.b, W=pb[3].b)
                    P.tt("dve", gta, pb[3].t[:, 0:32], gb4.t[:], ALU.add, R=pb[3].b + gb4.b, W=gsT.b)
                    P.act(r3(sp_all), r3(gta)[:, :, 4:8], AF.Exp, R=gsT.b, W=gsT.b, scale=-1.0)
                    P.act(sp_all, sp_all, AF.Ln, R=gsT.b, W=gsT.b, bias=1.0)
                    for sb in range(4):
                        P.mm(pb[3].t[:, 32 + sb * 4:36 + sb * 4], ps.tri_f, sp_all[:, sb * 4:(sb + 1) * 4], True, True, R=gsT.b + ps.CR, W=pb[3].b)
                        P.mm(pb[3].t[:, 48 + sb * 4:52 + sb * 4], ps.ones_f, sp_all[:, sb * 4:(sb + 1) * 4], True, True, R=gsT.b + ps.CR, W=pb[3].b)
                    P.tt("dve", r3(t1a), r3(gta)[:, :, 0:4], r3(pb[3].t[:, 32:48]), ALU.add, R=gsT.b + pb[3].b, W=gsT.b)
                    P.act(a_all, t1a, AF.Exp, R=gsT.b, W=gsT.b, bias=LNS)
                    P.act(eF, pb[3].t[:, 32:64], AF.Exp, R=pb[3].b, W=gsT.b, scale=-1.0)
                    P.tt("dve", w_all, a_all, eF[:, 16:32], ALU.mult, R=gsT.b, W=gsT.b)
                    for sb in range(4):
                        cols = slice(sb * 128, (sb + 1) * 128)
                        for kc in range(8):
                            P.mm(pb[2].t[:, :], h.t[:, kc, cols], win.t[:, kc, 1536:2048], kc == 0, kc == 7, R=[h.b[kc]] + win.b, W=pb[2].b)
                        P.copy("act", vaug.t[:, sb * 4:sb * 4 + 4, 0:128], pb[2].t[:, :].rearrange("p (a b) -> p a b", a=4),
                               R=pb[2].b, W=[vaug.b[sb]])
                        for kc in range(8):
                            P.mm(pb[3].t[:, :], h.t[:, kc, cols], win.t[:, kc, 2048:2560], kc == 0, kc == 7, R=[h.b[kc]] + win.b, W=pb[3].b)
                        g_ = go[sb % 4]
                        P.act(g_.t[:], pb[3].t[:, :], AF.Sigmoid, R=pb[3].b, W=g_.b)
                        P.tt("pool", g_.t[:], g_.t[:], rows.t[:, 0:512], ALU.mult, R=g_.b + rows.b, W=g_.b)
                        s_ = gsT
                        a_, ebt, Fd, w_ = (G[:, 16 + sb * 4:20 + sb * 4], G[:, 32 + sb * 4:36 + sb * 4], G[:, 48 + sb * 4:52 + sb * 4],
                                           G[:, 64 + sb * 4:68 + sb * 4])
                        kk = ktok[sb % 2]
                        for hh in range(4):
                            P.tr(pb4b[:, hh * 128:(hh + 1) * 128], kT.t[:, hh, cols], ps.ident_b, R=[kT.b[hh]] + ps.CR, W=pb[4].b)
                        P.copy("dve", kk.t[:], pb4b[:, 0:512].rearrange("p (a b) -> p a b", a=4), R=pb[4].b, W=kk.b)
                        yt = ytok[sb % 2]
                        sbs = ss[sb % 2]
                        mv3 = sbs.t[:, 0:8].rearrange("p (a t) -> p a t", a=4)
                        vv4, rs4, nm4 = sbs.t[:, 8:12], sbs.t[:, 12:16], sbs.t[:, 16:20]
                        hms = []
                        for hh in range(4):
                            hcnt += 1
                            bank = pb[E_BANKS[hcnt % len(E_BANKS)]]
                            pt = PT[hcnt % NR]
                            v_ = vw[hcnt % NR]
                            hm_ = hm[hcnt % NR]
                            hms.append(hm_)
                            x_ = hs[hcnt % NR]
                            r_, d_, rec, scl, st6 = (x_.t[:, 0:1], x_.t[:, 1:2], x_.t[:, 2:3], x_.t[:, 3:4], x_.t[:, 4:10])
                            P.mm(bank.t[:, 0:128], kT.t[:, hh, cols], qT.t[:, hh, cols], True, True, R=[kT.b[hh], qT.b[hh]], W=bank.b)
                            P.stt(pt.t[:], bank.t[:, 0:128], a_[:, hh:hh + 1], ps.tri_b, ALU.mult, ALU.mult, R=bank.b + s_.b + ps.CR, W=pt.b)
                            P.mm(bank.t[:, 128:257], pt.t[:], vaug.t[:, sb * 4 + hh, :], True, False, R=pt.b + [vaug.b[sb]], W=bank.b)
                            P.mm(bank.t[:, 128:257], qT.t[:, hh, cols], Cb.t[:, hh, :], False, True, R=[qT.b[hh], Cb.b[hh]], W=bank.b)
                            P.tt("dve", r_, bank.t[:, 256:257], ebt[:, hh:hh + 1], ALU.mult, R=bank.b + s_.b, W=x_.b)
                            P.ts("dve", d_, r_, -1.0, ALU.mult, R=x_.b, W=x_.b)
                            P.tt("dve", d_, d_, r_, ALU.max, R=x_.b, W=x_.b)
                            P.ts("dve", d_, d_, 1.0, ALU.max, R=x_.b, W=x_.b)
                            P.op("dve", lambda e, o=rec, i=d_: e.reciprocal(o, i), R=x_.b, W=x_.b, est=120.0)
                            P.tt("dve", scl, rec, ebt[:, hh:hh + 1], ALU.mult, R=x_.b + s_.b, W=x_.b)
                            P.act(hm_.t[:], bank.t[:, 128:256], AF.Identity, R=bank.b + x_.b, W=hm_.b, scale=scl)
                            P.op("dve", lambda e, o=st6, i=hm_.t[:]: e.bn_stats(o, i), R=hm_.b, W=x_.b, est=250.0)
                            P.op("dve", lambda e, o=mv3[:, hh, :], i=st6: e.bn_aggr(o, i), R=x_.b + sbs.b, W=sbs.b, est=120.0)
                            P.ts("pool", v_.t[:], vaug.t[:, sb * 4 + hh, :], w_[:, hh:hh + 1], ALU.mult, R=[vaug.b[sb]] + s_.b, W=v_.b)
                            P.mm(pb[5].t[:, 0:129], kk.t[:, hh, :], v_.t[:], True, True, R=kk.b + v_.b, W=pb[5].b)
                            P.stt(Cf.t[:, hh, :], Cf.t[:, hh, :], Fd[:, hh:hh + 1], pb[5].t[:, 0:129], ALU.mult, ALU.add,
                                  R=[Cf.b[hh]] + s_.b + pb[5].b, W=[Cf.b[hh]])
                            P.copy("act", Cb.t[:, hh, :], Cf.t[:, hh, :], R=[Cf.b[hh]], W=[Cb.b[hh]])
                        P.act(rs4, mv3[:, :, 1], AF.Ln, R=sbs.b, W=sbs.b, bias=LN_EPS)
                        P.act(rs4, rs4, AF.Exp, R=sbs.b, W=sbs.b, scale=-0.5)
                        P.stt(nm4, mv3[:, :, 0], -1.0, rs4, ALU.mult, ALU.mult, R=sbs.b, W=sbs.b)
                        for hh in range(4):
                            hm_ = hms[hh]
                            P.act(hm_.t[:], hm_.t[:], AF.Identity, R=hm_.b + sbs.b, W=hm_.b, bias=nm4[:, hh:hh + 1], scale=rs4[:, hh:hh + 1])
                            P.tt("dve", yt.t[:, hh, :], hm_.t[:], g_.t[:, hh * 128:(hh + 1) * 128], ALU.mult, R=hm_.b + g_.b, W=yt.b)
                        for hh in range(4):
                            P.tr(pb4b[:, 512 + hh * 128:512 + (hh + 1) * 128], yt.t[:, hh, :], ps.ident_b, R=yt.b + ps.CR, W=pb[4].b)
                        P.copy("dve", mixT.t[:, 4:8, cols], pb4b[:, 512:1024].rearrange("p (a b) -> p a b", a=4), R=pb[4].b, W=mixT.b[4:8])
                    for oc in range(8):
                        bank = pb[(0, 1, 6, 7)[oc % 4]]
                        for kc in range(8):
                            P.mm(bank.t[:, :], wout.t[:, kc, oc * 128:(oc + 1) * 128], mixT.t[:, kc, :], kc == 0, kc == 7,
                                 R=[mixT.b[kc]] + wout.b, W=bank.b)
                        P.stt(xT.t[:, oc, :], bank.t[:, :], mod_gate(ps, 0, q, oc), xT.t[:, oc, :], ALU.mult, ALU.add,
                              R=bank.b + [xT.b[oc]] + ps.mod.b, W=[xT.b[oc]])
                        ln_stats_chunk(P, ps, lb, xT, oc, TT, pb[2], pb[3])
                    ln_finish(P, ps, lb, xT, TT, pb[2], pb[3], "e_ln_g", "e_ln_b", LN_EPS / (ALPHA * ALPHA))
                    P.dma(io.X[0][:, :, tok0:tok0 + TT].rearrange("c p t -> p c t"), xT.t[:], R=xT.b, W=[io.Xb[0]])
            P.flush()


def Fdn_alloc(P, wctx, layer):
    return Tile(P, wctx, f"f_wdn{layer}", [128, NFC, 1024], BF16)


def Fdn_load(P, lctx, io, wdn, layer, engines=None):
    stg2 = [Tile(P, lctx, f"f_stgd{layer}{i}", [128, NFC, 128], F32) for i in range(2)]
    load_cast(P, wdn, 0, wslice(io.f_w_down[layer]), 1024, stg2, NFC, blk=128, engines=engines)


def stage_F(P, cfg, io, ps, layer, sl, Xin, Xinb, Xout, Xoutb, final, wdn=None):
    S, NSEQ = cfg.S, cfg.NSEQ
    TT = 256
    NT = S // TT
    L = f"{layer}"
    gname, bname, cname = f"f_ln_g{layer}", f"f_ln_b{layer}", f"f_conv{layer}"
    with ExitStack() as wctx:
        wup = Tile(P, wctx, "f_wup" + L, [128, 8, 2 * DFF], BF16)
        pre = wdn is not None
        if not pre:
            wdn = Fdn_alloc(P, wctx, layer)
        with ExitStack() as lctx:
            stg = [Tile(P, lctx, f"f_stg{L}{i}", [128, 8, 512], F32) for i in range(2)]
            load_cast(P, wup, 0, wslice(io.f_w_up[layer]), 2 * DFF, stg, 8)
            if not pre:
                Fdn_load(P, lctx, io, wdn, layer)
            P.flush()
        with ExitStack() as cx:
            xTs = [Tile(P, cx, f"f_xT{L}{i}", [128, 8, TT], F32, nsub=8) for i in range(2)]
            hs_ = [Tile(P, cx, f"f_h{L}{i}", [128, 8, TT], BF16, nsub=8) for i in range(2)]
            ms_ = [Tile(P, cx, f"f_m{L}{i}", [128, NFC, TT], BF16, nsub=NFC) for i in range(2)]
            halo = Tile(P, cx, "f_halo" + L, [128, NFC, 2], F32, nsub=NFC)
            Gw = [Tile(P, cx, f"f_gw{L}{i}", [128, 2 + TT], F32) for i in range(4)]
            ca = [Tile(P, cx, f"f_ca{L}{i}", [128, TT], F32) for i in range(2)]
            gl = [Tile(P, cx, f"f_gl{L}{i}", [128, TT], F32) for i in range(2)]
            lb = LNBufs(P, cx, TT, "f" + L, merged=True)
            pb = [Tile(P, cx, f"f_bank{L}{i}", [128, 512], F32, psum=True) for i in range(8)]
            xo = [Tile(P, cx, f"f_xo{L}{i}", [128, 1024], F32) for i in range(2)] if final else None
            outs = []
            for q in range(NSEQ):
                for ti in range(NT):
                    tok0 = q * S + ti * TT
                    xT = xTs[(q * NT + ti) % 2]
                    h = hs_[(q * NT + ti) % 2]
                    m = ms_[(q * NT + ti) % 2]
                    P.dma(xT.t[:], Xin[:, :, tok0:tok0 + TT].rearrange("c p t -> p c t"), R=[Xinb], W=xT.b)
                    for c in range(8):
                        P.act(h.t[:, c, :], xT.t[:, c, :], AF.Identity, R=[xT.b[c]] + ps.mod.b, W=[h.b[c]],
                              bias=mod_shift(ps, sl, q, c), scale=mod_sc1(ps, sl, q, c))
                    for fc in range(NFC):
                        bg = pb[fc % 2]
                        ba = pb[2 + fc % 3]
                        for kc in range(8):
                            P.mm(bg.t[:, 0:TT], wup.t[:, kc, DFF + fc * 128:DFF + (fc + 1) * 128], h.t[:, kc, :], kc == 0, kc == 7,
                                 R=[h.b[kc]] + wup.b, W=bg.b)
                        for kc in range(8):
                            P.mm(ba.t[:, 0:TT], wup.t[:, kc, fc * 128:(fc + 1) * 128], h.t[:, kc, :], kc == 0, kc == 7,
                                 R=[h.b[kc]] + wup.b, W=ba.b)
                        gw = Gw[fc % 4]
                        if ti == 0:
                            P.memset("pool", gw.t[:, 0:2], 0.0, W=gw.b)
                        else:
                            P.copy("pool", gw.t[:, 0:2], halo.t[:, fc, :], R=[halo.b[fc]], W=gw.b)
                        P.copy("act", gw.t[:, 2:2 + TT], bg.t[:, 0:TT], R=bg.b, W=gw.b)
                        P.copy("pool", halo.t[:, fc, :], gw.t[:, TT:TT + 2], R=gw.b, W=[halo.b[fc]])
                        ac = ca[fc % 2]
                        P.act(ac.t[:], bg.t[:, 0:TT], AF.Identity, R=bg.b + ps.vecs.b, W=ac.b, scale=vcol(ps, cname, fc * 3 + 2))
                        for tap in (0, 1):
                            P.stt(ac.t[:], gw.t[:, tap:tap + TT], vcol(ps, cname, fc * 3 + tap), ac.t[:], ALU.mult, ALU.add,
                                  R=gw.b + ac.b + ps.vecs.b, W=ac.b)
                        g_ = gl[fc % 2]
                        P.act(g_.t[:], ac.t[:], AF.Gelu, R=ac.b, W=g_.b)
                        P.tt("dve", m.t[:, fc, :], ba.t[:, 0:TT], g_.t[:], ALU.mult, R=ba.b + g_.b, W=[m.b[fc]])
                    for oc in range(8):
                        bank = pb[5 + oc % 2]
                        for fc in range(NFC):
                            P.mm(bank.t[:, 0:TT], wdn.t[:, fc, oc * 128:(oc + 1) * 128], m.t[:, fc, :], fc == 0, fc == NFC - 1,
                                 R=[m.b[fc]] + wdn.b, W=bank.b)
                        P.stt(xT.t[:, oc, :], bank.t[:, 0:TT], mod_gate(ps, sl, q, oc), xT.t[:, oc, :], ALU.mult, ALU.add,
                              R=bank.b + [xT.b[oc]] + ps.mod.b, W=[xT.b[oc]])
                        ln_stats_chunk(P, ps, lb, xT, oc, TT, pb[7], None)
                    ln_finish(P, ps, lb, xT, TT, pb[7], None, gname, bname, LN_EPS / (ALPHA * ALPHA))
                    if not final:
                        P.dma(Xout[:, :, tok0:tok0 + TT].rearrange("c p t -> p c t"), xT.t[:], R=xT.b, W=[Xoutb])
                    else:
                        for sb in range(TT // 128):
                            x_ = xo[sb % 2]
                            for half in range(2):
                                bank = pb[5 + half]
                                for j in range(4):
                                    c = half * 4 + j
                                    o_ = P.tr(bank.t[:, j * 128:(j + 1) * 128], xT.t[:, c, sb * 128:(sb + 1) * 128], ps.ident_f,
                                              R=[xT.b[c]] + ps.CR, W=bank.b)
                                    o_.delay = 3000.0
                                    o_.est = 230.0
                                P.copy("dve" if half == 0 else "act", x_.t[:, half * 512:(half + 1) * 512], bank.t[:, :], R=bank.b, W=x_.b)
                            outs.append(P.dma(io.out[tok0 + sb * 128:tok0 + (sb + 1) * 128, :], x_.t[:], R=x_.b))
            P.flush(final=final)


def stage_M1(P, cfg, io, ps, Xin, Xinb):
    S, NSEQ = cfg.S, cfg.NSEQ
    TT = 512
    NT = S // TT
    sl = 2
    PI = math.pi
    C1 = 6.28125
    C2 = TWO_PI - 6.28125
    with ExitStack() as wctx:
        win = Tile(P, wctx, "m_win", [128, 8, 896], BF16)
        wuq = Tile(P, wctx, "m_wuq", [128, 4, 2048], BF16)
        wukv = Tile(P, wctx, "m_wukv", [128, 2, 2048], BF16)
        wv = Tile(P, wctx, "m_wv", [128, 2, 1024], BF16)
        with ExitStack() as lctx:
            stg = [Tile(P, lctx, f"m_stg{i}", [128, 8, 512], F32) for i in range(2)]
            load_cast(P, win, 0, wslice(io.o_w_in), 896, stg, 8)
            k = 0
            for c0 in range(0, 2048, 512):
                s_ = stg[k % 2]
                k += 1
                P.dma(s_.t[:, 0:4, :], io.o_w_uq[:, c0:c0 + 512].rearrange("(kc p) n -> p kc n", p=128), W=s_.b)
                for kc in range(4):
                    P.ts(P.any3() if False else "dve", wuq.t[:, kc, c0:c0 + 512], s_.t[:, kc, :], vcol(ps, "o_q_norm", kc), ALU.mult,
                         R=s_.b + ps.vecs.b, W=wuq.b)
            for c0 in range(0, 2048, 512):
                s_ = stg[k % 2]
                k += 1
                P.dma(s_.t[:, 0:2, :], io.o_w_ukv[:, c0:c0 + 512].rearrange("(kc p) n -> p kc n", p=128), W=s_.b)
                for kc in range(2):
                    P.ts("dve", wukv.t[:, kc, c0:c0 + 512], s_.t[:, kc, :], vcol(ps, "o_kv_norm", kc), ALU.mult,
                         R=s_.b + ps.vecs.b, W=wukv.b)
            for hh in range(8):
                P.copy("pool", wv.t[:, :, hh * 128:(hh + 1) * 128], wukv.t[:, :, hh * 256 + 128:hh * 256 + 256], R=wukv.b, W=wv.b)
            P.flush()
        with ExitStack() as cx:
            xTs = [Tile(P, cx, f"m_xT{i}", [128, 8, TT], F32, nsub=8) for i in range(2)]
            hs_ = [Tile(P, cx, f"m_h{i}", [128, 8, TT], BF16, nsub=8) for i in range(2)]
            cq = Tile(P, cx, "m_cq", [128, 6, TT], F32, nsub=6)
            sq = [Tile(P, cx, f"m_sq{i}", [128, TT], BF16) for i in range(2)]
            rr = [Tile(P, cx, f"m_rr{i}", [128, TT], F32) for i in range(2)]
            nq = Tile(P, cx, "m_nq", [128, 6, TT], BF16, nsub=6)
            posi = Tile(P, cx, "m_posi", [64, TT], I32)
            ang = Tile(P, cx, "m_ang", [64, TT], F32)
            kf = Tile(P, cx, "m_kf", [64, TT], F32)
            ki = Tile(P, cx, "m_ki", [64, TT], I32)
            msk = Tile(P, cx, "m_msk", [64, TT], F32)
            rs = Tile(P, cx, "m_rs", [64, TT], F32)
            rcs = Tile(P, cx, "m_rcs", [64, TT], F32)
            cos2 = Tile(P, cx, "m_cos2", [64, TT], F32)
            sin2 = Tile(P, cx, "m_sin2", [64, TT], F32)
            t1 = [Tile(P, cx, f"m_t1{i}", [64, TT], F32) for i in range(2)]
            t2 = [Tile(P, cx, f"m_t2{i}", [64, TT], F32) for i in range(2)]
            qn_t = Tile(P, cx, "m_qn", [128, 8, TT], BF16)
            qr_t = Tile(P, cx, "m_qr", [64, 8, TT], BF16)
            kn_t = Tile(P, cx, "m_kn", [128, 8, TT], BF16)
            kr_t = Tile(P, cx, "m_kr", [64, TT], BF16)
            v_t = Tile(P, cx, "m_v", [128, 4, 1024], BF16)
            pb = [Tile(P, cx, f"m_bank{i}", [128, 512], F32, psum=True) for i in range(8)]
            ab = io.ab
            rcnt = 0

            def rope(bA, bB, dst, dstb):
                nonlocal rcnt
                rcnt += 1
                a_, b_ = t1[rcnt % 2], t2[rcnt % 2]
                P.tt("dve", a_.t[:], bA.t[0:64, :], cos2.t[:], ALU.mult, R=bA.b + cos2.b, W=a_.b)
                P.tt("dve", b_.t[:], bB.t[0:64, :], sin2.t[:], ALU.mult, R=bB.b + sin2.b, W=b_.b)
                P.tt("pool", dst, a_.t[:], b_.t[:], ALU.add, R=a_.b + b_.b, W=dstb)

            for q in range(NSEQ):
                for ti in range(NT):
                    tok0 = q * S + ti * TT
                    xT = xTs[(q * NT + ti) % 2]
                    h = hs_[(q * NT + ti) % 2]
                    P.dma(xT.t[:], Xin[:, :, tok0:tok0 + TT].rearrange("c p t -> p c t"), R=[Xinb], W=xT.b)
                    P.dma(posi.t[:], io.pos[q, ti * TT:(ti + 1) * TT].partition_broadcast(64), W=posi.b)
                    for c in range(8):
                        P.act(h.t[:, c, :], xT.t[:, c, :], AF.Identity, R=[xT.b[c]] + ps.mod.b, W=[h.b[c]],
                              bias=mod_shift(ps, sl, q, c), scale=mod_sc1(ps, sl, q, c))
                    if DBG < 2:
                        continue
                    P.copy("dve", ang.t[:], posi.t[:], R=posi.b, W=ang.b)
                    P.ts("dve", ang.t[:], ang.t[:], vcol(ps, "inv2", 0, 64), ALU.mult, R=ang.b + ps.vecs.b, W=ang.b)
                    P.ts("dve", kf.t[:], ang.t[:], 1.0 / TWO_PI, ALU.mult, R=ang.b, W=kf.b)
                    P.copy("dve", ki.t[:], kf.t[:], R=kf.b, W=ki.b)
                    P.copy("dve", kf.t[:], ki.t[:], R=ki.b, W=kf.b)
                    P.stt(rs.t[:], kf.t[:], -C1, ang.t[:], ALU.mult, ALU.add, R=kf.b + ang.b, W=rs.b)
                    P.stt(rs.t[:], kf.t[:], -C2, rs.t[:], ALU.mult, ALU.add, R=kf.b + rs.b, W=rs.b)
                    for tgt, shift in ((rs, 0.0), (rcs, PI / 2)):
                        if tgt is rcs:
                            P.ts("dve", rcs.t[:], rs.t[:], shift, ALU.add, R=rs.b, W=rcs.b)
                        P.ts("dve", msk.t[:], tgt.t[:], PI, ALU.is_gt, R=tgt.b, W=msk.b)
                        P.stt(tgt.t[:], msk.t[:], -TWO_PI, tgt.t[:], ALU.mult, ALU.add, R=msk.b + tgt.b, W=tgt.b)
                        P.ts("dve", msk.t[:], tgt.t[:], -PI, ALU.is_lt, R=tgt.b, W=msk.b)
                        P.stt(tgt.t[:], msk.t[:], TWO_PI, tgt.t[:], ALU.mult, ALU.add, R=msk.b + tgt.b, W=tgt.b)
                    P.act(sin2.t[:], rs.t[:], AF.Sin, R=rs.b + ps.vecs.b, W=sin2.b, scale=vcol(ps, "sgn", 0, 64))
                    P.act(cos2.t[:], rcs.t[:], AF.Sin, R=rcs.b, W=cos2.b)
                    if DBG < 3:
                        continue
                    for oc in range(6):
                        bank = pb[oc % 2]
                        for kc in range(8):
                            P.mm(bank.t[:, :], win.t[:, kc, oc * 128:(oc + 1) * 128], h.t[:, kc, :], kc == 0, kc == 7,
                                 R=[h.b[kc]] + win.b, W=bank.b)
                        P.copy("dve", cq.t[:, oc, :], bank.t[:, :], R=bank.b, W=[cq.b[oc]])
                        s_ = sq[oc % 2]
                        P.act(s_.t[:], bank.t[:, :], AF.Square, R=bank.b, W=s_.b)
                        if oc < 4:
                            P.mm(pb[6].t[:, :], ps.ones_b, s_.t[:], oc == 0, oc == 3, R=s_.b + ps.CR, W=pb[6].b)
                        else:
                            P.mm(pb[7].t[:, :], ps.ones_b, s_.t[:], oc == 4, oc == 5, R=s_.b + ps.CR, W=pb[7].b)
                    for which, (bk, n, ocs) in enumerate(((pb[6], 512.0, range(0, 4)), (pb[7], 256.0, range(4, 6)))):
                        r_ = rr[which]
                        P.ts("dve", r_.t[:], bk.t[:, :], 1.0 / n, ALU.mult, R=bk.b, W=r_.b, s2=RMS_EPS, op1=ALU.add)
                        P.act(r_.t[:], r_.t[:], AF.Ln, R=r_.b, W=r_.b)
                        P.act(r_.t[:], r_.t[:], AF.Exp, R=r_.b, W=r_.b, scale=-0.5)
                        for oc in ocs:
                            P.tt("pool" if oc % 2 else "dve", nq.t[:, oc, :], cq.t[:, oc, :], r_.t[:], ALU.mult,
                                 R=[cq.b[oc]] + r_.b, W=[nq.b[oc]])
                    if DBG < 4:
                        continue
                    for kc in range(8):
                        P.mm(pb[2].t[0:64, :], win.t[:, kc, 768:832], h.t[:, kc, :], kc == 0, kc == 7, R=[h.b[kc]] + win.b, W=pb[2].b)
                    for kc in range(8):
                        P.mm(pb[3].t[0:64, :], win.t[:, kc, 832:896], h.t[:, kc, :], kc == 0, kc == 7, R=[h.b[kc]] + win.b, W=pb[3].b)
                    rope(pb[2], pb[3], kr_t.t[:], kr_t.b)
                    if DBG < 5:
                        continue
                    for hh in range(8):
                        bank = pb[hh % 2]
                        for kc in range(4):
                            P.mm(bank.t[:, :], wuq.t[:, kc, hh * 192:hh * 192 + 128], nq.t[:, kc, :], kc == 0, kc == 3,
                                 R=[nq.b[kc]] + wuq.b, W=bank.b)
                        P.copy("act", qn_t.t[:, hh, :], bank.t[:, :], R=bank.b, W=qn_t.b)
                        bA, bB = pb[2 + 2 * (hh % 2)], pb[3 + 2 * (hh % 2)]
                        for kc in range(4):
                            P.mm(bA.t[0:64, :], wuq.t[:, kc, hh * 192 + 128:hh * 192 + 192], nq.t[:, kc, :], kc == 0, kc == 3,
                                 R=[nq.b[kc]] + wuq.b, W=bA.b)
                        for kc in range(4):
                            P.mm(bB.t[0:64, :], wuq.t[:, kc, 1536 + hh * 64:1536 + (hh + 1) * 64], nq.t[:, kc, :], kc == 0, kc == 3,
                                 R=[nq.b[kc]] + wuq.b, W=bB.b)
                        rope(bA, bB, qr_t.t[:, hh, :], qr_t.b)
                        bank = pb[6 + hh % 2]
                        for kc in range(2):
                            P.mm(bank.t[:, :], wukv.t[:, kc, hh * 256:hh * 256 + 128], nq.t[:, 4 + kc, :], kc == 0, kc == 1,
                                 R=[nq.b[4 + kc]] + wukv.b, W=bank.b)
                        P.copy("act", kn_t.t[:, hh, :], bank.t[:, :], R=bank.b, W=kn_t.b)
                    if DBG < 6:
                        continue
                    for sb in range(4):
                        for cb in range(2):
                            bank = pb[cb]
                            for kc in range(2):
                                P.mm(bank.t[:, :], nq.t[:, 4 + kc, sb * 128:(sb + 1) * 128], wv.t[:, kc, cb * 512:(cb + 1) * 512],
                                     kc == 0, kc == 1, R=[nq.b[4 + kc]] + wv.b, W=bank.b)
                            P.copy("act" if cb else "dve", v_t.t[:, sb, cb * 512:(cb + 1) * 512], bank.t[:, :], R=bank.b, W=v_t.b)
                    if DBG < 7:
                        continue
                    tk = slice(tok0, tok0 + TT)
                    P.dma(io.QN[:, :, tk].rearrange("h p t -> p h t"), qn_t.t[:], R=qn_t.b, W=ab["QN"])
                    P.dma(io.QR[:, :, tk].rearrange("h p t -> p h t"), qr_t.t[:], R=qr_t.b, W=ab["QR"])
                    P.dma(io.KN[:, :, tk].rearrange("h p t -> p h t"), kn_t.t[:], R=kn_t.b, W=ab["KN"])
                    P.dma(io.KR[:, tk], kr_t.t[:], R=kr_t.b, W=ab["KR"])
                    P.dma(io.V[tk, :].rearrange("(sb p) e -> p sb e", p=128), v_t.t[:], R=v_t.b, W=ab["V"])
            P.flush()


def stage_M2(P, cfg, io, ps, prefetch=None):
    S, NSEQ = cfg.S, cfg.NSEQ
    NB = S // 128
    NSB = NB // 4
    SCALE = 192.0 ** -0.5
    ab = io.ab
    with ExitStack() as cx:
        kn = [Tile(P, cx, f"a_kn{i}", [128, S], BF16) for i in range(2)]
        kr = [Tile(P, cx, f"a_kr{i}", [128, S], BF16) for i in range(2)]
        qn = [Tile(P, cx, f"a_qn{i}", [128, S], BF16) for i in range(2)]
        qr = [Tile(P, cx, f"a_qr{i}", [128, S], BF16) for i in range(2)]
        va = [Tile(P, cx, f"a_va{i}", [128, NB, 129], BF16) for i in range(2)]
        NPT = 4
        PT = [Tile(P, cx, f"a_PT{i}", [128, 4, 128], BF16) for i in range(NPT)]
        otok = [Tile(P, cx, f"a_otok{i}", [128, 128], BF16) for i in range(2)]
        rec = [Tile(P, cx, f"a_rec{i}", [128, 1], F32) for i in range(2)]
        oT = [Tile(P, cx, f"a_oT{i}", [128, 512], BF16) for i in range(2)]
        stb = [Tile(P, cx, f"a_st{i}", [128, 512], F32, psum=True) for i in range(3)]
        acc = [Tile(P, cx, f"a_acc{i}", [128, 512], F32, psum=True) for i in range(4)]
        trb = Tile(P, cx, "a_tr", [128, 512], F32, psum=True)
        trbb = trb.t[:].bitcast(BF16)
        if prefetch is not None:
            prefetch(cx)
        for i in range(2):
            P.memset("pool", va[i].t[:, :, 128:129], 1.0, W=va[i].b)
            P.memset("pool", kr[i].t[64:128, :], 0.0, W=kr[i].b)
            P.memset("pool", qr[i].t[64:128, :], 0.0, W=qr[i].b)
        it = 0
        gcnt = 0
        qcnt = 0
        for q in range(NSEQ):
            sq = slice(q * S, (q + 1) * S)
            for hh in range(8):
                i = it % 2
                it += 1
                P.dma(kn[i].t[:], io.KN[hh, :, sq], R=ab["KN"], W=kn[i].b)
                P.dma(kr[i].t[0:64, :], io.KR[:, sq], R=ab["KR"], W=kr[i].b)
                P.dma(qn[i].t[:], io.QN[hh, :, sq], R=ab["QN"], W=qn[i].b)
                P.dma(qr[i].t[0:64, :], io.QR[hh, :, sq], R=ab["QR"], W=qr[i].b)
                for k0 in range(0, NB, 8):
                    P.dma(va[i].t[:, k0:k0 + 8, 0:128],
                          io.V[q * S + k0 * 128:q * S + (k0 + 8) * 128, hh * 128:(hh + 1) * 128].rearrange("(kb p) e -> p kb e", p=128),
                          R=ab["V"], W=va[i].b)
                for sb in range(NSB):
                    o4 = oT[sb % 2]
                    for kb in range(4 * sb + 4):
                        r = max(kb - 4 * sb, 0)
                        n = 4 - r
                        ks = slice(kb * 128, (kb + 1) * 128)
                        qs = slice((4 * sb + r) * 128, (4 * sb + 4) * 128)
                        st = stb[gcnt % 3]
                        pt = PT[gcnt % NPT]
                        gcnt += 1
                        P.mm(st.t[:, 0:n * 128], kn[i].t[:, ks], qn[i].t[:, qs], True, False, R=kn[i].b + qn[i].b, W=st.b)
                        P.mm(st.t[:, 0:n * 128], kr[i].t[:, ks], qr[i].t[:, qs], False, True, R=kr[i].b + qr[i].b, W=st.b)
                        P.act(pt.t[:, r:4, :], st.t[:, 0:n * 128].rearrange("p (a b) -> p a b", a=n), AF.Exp, R=st.b, W=pt.b, scale=SCALE)
                        if kb >= 4 * sb:
                            P.memset("pool", pt.t[64:128, r, 0:64], 0.0, W=pt.b)
                        for j in range(r, 4):
                            last = (kb == 4 * sb + j)
                            P.mm(acc[j].t[:, 0:129], pt.t[:, j, :], va[i].t[:, kb, :], kb == 0, last, R=pt.b + va[i].b, W=acc[j].b)
                            if last:
                                ac = acc[j]
                                r_ = rec[qcnt % 2]
                                ot = otok[qcnt % 2]
                                qcnt += 1
                                P.op("dve", lambda e, o=r_.t[:], a=ac.t[:, 128:129]: e.reciprocal(o, a), R=ac.b, W=r_.b, est=150.0)
                                P.ts("dve", ot.t[:], ac.t[:, 0:128], r_.t[:, 0:1], ALU.mult, R=ac.b + r_.b, W=ot.b)
                                P.tr(trbb[:, j * 128:(j + 1) * 128], ot.t[:], ps.ident_b, R=ot.b + ps.CR, W=trb.b)
                    P.copy("dve", o4.t[:], trbb[:, 0:512], R=trb.b, W=o4.b)
                    c0 = q * S + sb * 512
                    P.dma(io.OT[hh, :, c0:c0 + 512], o4.t[:], R=o4.b, W=ab["OT"])
        P.flush()


def M3_alloc(P, wctx):
    return Tile(P, wctx, "o_wout", [128, 8, 1024], BF16)


def M3_load(P, lctx, io, wout, engines=None):
    stg = [Tile(P, lctx, f"o_stg{i}", [128, 8, 256], F32) for i in range(2)]
    load_cast(P, wout, 0, wslice(io.o_w_out), 1024, stg, 8, blk=256, engines=engines)


def stage_M3(P, cfg, io, ps, Xin, Xinb, Xout, Xoutb, wout=None):
    S, NSEQ = cfg.S, cfg.NSEQ
    TT = 512
    NT = S // TT
    sl = 2
    with ExitStack() as wctx:
        if wout is None:
            wout = M3_alloc(P, wctx)
            with ExitStack() as lctx:
                M3_load(P, lctx, io, wout)
                P.flush()
        with ExitStack() as cx:
            xTs = [Tile(P, cx, f"o_xT{i}", [128, 8, TT], F32, nsub=8) for i in range(2)]
            oTs = [Tile(P, cx, f"o_oT{i}", [128, 8, TT], BF16) for i in range(2)]
            lb = LNBufs(P, cx, TT, "o", split=True)
            pb = [Tile(P, cx, f"o_bank{i}", [128, 512], F32, psum=True) for i in range(6)]
            for q in range(NSEQ):
                for ti in range(NT):
                    tok0 = q * S + ti * TT
                    tk = slice(tok0, tok0 + TT)
                    xT = xTs[(q * NT + ti) % 2]
                    oT = oTs[(q * NT + ti) % 2]
                    P.dma(xT.t[:], Xin[:, :, tk].rearrange("c p t -> p c t"), R=[Xinb], W=xT.b)
                    P.dma(oT.t[:], io.OT[:, :, tk].rearrange("h p t -> p h t"), R=io.ab["OT"], W=oT.b)
                    for oc in range(8):
                        bank = pb[oc % 4]
                        for kc in range(8):
                            P.mm(bank.t[:, :], wout.t[:, kc, oc * 128:(oc + 1) * 128], oT.t[:, kc, :], kc == 0, kc == 7,
                                 R=oT.b + wout.b, W=bank.b)
                        P.stt(xT.t[:, oc, :], bank.t[:, :], mod_gate(ps, sl, q, oc), xT.t[:, oc, :], ALU.mult, ALU.add,
                              R=bank.b + [xT.b[oc]] + ps.mod.b, W=[xT.b[oc]])
                        ln_stats_chunk(P, ps, lb, xT, oc, TT, pb[4], pb[5])
                    ln_finish(P, ps, lb, xT, TT, pb[4], pb[5], "o_ln_g", "o_ln_b", LN_EPS / (ALPHA * ALPHA))
                    P.dma(Xout[:, :, tk].rearrange("c p t -> p c t"), xT.t[:], R=xT.b, W=[Xoutb])
            P.flush()


def build_program(cfg, upto=99, dbg_x=None):
    nc = bass.Bass("TRN2", target_bir_lowering=False)
    io = declare_io(nc, cfg)
    P = Prog(nc)
    dbg = None
    if dbg_x is not None:
        dbg = [nc.dram_tensor(f"dbg{i}", [8, 128, cfg.T], F32, kind="ExternalOutput").ap() for i in dbg_x]
    with ExitStack() as ctx:
        ps = setup_persistent(P, ctx, io, cfg)
        with ExitStack() as ectx:
            EW = None
            if upto >= 1:
                EW = E_alloc(P, ectx)
            with ExitStack() as lctx:
                if upto >= 1:
                    E_load(P, lctx, io, EW)
                stage_ada(P, cfg, io, ps)
            if upto >= 1:
                stage_E(P, cfg, io, ps, EW)
        if upto >= 2:
            stage_F(P, cfg, io, ps, 0, 1, io.X[0], io.Xb[0], io.X[1], io.Xb[1], False)
        if upto >= 3:
            stage_M1(P, cfg, io, ps, io.X[1], io.Xb[1])
        if upto >= 4:
            stage_M2(P, cfg, io, ps)
        if upto >= 5:
            stage_M3(P, cfg, io, ps, io.X[1], io.Xb[1], io.X[2], io.Xb[2])
        if upto >= 6:
            stage_F(P, cfg, io, ps, 1, 3, io.X[2], io.Xb[2], io.X[3], io.Xb[3], True)
        if dbg is not None:
            for d, i in zip(dbg, dbg_x):
                P.dma(d[:, :, :], io.X[i][:, :, :], R=[io.Xb[i]])
            P.flush(final=True)
    return nc, P


def make_consts():
    c = np.zeros((128, 5, 128), np.float32)
    c[:, 0, :] = np.eye(128, dtype=np.float32)
    c[:, 1, :] = np.triu(np.ones((128, 128), np.float32))
    c[:, 2, :] = 1.0
    rc = np.zeros((128, 4, 16), np.float32)
    for g, w in enumerate((2, 4, 8, 16)):
        t = np.arange(16)
        rc[:, g, :] = (w / np.minimum(t + 1, w)).astype(np.float32)[None, :]
    return c, rc


def pack_shared(inp):
    f = lambda a: np.ascontiguousarray(np.asarray(a, np.float32))
    sh = {}
    vecs = np.zeros((128, NV), np.float32)

    def put(name, arr):
        arr = np.asarray(arr, np.float32)
        vecs[:arr.shape[0], VOFF[name]:VOFF[name] + arr.shape[1]] = arr
    adab = [inp["e_ada_b"][0], inp["f_ada_b"][0], inp["o_ada_b"][0], inp["f_ada_b"][1]]
    for sl in range(4):
        put(f"adab{sl}", chunked(adab[sl], 24))
    put("e_pool_scale", chunked(inp["e_pool_scale"][0], 4))
    cq = np.asarray(inp["e_conv_qk"][0], np.float32)
    put("e_conv", cq.reshape(4, 8, 128).transpose(2, 1, 0).reshape(128, 32))
    put("e_ln_g", chunked(inp["e_ln_g"][0], 8))
    put("e_ln_b", chunked(inp["e_ln_b"][0], 8))
    put("o_ln_g", chunked(inp["o_ln_g"][0], 8))
    put("o_ln_b", chunked(inp["o_ln_b"][0], 8))
    put("o_q_norm", chunked(inp["o_q_norm"][0], 4))
    put("o_kv_norm", chunked(inp["o_kv_norm"][0], 2))
    for l in range(2):
        fcv = np.asarray(inp["f_conv"][l], np.float32)
        put(f"f_conv{l}", fcv.reshape(3, NFC, 128).transpose(2, 1, 0).reshape(128, 66))
        put(f"f_ln_g{l}", chunked(inp["f_ln_g"][l], 8))
        put(f"f_ln_b{l}", chunked(inp["f_ln_b"][l], 8))
    half = 32
    inv = (10000.0 ** (-np.arange(half, dtype=np.float32) / half)).astype(np.float32)
    inv2 = np.zeros((128, 1), np.float32)
    inv2[:64, 0] = np.concatenate([inv, inv])
    put("inv2", inv2)
    sgn = np.ones((128, 1), np.float32)
    sgn[:32] = -1.0
    put("sgn", sgn)
    sh["vecs"] = vecs
    sh["rows"] = f(np.concatenate([inp["e_head_norm"][0], inp["e_gate_b"][0]])[None, :])
    sh["consts"], sh["rc"] = make_consts()
    ada = [inp["e_ada_w"][0], inp["f_ada_w"][0], inp["o_ada_w"][0], inp["f_ada_w"][1]]
    for i in range(4):
        sh[f"ada_w{i}"] = f(ada[i])
    sh["e_w_in"] = f(inp["e_w_in"][0])
    sh["e_pool_w"] = f(np.asarray(inp["e_pool_w"][0]).transpose(1, 0, 2))
    sh["e_w_out"] = f(inp["e_w_out"][0])
    perm = (np.arange(64) + 32) % 64
    owin = np.asarray(inp["o_w_in"][0], np.float32)
    sh["o_w_in"] = f(np.concatenate([owin, owin[:, 768:832][:, perm]], axis=1))
    wuq = np.asarray(inp["o_w_uq"][0], np.float32)
    rp = [wuq[:, hh * 192 + 128: hh * 192 + 192][:, perm] for hh in range(8)]
    sh["o_w_uq"] = f(np.concatenate([wuq] + rp, axis=1))
    sh["o_w_ukv"] = f(inp["o_w_ukv"][0])
    sh["o_w_out"] = f(inp["o_w_out"][0])
    for l in range(2):
        sh[f"f_w_up{l}"] = f(inp["f_w_up"][l])
        sh[f"f_w_down{l}"] = f(inp["f_w_down"][l])
    return sh


def pack_core(inp, core, cfg):
    NSEQ, S = cfg.NSEQ, cfg.S
    b0 = core * NSEQ
    m = {}
    m["x"] = np.ascontiguousarray(np.asarray(inp["x"][b0:b0 + NSEQ], np.float32).reshape(NSEQ * S, D))
    m["pos"] = np.ascontiguousarray(np.asarray(inp["positions"][b0:b0 + NSEQ], np.int32))
    c = np.asarray(inp["c"][b0:b0 + NSEQ], np.float32)
    m["cT"] = np.ascontiguousarray(c.reshape(NSEQ, 8, 128).transpose(2, 1, 0))
    return m


N_CORES = 8
_CACHE = {}


def kernel(**inputs):
    B, S = inputs["x"].shape[0], inputs["x"].shape[1]
    nseq = B // N_CORES
    cfg = Cfg(S, nseq)
    key = (S, nseq)
    if key not in _CACHE:
        _CACHE[key] = build_program(cfg)[0]
    nc = _CACHE[key]
    sh = pack_shared(inputs)
    in_maps = []
    for core in range(N_CORES):
        m = dict(sh)
        m.update(pack_core(inputs, core, cfg))
        in_maps.append(m)
    res = run_bass_kernel_spmd(nc, in_maps, core_ids=list(range(N_CORES)))
    out = np.concatenate([np.asarray(r["out"], np.float32).reshape(nseq, S, D) for r in res.results], axis=0)
    return out
```
